# Optimizing a Trainium2 kernel written in Bass

```python
import math
import jax, jax.numpy as jnp
from jax import lax
import numpy as np

D_MODEL = 2048
BATCH = 1
SEQ = 8192
DEPTH = 4

MLA_HEADS = 8
MLA_NOPE_DIM = 128
MLA_ROPE_DIM = 64
MLA_V_DIM = 128
MLA_QK_DIM = MLA_NOPE_DIM + MLA_ROPE_DIM
MLA_Q_RANK = 512
MLA_KV_RANK = 512
ROPE_THETA = 10000.0
ATTN_BLOCK = 128
GDN_HEADS = 8
GDN_HEAD_DIM = 128
GDN_DIM = GDN_HEADS * GDN_HEAD_DIM
GDN_CONV = 4
GDN_CHUNK = 64
GLA_HEADS = 4
GLA_KEY_DIM = D_MODEL // 2
GLA_VALUE_DIM = D_MODEL
GLA_HEAD_K = GLA_KEY_DIM // GLA_HEADS
GLA_HEAD_V = GLA_VALUE_DIM // GLA_HEADS
GLA_GATE_RANK = 16
GLA_GATE_NORMALIZER = 16.0
GLA_CHUNK = 64
D_FF = -(-8 * D_MODEL // (3 * 256)) * 256
MLA_IN = MLA_Q_RANK + MLA_KV_RANK + MLA_ROPE_DIM
GDN_IN = 4 * GDN_DIM + 2 * GDN_HEADS
EVEN_IN = MLA_IN + GDN_IN
EVEN_MIX = MLA_HEADS * MLA_V_DIM + GDN_DIM
GLA_IN = 2 * GLA_KEY_DIM + 2 * GLA_VALUE_DIM + GLA_GATE_RANK
N_EVEN = (DEPTH + 1) // 2
N_ODD = DEPTH // 2
EPS = 1e-6

kernel_name = "hybrid_mla_gdn_gla_adaln_trunk"


def rms_norm(x, gain):
    xf = x.astype(jnp.float32)
    y = xf * lax.rsqrt(jnp.mean(xf * xf, axis=-1, keepdims=True) + EPS)
    return (y * gain.astype(jnp.float32)).astype(x.dtype)


def l2_norm(x):
    xf = x.astype(jnp.float32)
    return xf * lax.rsqrt(jnp.sum(xf * xf, axis=-1, keepdims=True) + EPS)


def split_cols(t, sizes):
    return jnp.split(t, np.cumsum(sizes)[:-1].tolist(), axis=-1)


def ada_modulation(c, w, b):
    m = (jax.nn.silu(c) @ w + b)[:, None, :]
    return jnp.split(m, 3, axis=-1)


def rope_tables(positions):
    half = MLA_ROPE_DIM // 2
    inv_freq = ROPE_THETA ** (-jnp.arange(half, dtype=jnp.float32) / half)
    ang = positions.astype(jnp.float32)[..., None] * inv_freq
    return jnp.cos(ang), jnp.sin(ang)


def apply_rope(x, cos, sin):
    x1, x2 = jnp.split(x.astype(jnp.float32), 2, axis=-1)
    return jnp.concatenate([x1 * cos - x2 * sin, x2 * cos + x1 * sin], axis=-1)


def causal_depthwise_conv(x, w):
    k_width, s = w.shape[0], x.shape[1]
    xp = jnp.pad(x, ((0, 0), (k_width - 1, 0), (0, 0)))
    y = xp[:, 0:s] * w[0]
    for j in range(1, k_width):
        y = y + xp[:, j:j + s] * w[j]
    return y


def causal_block_attention(q, k, v):
    b, h, s, dqk = q.shape
    nb = s // ATTN_BLOCK
    scale = 1.0 / math.sqrt(dqk)
    qb = jnp.moveaxis(q.reshape(b, h, nb, ATTN_BLOCK, dqk), 2, 0)
    kpos = jnp.arange(s)

    def one_block(args):
        q_blk, i = args
        qpos = i * ATTN_BLOCK + jnp.arange(ATTN_BLOCK)
        sc = jnp.einsum('bhqd,bhkd->bhqk', q_blk, k) * scale
        sc = jnp.where(kpos[None, :] <= qpos[:, None], sc, -jnp.inf)
        p = jax.nn.softmax(sc, axis=-1)
        return jnp.einsum('bhqk,bhkd->bhqd', p, v)

    o = lax.map(one_block, (qb, jnp.arange(nb)))
    return jnp.moveaxis(o, 0, 2).reshape(b, h, s, v.shape[-1])


def mla_branch(c_q, c_kv, k_rope, cos, sin, q_norm, w_uq, kv_norm, w_ukv):
    b, s, _ = c_q.shape
    q = (rms_norm(c_q, q_norm) @ w_uq).reshape(b, s, MLA_HEADS, MLA_QK_DIM)
    kv = (rms_norm(c_kv, kv_norm) @ w_ukv).reshape(b, s, MLA_HEADS, MLA_NOPE_DIM + MLA_V_DIM)
    q_nope, q_rope = q[..., :MLA_NOPE_DIM], q[..., MLA_NOPE_DIM:]
    k_nope, v = kv[..., :MLA_NOPE_DIM], kv[..., MLA_NOPE_DIM:]
    q_rope = apply_rope(q_rope, cos[:, :, None, :], sin[:, :, None, :])
    k_rope = apply_rope(k_rope, cos, sin)
    q = jnp.concatenate([q_nope.astype(jnp.float32), q_rope], axis=-1)
    k = jnp.concatenate([k_nope.astype(jnp.float32),
                         jnp.broadcast_to(k_rope[:, :, None, :], (b, s, MLA_HEADS, MLA_ROPE_DIM))], axis=-1)
    o = causal_block_attention(q.transpose(0, 2, 1, 3), k.transpose(0, 2, 1, 3),
                               v.astype(jnp.float32).transpose(0, 2, 1, 3))
    return o.transpose(0, 2, 1, 3).reshape(b, s, MLA_HEADS * MLA_V_DIM)


def gated_delta_rule_chunked(q, k, v, g, beta):
    b, h, s, dk = q.shape
    dv = v.shape[-1]
    cs, n = GDN_CHUNK, s // GDN_CHUNK
    q = q * (1.0 / math.sqrt(dk))
    q, k, v = (t.reshape(b, h, n, cs, t.shape[-1]) for t in (q, k, v))
    g, beta = g.reshape(b, h, n, cs), beta.reshape(b, h, n, cs)
    gc = jnp.cumsum(g, axis=-1)
    incl = jnp.tril(jnp.ones((cs, cs), dtype=bool))
    strict = jnp.tril(jnp.ones((cs, cs), dtype=bool), -1)
    decay = jnp.exp(jnp.where(incl, gc[..., :, None] - gc[..., None, :], -jnp.inf))
    k_beta = k * beta[..., None]
    lower = jnp.where(strict, jnp.einsum('bhnid,bhnjd->bhnij', k_beta, k) * decay, 0.0)
    t_mat = jnp.eye(cs, dtype=q.dtype) + lower
    rhs = jnp.concatenate([v * beta[..., None], k_beta * jnp.exp(gc)[..., None]], axis=-1)
    sol = lax.linalg.triangular_solve(t_mat, rhs, left_side=True, lower=True, unit_diagonal=True)
    u, w = sol[..., :dv], sol[..., dv:]
    intra = jnp.einsum('bhnid,bhnjd->bhnij', q, k) * decay
    g_last = gc[..., -1]
    k_dec = k * jnp.exp(g_last[..., None] - gc)[..., None]
    q_dec = q * jnp.exp(gc)[..., None]

    def step(state, inp):
        q_d, k_d, u_c, w_c, a_c, gl = inp
        v_new = u_c - jnp.einsum('bhcd,bhde->bhce', w_c, state)
        o = jnp.einsum('bhcd,bhde->bhce', q_d, state) + jnp.einsum('bhij,bhje->bhie', a_c, v_new)
        state = state * jnp.exp(gl)[..., None, None] + jnp.einsum('bhcd,bhce->bhde', k_d, v_new)
        return state, o

    xs = tuple(jnp.moveaxis(t, 2, 0) for t in (q_dec, k_dec, u, w, intra, g_last))
    _, o = lax.scan(step, jnp.zeros((b, h, dk, dv), q.dtype), xs)
    return jnp.moveaxis(o, 0, 2).reshape(b, h, s, dv)


def gdn_branch(qkv, z, b_logit, a_logit, conv_w, a_log, dt_bias, norm_g):
    bsz, s, _ = qkv.shape
    qkv = jax.nn.silu(causal_depthwise_conv(qkv, conv_w))
    q, k, v = jnp.split(qkv, 3, axis=-1)
    heads = lambda t: t.reshape(bsz, s, GDN_HEADS, GDN_HEAD_DIM)
    q, k = l2_norm(heads(q)), l2_norm(heads(k))
    v = heads(v).astype(jnp.float32)
    beta = jax.nn.sigmoid(b_logit.astype(jnp.float32))
    g = -jnp.exp(a_log.astype(jnp.float32)) * jax.nn.softplus(a_logit.astype(jnp.float32) + dt_bias)
    o = gated_delta_rule_chunked(q.transpose(0, 2, 1, 3), k.transpose(0, 2, 1, 3), v.transpose(0, 2, 1, 3),
                                 g.transpose(0, 2, 1), beta.transpose(0, 2, 1))
    o = rms_norm(o.transpose(0, 2, 1, 3), norm_g) * jax.nn.silu(heads(z).astype(jnp.float32))
    return o.reshape(bsz, s, GDN_DIM)


def ab_mixer(h, cos, sin, w_in, q_norm, w_uq, kv_norm, w_ukv, conv_w, a_log, dt_bias, gdn_norm, w_out):
    proj = h @ w_in
    c_q, c_kv, k_rope, qkv, z, b_logit, a_logit = split_cols(
        proj, [MLA_Q_RANK, MLA_KV_RANK, MLA_ROPE_DIM, 3 * GDN_DIM, GDN_DIM, GDN_HEADS, GDN_HEADS])
    o_a = mla_branch(c_q, c_kv, k_rope, cos, sin, q_norm, w_uq, kv_norm, w_ukv)
    o_b = gdn_branch(qkv, z, b_logit, a_logit, conv_w, a_log, dt_bias, gdn_norm)
    return jnp.concatenate([o_a, o_b], axis=-1).astype(h.dtype) @ w_out


def gla_chunked(q, k, v, gk):
    b, h, s, dk = q.shape
    dv = v.shape[-1]
    cs, n = GLA_CHUNK, s // GLA_CHUNK
    q = q * (1.0 / math.sqrt(dk))
    chunks = lambda t: jnp.moveaxis(t.reshape(b, h, n, cs, t.shape[-1]), 2, 0)
    bc = jnp.cumsum(gk.reshape(b, h, n, cs, dk), axis=3)
    causal = jnp.tril(jnp.ones((cs, cs), dtype=bool))[:, :, None]

    def step(state, inp):
        q_c, k_c, v_c, b_c = inp
        o_inter = jnp.einsum('bhcd,bhde->bhce', q_c * jnp.exp(b_c), state)
        rel = jnp.exp(jnp.where(causal, b_c[:, :, :, None, :] - b_c[:, :, None, :, :], -jnp.inf))
        scores = jnp.einsum('bhid,bhjd,bhijd->bhij', q_c, k_c, rel)
        o = o_inter + jnp.einsum('bhij,bhje->bhie', scores, v_c)
        b_last = b_c[:, :, -1:, :]
        state = state * jnp.exp(b_last[:, :, 0, :, None]) + jnp.einsum('bhcd,bhce->bhde', k_c * jnp.exp(b_last - b_c), v_c)
        return state, o

    _, o = lax.scan(step, jnp.zeros((b, h, dk, dv), q.dtype),
                    (chunks(q), chunks(k), chunks(v), jnp.moveaxis(bc, 2, 0)))
    return jnp.moveaxis(o, 0, 2).reshape(b, h, s, dv)


def gla_mixer(h, w_in, w_gk2, b_gk2, norm_g, w_out):
    bsz, s, _ = h.shape
    q, k, v, r, gk_low = split_cols(h @ w_in, [GLA_KEY_DIM, GLA_KEY_DIM, GLA_VALUE_DIM, GLA_VALUE_DIM, GLA_GATE_RANK])
    gk = jax.nn.log_sigmoid((gk_low @ w_gk2 + b_gk2).astype(jnp.float32)) / GLA_GATE_NORMALIZER
    hk = lambda t: t.astype(jnp.float32).reshape(bsz, s, GLA_HEADS, GLA_HEAD_K).transpose(0, 2, 1, 3)
    hv = lambda t: t.astype(jnp.float32).reshape(bsz, s, GLA_HEADS, GLA_HEAD_V).transpose(0, 2, 1, 3)
    o = gla_chunked(hk(q), hk(k), hv(v), hk(gk)).transpose(0, 2, 1, 3)
    o = rms_norm(o, norm_g) * jax.nn.silu(r.astype(jnp.float32).reshape(bsz, s, GLA_HEADS, GLA_HEAD_V))
    return o.reshape(bsz, s, GLA_VALUE_DIM).astype(h.dtype) @ w_out


def swiglu(h, w1, w3, w2):
    return (jax.nn.silu(h @ w1) * (h @ w3)) @ w2


def setup_inputs(seed: int = 0) -> dict:
    key = jax.random.key(seed)
    ks = iter(jax.random.split(key, 32))
    nrm = lambda shape, std: jax.random.normal(next(ks), shape, jnp.float32) * std
    gain = lambda shape: 1.0 + nrm(shape, 0.02)
    offset = jax.random.randint(next(ks), (BATCH, 1), 0, 1024, dtype=jnp.int32)
    positions = offset + jnp.arange(SEQ, dtype=jnp.int32)[None, :]
    a_log = jnp.log(jax.random.uniform(next(ks), (N_EVEN, GDN_HEADS), jnp.float32, 1.0, 16.0))
    dt = jnp.exp(jax.random.uniform(next(ks), (N_EVEN, GDN_HEADS), jnp.float32, math.log(1e-3), math.log(1e-1)))
    dt_bias = dt + jnp.log(-jnp.expm1(-dt))
    return {
        "x": nrm((BATCH, SEQ, D_MODEL), 1.0),
        "c": nrm((BATCH, D_MODEL), 1.0),
        "positions": positions,
        "norm_g": gain((DEPTH, 2, D_MODEL)),
        "ada_w": nrm((DEPTH, 2, D_MODEL, 3 * D_MODEL), 0.5 * D_MODEL ** -0.5),
        "ada_b": nrm((DEPTH, 2, 3 * D_MODEL), 0.01),
        "ab_w_in": nrm((N_EVEN, D_MODEL, EVEN_IN), D_MODEL ** -0.5),
        "mla_q_norm": gain((N_EVEN, MLA_Q_RANK)),
        "mla_w_uq": nrm((N_EVEN, MLA_Q_RANK, MLA_HEADS * MLA_QK_DIM), MLA_Q_RANK ** -0.5),
        "mla_kv_norm": gain((N_EVEN, MLA_KV_RANK)),
        "mla_w_ukv": nrm((N_EVEN, MLA_KV_RANK, MLA_HEADS * (MLA_NOPE_DIM + MLA_V_DIM)), MLA_KV_RANK ** -0.5),
        "gdn_conv_w": nrm((N_EVEN, GDN_CONV, 3 * GDN_DIM), GDN_CONV ** -0.5),
        "gdn_a_log": a_log,
        "gdn_dt_bias": dt_bias,
        "gdn_norm": gain((N_EVEN, GDN_HEAD_DIM)),
        "ab_w_out": nrm((N_EVEN, EVEN_MIX, D_MODEL), EVEN_MIX ** -0.5),
        "gla_w_in": nrm((N_ODD, D_MODEL, GLA_IN), D_MODEL ** -0.5),
        "gla_w_gk2": nrm((N_ODD, GLA_GATE_RANK, GLA_KEY_DIM), GLA_GATE_RANK ** -0.5),
        "gla_b_gk2": nrm((N_ODD, GLA_KEY_DIM), 0.01),
        "gla_norm": gain((N_ODD, GLA_HEAD_V)),
        "gla_w_out": nrm((N_ODD, GLA_VALUE_DIM, D_MODEL), GLA_VALUE_DIM ** -0.5),
        "ffn_w1": nrm((DEPTH, D_MODEL, D_FF), D_MODEL ** -0.5),
        "ffn_w3": nrm((DEPTH, D_MODEL, D_FF), D_MODEL ** -0.5),
        "ffn_w2": nrm((DEPTH, D_FF, D_MODEL), D_FF ** -0.5),
        "final_norm": gain((D_MODEL,)),
    }


def reference(x, c, positions, norm_g, ada_w, ada_b, ab_w_in, mla_q_norm, mla_w_uq, mla_kv_norm, mla_w_ukv,
              gdn_conv_w, gdn_a_log, gdn_dt_bias, gdn_norm, ab_w_out, gla_w_in, gla_w_gk2, gla_b_gk2, gla_norm,
              gla_w_out, ffn_w1, ffn_w3, ffn_w2, final_norm):
    cos, sin = rope_tables(positions)
    for layer in range(DEPTH):
        i = layer // 2
        shift, scale, gate = ada_modulation(c, ada_w[layer, 0], ada_b[layer, 0])
        h = rms_norm(x, norm_g[layer, 0]) * (1.0 + scale) + shift
        if layer % 2 == 0:
            mix = ab_mixer(h, cos, sin, ab_w_in[i], mla_q_norm[i], mla_w_uq[i], mla_kv_norm[i], mla_w_ukv[i],
                           gdn_conv_w[i], gdn_a_log[i], gdn_dt_bias[i], gdn_norm[i], ab_w_out[i])
        else:
            mix = gla_mixer(h, gla_w_in[i], gla_w_gk2[i], gla_b_gk2[i], gla_norm[i], gla_w_out[i])
        x = x + gate * mix
        shift, scale, gate = ada_modulation(c, ada_w[layer, 1], ada_b[layer, 1])
        h = rms_norm(x, norm_g[layer, 1]) * (1.0 + scale) + shift
        x = x + gate * swiglu(h, ffn_w1[layer], ffn_w3[layer], ffn_w2[layer])
    return rms_norm(x, final_norm)
```

```python
import numpy as np
import concourse.bass as bass
import concourse.mybir as mybir
from concourse.bass_utils import run_bass_kernel_spmd
from contextlib import ExitStack


F32 = mybir.dt.float32
BF16 = mybir.dt.bfloat16
I32 = mybir.dt.int32
AF = mybir.ActivationFunctionType
ALU = mybir.AluOpType
AX = mybir.AxisListType


class Buf:
    __slots__ = ("name", "last_w", "readers")

    def __init__(self, name=""):
        self.name = name
        self.last_w = None
        self.readers = []


class V:
    __slots__ = ("ap", "buf")

    def __init__(self, ap, buf):
        self.ap = ap
        self.buf = buf

    def __getitem__(self, k):
        return V(self.ap[k], self.buf)

    def rearrange(self, s, **kw):
        return V(self.ap.rearrange(s, **kw), self.buf)

    def bitcast(self, dt):
        return V(self.ap.bitcast(dt), self.buf)

    def bcast(self, shape):
        return V(self.ap.broadcast_to(shape), self.buf)

    def sub(self, k, name=""):
        return V(self.ap[k], Buf(name))

    @property
    def shape(self):
        return self.ap.shape


class Op:
    __slots__ = ("eng", "fn", "deps", "signals", "count", "is_dma", "lane", "idx")


ENGS = ("pe", "act", "dve", "pool", "sp")


def _ap(x):
    return x.ap if isinstance(x, V) else x


class Prog:
    N_LANES = 6
    SAME_ENG_WINDOW = 6

    def __init__(self, nc):
        self.nc = nc
        self.ops = {e: [] for e in ENGS}
        self.stack = ExitStack()
        self.nbytes = 0

    def sbuf(self, name, shape, dtype):
        t = self.stack.enter_context(self.nc.sbuf_tensor(name, list(shape), dtype))
        return V(t[:], Buf(name))

    def psum(self, name, shape, dtype=F32):
        t = self.stack.enter_context(self.nc.psum_tensor(name, list(shape), dtype))
        return V(t[:], Buf(name))

    def add(self, eng, fn, reads=(), writes=(), is_dma=False):
        op = Op()
        op.eng = eng
        op.fn = fn
        op.signals = False
        op.count = None
        op.is_dma = is_dma
        op.lane = None
        lst = self.ops[eng]
        op.idx = len(lst)
        deps = []
        for r in reads:
            b = r.buf if isinstance(r, V) else r
            if b is None:
                continue
            if b.last_w is not None:
                deps.append(b.last_w)
        for w in writes:
            b = w.buf if isinstance(w, V) else w
            if b is None:
                continue
            if b.last_w is not None:
                deps.append(b.last_w)
            deps.extend(b.readers)
        fd = []
        seen = set()
        for d in deps:
            if d is op or id(d) in seen:
                continue
            seen.add(id(d))
            if (not d.is_dma) and d.eng == eng and not is_dma and (op.idx - d.idx) > self.SAME_ENG_WINDOW:
                continue
            if (not d.is_dma) and (not is_dma) and d.eng == eng == "pe":
                continue
            if (not d.is_dma) and d.eng == eng and is_dma:
                pass
            fd.append(d)
            d.signals = True
        op.deps = fd
        for r in reads:
            b = r.buf if isinstance(r, V) else r
            if b is None:
                continue
            b.readers = [x for x in b.readers if not (x.eng == eng and not x.is_dma and not is_dma)] + [op]
        for w in writes:
            b = w.buf if isinstance(w, V) else w
            if b is None:
                continue
            b.last_w = op
            b.readers = []
        lst.append(op)
        return op

    def matmul(self, out, lhsT, rhs, start=True, stop=True, **kw):
        reads = [lhsT, rhs] + ([] if start else [])
        return self.add("pe", lambda e: e.matmul(out.ap, lhsT.ap, rhs.ap, start=start, stop=stop, **kw),
                        reads=reads, writes=[out])

    def transpose(self, out, in_, ident):
        return self.add("pe", lambda e: e.transpose(out.ap, in_.ap, ident.ap), reads=[in_, ident], writes=[out])

    def act(self, out, in_, func, bias=None, scale=None, accum_out=None, eng="act"):
        reads = [in_]
        kw = {}
        if bias is not None:
            kw["bias"] = _ap(bias)
            if isinstance(bias, V):
                reads.append(bias)
        if scale is not None:
            kw["scale"] = _ap(scale)
            if isinstance(scale, V):
                reads.append(scale)
        writes = [out]
        if accum_out is not None:
            kw["accum_out"] = accum_out.ap
            writes.append(accum_out)
        return self.add(eng, lambda e: e.activation(out.ap, in_.ap, func, **kw), reads=reads, writes=writes)

    def tt(self, out, in0, in1, op, eng="dve"):
        return self.add(eng, lambda e: e.tensor_tensor(out.ap, in0.ap, in1.ap, op), reads=[in0, in1], writes=[out])

    def ts(self, out, in0, s1, s2, op0, op1=None, eng="dve", accum_out=None):
        reads = [in0] + [s for s in (s1, s2) if isinstance(s, V)]
        writes = [out]
        kw = {}
        if op1 is not None:
            kw["op1"] = op1
        if accum_out is not None:
            kw["accum_out"] = accum_out.ap
            writes.append(accum_out)
        return self.add(eng, lambda e: e.tensor_scalar(out.ap, in0.ap, _ap(s1), _ap(s2), op0, **kw),
                        reads=reads, writes=writes)

    def stt(self, out, in0, scalar, in1, op0, op1, eng="dve"):
        reads = [in0, in1] + ([scalar] if isinstance(scalar, V) else [])
        return self.add(eng, lambda e: e.scalar_tensor_tensor(out.ap, in0.ap, _ap(scalar), in1.ap, op0, op1),
                        reads=reads, writes=[out])

    def copy(self, out, in_, eng="dve"):
        if eng == "act":
            return self.add("act", lambda e: e.copy(out.ap, in_.ap), reads=[in_], writes=[out])
        return self.add(eng, lambda e: e.tensor_copy(out.ap, in_.ap), reads=[in_], writes=[out])

    def memset(self, out, val, eng="pool"):
        return self.add(eng, lambda e: e.memset(out.ap, val), writes=[out])

    def recip(self, out, in_):
        return self.add("dve", lambda e: e.reciprocal(out.ap, in_.ap), reads=[in_], writes=[out])

    def reduce(self, out, in_, op, axis=AX.X, eng="dve"):
        return self.add(eng, lambda e: e.tensor_reduce(out.ap, in_.ap, axis, op), reads=[in_], writes=[out])

    def dma(self, out, in_, q="sp", **kw):
        reads = [in_] if isinstance(in_, V) else []
        writes = [out] if isinstance(out, V) else []
        return self.add(q, lambda e: e.dma_start(out=_ap(out), in_=_ap(in_), **kw), reads=reads, writes=writes,
                        is_dma=True)

    def emit(self, final_wait_ops=()):
        nc = self.nc
        st = self.stack
        sem = {e: st.enter_context(nc.semaphore("s_" + e)) for e in ENGS}
        lanes = {e: [st.enter_context(nc.semaphore(f"l_{e}{i}")) for i in range(self.N_LANES)] for e in ENGS}
        for e in ENGS:
            c = 0
            lc = [0] * self.N_LANES
            nd = 0
            for op in self.ops[e]:
                if op.is_dma:
                    op.lane = nd % self.N_LANES
                    nd += 1
                    lc[op.lane] += 16
                    op.count = lc[op.lane]
                    op.signals = True
                elif op.signals:
                    c += 1
                    op.count = c
        for op in final_wait_ops:
            assert op.is_dma
        self.stats = {e: len(self.ops[e]) for e in ENGS}

        def run(e, engobj):
            waited = {}
            for op in self.ops[e]:
                need = {}
                for d in op.deps:
                    s = lanes[d.eng][d.lane] if d.is_dma else sem[d.eng]
                    key = id(s)
                    if need.get(key, (None, 0))[1] < d.count:
                        need[key] = (s, d.count)
                for key, (s, cnt) in need.items():
                    if waited.get(key, 0) >= cnt:
                        continue
                    engobj.wait_ge(s, cnt)
                    waited[key] = cnt
                ins = op.fn(engobj)
                if op.is_dma:
                    ins.then_inc(lanes[e][op.lane], 16)
                elif op.signals:
                    ins.then_inc(sem[e], 1)
            if e == "sp":
                for op in final_wait_ops:
                    engobj.wait_ge(lanes[op.eng][op.lane], op.count)

        with nc.Block() as block:
            @block.tensor
            def _(eng):
                run("pe", eng)

            @block.scalar
            def _(eng):
                run("act", eng)

            @block.vector
            def _(eng):
                run("dve", eng)

            @block.gpsimd
            def _(eng):
                run("pool", eng)

            @block.sync
            def _(eng):
                run("sp", eng)
        st.close()


D = 2048; KC = 16; TOK = 1024; NTT = 2; DFF = 5632; EPS = 1e-6

def new_nc():
    return bass.Bass("TRN2", target_bir_lowering=False)

class Ctx:
    pass

def setup_common(P, nc):
    C = Ctx()
    C.banks = [P.psum(f"pb{i}", [128, 512], F32) for i in range(8)]
    C.bi = 0
    C.ones = P.sbuf("ones_f", [128, 128], F32)
    P.memset(C.ones, 1.0)
    C.ev = 0
    return C

def bank(C):
    b = C.banks[C.bi % getattr(C, "nrot", 8)]
    C.bi += 1
    return b

def rms_rstd(P, C, src_chunks, nfeat, rstd_out, sqtmp, T=TOK):
    n = len(src_chunks)
    for tt in range(T // 512):
        ps = bank(C)
        for k, s in enumerate(src_chunks):
            sq = sqtmp[k % len(sqtmp)]
            P.act(sq[:, 0:512], s[:, tt*512:(tt+1)*512], AF.Square)
            P.matmul(ps, C.ones, sq[:, 0:512], start=(k == 0), stop=(k == n-1))
        P.act(rstd_out[:, tt*512:(tt+1)*512], ps, AF.Sqrt, bias=EPS, scale=1.0/nfeat)
    P.recip(rstd_out, rstd_out)

def load_w(P, wbuf, w_dram, k0, kn, f0, fw):
    dst = wbuf[:, 0:kn*fw].rearrange("p (k f) -> p k f", f=fw)
    src = w_dram[k0*128:(k0+kn)*128, f0:f0+fw].rearrange("(k p) f -> p k f", p=128)
    P.dma(dst, src, q="pool")
    return dst

def linear(P, C, w_dram, k0, kn, f_lo, f_hi, act, wbufs, consume, T=TOK, BW=256):
    bidx = getattr(C, "wrot", 0)
    for fb in range(f_lo, f_hi, BW):
        fw = min(BW, f_hi - fb)
        wb = load_w(P, wbufs[bidx % len(wbufs)], w_dram, k0, kn, fb, fw)
        bidx += 1
        for fc in range(0, fw, 128):
            fsz = min(128, fw - fc)
            for tt in range(T // 512):
                ps = bank(C)
                for k in range(kn):
                    P.matmul(ps[0:fsz, :], wb[:, k, fc:fc+fsz], act[:, k, tt*512:(tt+1)*512],
                             start=(k == 0), stop=(k == kn-1))
                consume(fb + fc, fsz, tt, ps[0:fsz, :])
    C.wrot = bidx

def build_t(has_post, has_pre, f_next, is_last):
    nc = new_nc()
    dt = lambda name, shape, kind="ExternalInput", d=F32: nc.dram_tensor(name, list(shape), d, kind=kind).ap()
    xT = dt("xT", [D, TOK])
    NCOL = 16 * 8 + 8
    cols_d = dt("cols", [128, NCOL])
    if has_post:
        oT = dt("oT", [D, TOK])
        gT = dt("gT", [1024 if has_post == "ab" else 2048, TOK])
        w_out = dt("w_out", [D, D])
        w1 = dt("w1", [D, DFF]); w3 = dt("w3", [D, DFF]); w2 = dt("w2", [DFF, D])
    if has_pre:
        w_in = dt("w_in", [D, f_next])
        projT = dt("projT", [f_next, TOK], kind="ExternalOutput")
    xT_out = dt("xT_out", [D, TOK], kind="ExternalOutput")

    P = Prog(nc)
    C = setup_common(P, nc)
    x = P.sbuf("x_sb", [128, KC, TOK], F32)
    xs = [x.sub((slice(None), k, slice(None)), f"x{k}") for k in range(KC)]
    hb = P.sbuf("hb", [128, KC, TOK], BF16)
    cols = P.sbuf("cols_sb", [128, NCOL], F32)
    P.dma(cols, cols_d)
    for k in range(KC):
        P.dma(xs[k], xT[k*128:(k+1)*128, :])
    wA = [P.sbuf(f"wA{i}", [128, 4096], BF16) for i in range(3)]
    wB = [P.sbuf(f"wB{i}", [128, 4096], BF16) for i in range(3)]
    tmp = [P.sbuf(f"tmp{i}", [128, TOK], F32) for i in range(3)]
    rstd = P.sbuf("rstd", [128, TOK], F32)
    stg = [P.sbuf(f"stg{i}", [128, 512], F32) for i in range(3)]
    gmod = P.sbuf("gmod", [128, 16], F32)
    fin = []
    def prenorm(ng0, sh0, sc0):
        rms_rstd(P, C, xs, D, rstd, tmp[0:2])
        P.ts(gmod, cols[:, sc0:sc0+16], 1.0, None, ALU.add)
        P.tt(gmod, gmod, cols[:, ng0:ng0+16], ALU.mult)
        for k in range(KC):
            t = tmp[k % 2]
            P.stt(t, xs[k], gmod[:, k:k+1], rstd, ALU.mult, ALU.mult)
            P.act(hb[:, k, :], t, AF.Identity, bias=cols[:, sh0+k:sh0+k+1])

    def resid_consumer(gate0):
        def consume(f0, fsz, tt, ps):
            k = f0 // 128
            assert f0 % 128 == 0 and fsz == 128
            xv = xs[k][:, tt*512:(tt+1)*512]
            P.stt(xv, ps, cols[:, gate0+k:gate0+k+1], xv, ALU.mult, ALU.add)
        return consume

    if has_post:
        if has_post == "ab":
            for k in range(8):
                t = tmp[k % 2]
                P.dma(t, oT[k*128:(k+1)*128, :])
                P.copy(hb[:, k, :], t, eng="act" if k % 2 else "dve")
            groups = [[8 + k] for k in range(8)]
            ncol0 = 128
        else:
            groups = [[4*h + j for j in range(4)] for h in range(4)]
            ncol0 = 128
        ot = [P.sbuf(f"ot{i}", [128, TOK], F32) for i in range(4)]
        for grp in groups:
            srcs = []
            for j, k in enumerate(grp):
                P.dma(ot[j], oT[k*128:(k+1)*128, :])
                srcs.append(ot[j])
            rms_rstd(P, C, srcs, 128 * len(grp), rstd, tmp[0:2])
            for j, k in enumerate(grp):
                gk = k - 8 if has_post == "ab" else k
                zt = tmp[2]
                P.dma(zt, gT[gk*128:(gk+1)*128, :])
                P.act(zt, zt, AF.Silu)
                t = tmp[j % 2]
                P.stt(t, ot[j], cols[:, ncol0+j:ncol0+j+1], rstd, ALU.mult, ALU.mult)
                P.tt(hb[:, k, :], t, zt, ALU.mult)
        linear(P, C, w_out, 0, KC, 0, D, hb, wA, resid_consumer(0))
        prenorm(16, 32, 48)
        gblk = P.sbuf("gblk", [128, 11, TOK], BF16)
        for bi in range(4):
            c0 = bi * 11
            bidx = 0
            for fb in range(c0*128, (c0+11)*128, 256):
                fw = min(256, (c0+11)*128 - fb)
                wa = load_w(P, wA[bidx % 3], w1, 0, KC, fb, fw)
                wb = load_w(P, wB[bidx % 3], w3, 0, KC, fb, fw)
                bidx += 1
                for fc in range(0, fw, 128):
                    kk = (fb + fc) // 128 - c0
                    for tt in range(NTT):
                        pa = bank(C); pb = bank(C)
                        for k in range(KC):
                            P.matmul(pa, wa[:, k, fc:fc+128], hb[:, k, tt*512:(tt+1)*512], start=(k == 0), stop=(k == KC-1))
                        for k in range(KC):
                            P.matmul(pb, wb[:, k, fc:fc+128], hb[:, k, tt*512:(tt+1)*512], start=(k == 0), stop=(k == KC-1))
                        s = stg[C.ev % 3]; C.ev += 1
                        P.act(s, pa, AF.Silu)
                        P.tt(gblk[:, kk, tt*512:(tt+1)*512], s, pb, ALU.mult)
            linear(P, C, w2, c0, 11, 0, D, gblk, wA, resid_consumer(64))
    if has_pre:
        prenorm(80, 96, 112)
        def consume(f0, fsz, tt, ps):
            s = stg[C.ev % 3]
            if C.ev % 2:
                P.copy(s[0:fsz, :], ps, eng="act")
            else:
                P.copy(s[0:fsz, :], ps, eng="dve")
            C.ev += 1
            fin.append(P.dma(projT[f0:f0+fsz, tt*512:(tt+1)*512], s[0:fsz, :]))
        linear(P, C, w_in, 0, KC, 0, f_next, hb, wA, consume)
    if is_last:
        rms_rstd(P, C, xs, D, rstd, tmp[0:2])
        for k in range(KC):
            P.stt(xs[k], xs[k], cols[:, 80+k:80+k+1], rstd, ALU.mult, ALU.mult)
    for k in range(KC):
        fin.append(P.dma(xT_out[k*128:(k+1)*128, :], xs[k]))
    P.emit(final_wait_ops=fin)
    return nc

def build_ada():
    nc = new_nc()
    c_col = nc.dram_tensor("c_col", [128, 16], F32, kind="ExternalInput").ap()
    w = nc.dram_tensor("w", [8, D, 768], F32, kind="ExternalInput").ap()
    b = nc.dram_tensor("b", [128, 48], F32, kind="ExternalInput").ap()
    o = nc.dram_tensor("o", [128, 48], F32, kind="ExternalOutput").ap()
    P = Prog(nc)
    cc = P.sbuf("cc", [128, 16], F32)
    sc = P.sbuf("sc", [128, 16], F32)
    bb = P.sbuf("bb", [128, 48], F32)
    ob = P.sbuf("ob", [128, 48], F32)
    P.dma(cc, c_col); P.dma(bb, b)
    P.act(sc, cc, AF.Silu)
    wb = [P.sbuf(f"w{i}", [128, 16, 768], F32) for i in range(2)]
    ps = P.psum("ps", [128, 512], F32)
    for m in range(8):
        wt = wb[m % 2]
        for q in range(4):
            P.dma(wt[:, 4*q:4*q+4, :], w[m, q*512:(q+1)*512, :].rearrange("(k p) f -> p k f", p=128), q="sp" if q % 2 else "act")
        for fc in range(6):
            for k in range(16):
                P.matmul(ps[:, m*6+fc:m*6+fc+1], wt[:, k, fc*128:(fc+1)*128], sc[:, k:k+1], start=(k == 0), stop=(k == 15))
    P.tt(ob, ps[:, 0:48], bb, ALU.add)
    f = P.dma(o, ob)
    P.emit(final_wait_ops=[f])
    return nc


S = 8192; NT = 16; EPS = 1e-6

def mla_phase(P, C, nc, pfx=""):
    dt = lambda name, shape, kind="ExternalInput", d=F32: nc.dram_tensor(pfx + name, list(shape), d, kind=kind).ap()
    cqT = dt("cqT", [512, S]); ckvT = dt("ckvT", [512, S]); krT = dt("krT", [64, S]); krsT = dt("krsT", [64, S])
    ncols = dt("mla_cols", [128, 10])
    wq = dt("wq", [512, 256])
    wkv = dt("wkv", [512, 256])
    pos = dt("pos", [1, S], d=I32)
    masks_d = dt("masks", [128, 4, 512])
    oT = dt("oaT", [128, S], kind="ExternalOutput")
    fin = []
    SCALE = 1.0 / np.sqrt(192.0)

    cols = P.sbuf("mcols", [128, 10], F32); P.dma(cols, ncols)
    wq_b = P.sbuf("wq_b", [128, 4, 256], BF16); P.dma(wq_b, wq.rearrange("(k p) f -> p k f", p=128), q="pool")
    wkv_b = P.sbuf("wkv_b", [128, 4, 256], BF16); P.dma(wkv_b, wkv.rearrange("(k p) f -> p k f", p=128), q="pool")
    mk = P.sbuf("mk", [128, 4, 512], BF16); P.dma(mk, masks_d, q="pool")
    ones_b = P.sbuf("ones_b", [128, 128], BF16); P.memset(ones_b, 1.0)

    qa = P.sbuf("qa", [128, S], BF16); qb = P.sbuf("qb", [64, S], BF16)
    ka = P.sbuf("ka", [128, S], BF16); kb = P.sbuf("kb", [64, S], BF16)
    vt = P.sbuf("vt", [128, 64, 128], BF16)
    cos2 = P.sbuf("cos2", [64, S], F32); sinpm = P.sbuf("sinpm", [64, S], F32)
    tl = lambda t: slice(t*512, (t+1)*512)
    qa_t = [qa.sub((slice(None), tl(t))) for t in range(NT)]; qb_t = [qb.sub((slice(None), tl(t))) for t in range(NT)]
    ka_t = [ka.sub((slice(None), tl(t))) for t in range(NT)]; kb_t = [kb.sub((slice(None), tl(t))) for t in range(NT)]
    vt_t = [vt.sub((slice(None), slice(4*t, 4*t+4), slice(None))) for t in range(NT)]

    SEG = 512
    pi_ = P.sbuf("pos_i", [64, SEG], I32); u = P.sbuf("rp_u", [64, SEG], F32); kf = P.sbuf("rp_kf", [64, SEG], F32)
    ki = P.sbuf("rp_ki", [64, SEG], I32)
    TWO_PI = float(2*np.pi)
    for sg in range(S // SEG):
        sl = slice(sg*SEG, (sg+1)*SEG)
        P.dma(pi_, pos[:, sl].partition_broadcast(64))
        P.copy(kf, pi_)
        for which, off, dst in (("sin", 0.5, sinpm), ("cos", 0.75, cos2)):
            P.ts(u, kf, cols[0:64, 8:9], 1.0/TWO_PI, ALU.mult, ALU.mult)
            P.ts(u, u, off, None, ALU.add)
            P.copy(ki, u)
            ang = dst[:, sl]
            P.copy(ang, ki)
            P.tt(u, u, ang, ALU.subtract)
            P.ts(ang, u, 0.0, None, ALU.is_lt)
            P.tt(u, u, ang, ALU.add)
            P.act(ang, u, AF.Sin, bias=-float(np.pi), scale=TWO_PI)
        P.ts(sinpm[:, sl], sinpm[:, sl], cols[0:64, 9:10], None, ALU.mult)

    cq0 = P.sbuf("cq0", [128, 4, 512], F32); cq = [cq0, cq0]
    cn = [P.sbuf(f"cn{i}", [128, 4, 512], BF16) for i in range(2)]
    sqt = [P.sbuf(f"sqt{i}", [128, 512], F32) for i in range(2)]
    rst = P.sbuf("rst", [128, 512], F32)
    kr = [P.sbuf(f"kr{i}", [64, 512], F32) for i in range(2)]
    krs = [P.sbuf(f"krs{i}", [64, 512], F32) for i in range(2)]
    r1 = P.sbuf("r1", [64, 512], F32); r2 = P.sbuf("r2", [64, 512], F32)
    tmpn = P.sbuf("tmpn", [128, 512], F32)

    def load_norm(src, t, ncol0, i):
        P.dma(cq[i], src[:, tl(t)].rearrange("(k p) t -> p k t", p=128))
        rms_rstd(P, C, [cq[i][:, k, :] for k in range(4)], 512, rst, sqt, T=512)
        for k in range(4):
            P.stt(tmpn, cq[i][:, k, :], cols[:, ncol0+k:ncol0+k+1], rst, ALU.mult, ALU.mult)
            P.copy(cn[i][:, k, :], tmpn, eng="act")

    def rope(dst, a, b, t):
        P.tt(r1, a, cos2[:, tl(t)], ALU.mult)
        P.tt(r2, b, sinpm[:, tl(t)], ALU.mult)
        P.tt(dst, r1, r2, ALU.add)

    for t in range(NT):
        load_norm(cqT, t, 0, 0)
        ps = bank(C)
        for k in range(4):
            P.matmul(ps, wq_b[:, k, 0:128], cn[0][:, k, :], start=(k == 0), stop=(k == 3))
        P.copy(qa_t[t], ps, eng="act")
        ps1 = bank(C)
        for k in range(4):
            P.matmul(ps1[0:64, :], wq_b[:, k, 128:192], cn[0][:, k, :], start=(k == 0), stop=(k == 3))
        ps2 = bank(C)
        for k in range(4):
            P.matmul(ps2[0:64, :], wq_b[:, k, 192:256], cn[0][:, k, :], start=(k == 0), stop=(k == 3))
        rope(qb_t[t], ps1[0:64, :], ps2[0:64, :], t)
        load_norm(ckvT, t, 4, 1)
        ps = bank(C)
        for k in range(4):
            P.matmul(ps, wkv_b[:, k, 0:128], cn[1][:, k, :], start=(k == 0), stop=(k == 3))
        P.copy(ka_t[t], ps, eng="act")
        ps = bank(C)
        for blk in range(4):
            for k in range(4):
                P.matmul(ps[:, blk*128:(blk+1)*128], cn[1][:, k, blk*128:(blk+1)*128], wkv_b[:, k, 128:256],
                         start=(k == 0), stop=(k == 3))
        P.copy(vt_t[t], ps.rearrange("p (b d) -> p b d", b=4), eng="act")
        i = t % 2
        P.dma(kr[i], krT[:, tl(t)]); P.dma(krs[i], krsT[:, tl(t)])
        rope(kb_t[t], kr[i], krs[i], t)

    sb = [C.banks[0], C.banks[1], C.banks[2]]
    oacc = [C.banks[3], C.banks[4]]; sacc = [C.banks[5], C.banks[6]]
    pb = [P.sbuf(f"pexp{i}", [128, 512], BF16) for i in range(4)]
    rs = P.sbuf("rs", [128, 512], F32)
    ost = [P.sbuf(f"ost{i}", [128, 512], F32) for i in range(2)]
    n = 0
    for j in range(NT):
        oa = oacc[j % 2]; sa = sacc[j % 2]
        nkb = 4*j + 4
        for b in range(nkb):
            ps = sb[n % 3]; pe = pb[n % 4]; n += 1
            kt, ko = b // 4, (b % 4) * 128
            P.matmul(ps, ka_t[kt][:, ko:ko+128], qa_t[j], start=True, stop=False)
            P.matmul(ps, kb_t[kt][:, ko:ko+128], qb_t[j], start=False, stop=True)
            P.act(pe, ps, AF.Exp, scale=SCALE)
            if b >= 4*j:
                P.tt(pe, pe, mk[:, b - 4*j, :], ALU.mult, eng="pool")
            P.matmul(oa, vt_t[kt][:, b % 4, :], pe, start=(b == 0), stop=(b == nkb-1))
            P.matmul(sa, ones_b, pe, start=(b == 0), stop=(b == nkb-1))
        P.recip(rs, sa)
        o = ost[j % 2]
        P.tt(o, oa, rs, ALU.mult)
        fin.append(P.dma(oT[:, tl(j)], o))
    return fin

def build_mla():
    nc = new_nc()
    P = Prog(nc)
    C = setup_common(P, nc)
    fin = mla_phase(P, C, nc)
    P.emit(final_wait_ops=fin)
    return nc


S = 8192; NCH = 128; CH = 64; TILE = 512; CPT = 8

def gla_phase(P, C, nc, pfx=""):
    dt = lambda name, shape, kind="ExternalInput", d=F32: nc.dram_tensor(pfx + name, list(shape), d, kind=kind).ap()
    qT = dt("gqT", [256, S]); kT = dt("gkT", [256, S])
    ktok = dt("gktok", [S, 256]); vtok = dt("gvtok", [S, 256])
    glow = dt("glowT", [16, S]); waug = dt("gwaug", [17, 256])
    cst = dt("gcst", [64, 192])
    o_d = dt("go", [S, 256], kind="ExternalOutput")
    fin = []
    cs = P.sbuf("gcst_sb", [64, 192], F32); P.dma(cs, cst)
    triS = cs[:, 0:64]; triR = cs[:, 64:128]; maskT = cs[:, 128:192]
    wa = P.sbuf("gwa", [17, 256], F32); P.dma(wa, waug)
    st = P.sbuf("gstate", [128, 2, 256], F32); P.memset(st, 0.0)
    stb = P.sbuf("gstate_b", [128, 2, 256], BF16); P.memset(stb, 0.0)
    NB = 2
    q_in = [P.sbuf(f"gq_in{i}", [128, 2, TILE], F32) for i in range(NB)]
    k_in = [P.sbuf(f"gk_in{i}", [128, 2, TILE], F32) for i in range(NB)]
    kt_in = [P.sbuf(f"gkt_in{i}", [64, CPT, 256], F32) for i in range(NB)]
    v_in = [P.sbuf(f"gv_in{i}", [64, CPT, 256], BF16) for i in range(NB)]
    gl_in = [P.sbuf(f"ggl_in{i}", [17, TILE], F32) for i in range(NB)]
    for g in gl_in:
        P.memset(g, 1.0)
    o_st = [P.sbuf(f"go_st{i}", [64, CPT, 256], F32) for i in range(NB)]
    R = 3
    e1 = [P.sbuf(f"ge1{i}", [64, 256], F32) for i in range(R)]
    lnv = [P.sbuf(f"glnv{i}", [64, 256], F32) for i in range(R)]
    E = [P.sbuf(f"gE{i}", [128, 128], F32) for i in range(R+1)]
    Ei = [P.sbuf(f"gEi{i}", [128, 128], F32) for i in range(R)]
    Er = [P.sbuf(f"gEr{i}", [64, 256], F32) for i in range(R)]
    qt = [P.sbuf(f"gqt{i}", [128, 2, 64], BF16) for i in range(R)]
    kt = [P.sbuf(f"gkt{i}", [128, 2, 64], BF16) for i in range(R)]
    kd = [P.sbuf(f"gkd{i}", [64, 256], BF16) for i in range(R)]
    atm = [P.sbuf(f"gatm{i}", [64, 64], BF16) for i in range(R)]
    for t in range(S // TILE):
        i = t % NB
        sl = slice(t*TILE, (t+1)*TILE)
        P.dma(q_in[i], qT[:, sl].rearrange("(k p) t -> p k t", p=128))
        P.dma(k_in[i], kT[:, sl].rearrange("(k p) t -> p k t", p=128))
        P.dma(kt_in[i], ktok[sl, :].rearrange("(c p) d -> p c d", p=64))
        P.dma(v_in[i], vtok[sl, :].rearrange("(c p) d -> p c d", p=64), q="pool")
        P.dma(gl_in[i][0:16, :], glow[:, sl])
        for cc in range(CPT):
            c = t*CPT + cc
            r = c % R
            cl = slice(cc*64, (cc+1)*64)
            ps = bank(C)
            P.matmul(ps[0:64, 0:256], gl_in[i][:, cl], wa)
            P.act(e1[r], ps[0:64, 0:256], AF.Exp, scale=-1.0)
            P.act(lnv[r], e1[r], AF.Ln, bias=1.0)
            ps2 = bank(C)
            for dc in range(2):
                P.matmul(ps2[:, dc*64:(dc+1)*64], lnv[r][:, dc*128:(dc+1)*128], triS)
            ps3 = bank(C)
            P.matmul(ps3[0:64, 0:256], triR, lnv[r])
            Ec = E[c % (R+1)]
            P.act(Ec, ps2[:, 0:128], AF.Exp)
            P.act(Ei[r], ps2[:, 0:128], AF.Exp, scale=-1.0)
            P.act(Er[r], ps3[0:64, 0:256], AF.Exp)
            P.stt(qt[r], q_in[i][:, :, cl], 1.0/16.0, Ec.rearrange("p (k j) -> p k j", k=2), ALU.mult, ALU.mult)
            P.tt(kt[r], k_in[i][:, :, cl], Ei[r].rearrange("p (k j) -> p k j", k=2), ALU.mult)
            P.tt(kd[r], kt_in[i][:, cc, :], Er[r], ALU.mult)
            ps4 = bank(C)
            for dc in range(2):
                P.matmul(ps4[0:64, 0:64], kt[r][:, dc, :], qt[r][:, dc, :], start=(dc == 0), stop=(dc == 1))
            P.tt(atm[r], ps4[0:64, 0:64], maskT, ALU.mult)
            ps5 = bank(C)
            for dc in range(2):
                P.matmul(ps5[0:64, 0:256], qt[r][:, dc, :], stb[:, dc, :], start=(dc == 0), stop=False)
            P.matmul(ps5[0:64, 0:256], atm[r], v_in[i][:, cc, :], start=False, stop=True)
            P.copy(o_st[i][:, cc, :], ps5[0:64, 0:256], eng="act")
            ps6 = bank(C)
            for dc in range(2):
                P.matmul(ps6[:, dc*256:(dc+1)*256], kd[r][:, dc*128:(dc+1)*128], v_in[i][:, cc, :])
            for dc in range(2):
                P.stt(st[:, dc, :], st[:, dc, :], Ec[:, dc*64+63:dc*64+64], ps6[:, dc*256:(dc+1)*256], ALU.mult, ALU.add)
            P.copy(stb, st, eng="act")
        fin.append(P.dma(o_d[sl, :].rearrange("(c p) e -> p c e", p=64), o_st[i]))
    return fin

def build_gla():
    nc = new_nc()
    P = Prog(nc)
    C = setup_common(P, nc)
    fin = gla_phase(P, C, nc)
    P.emit(final_wait_ops=fin)
    return nc


DEBUG = False
S = 8192; TILE = 512; CPT = 8; EPS = 1e-6

def gdn_phase(P, C, nc, pfx=""):
    dt = lambda name, shape, kind="ExternalInput", d=F32: nc.dram_tensor(pfx + name, list(shape), d, kind=kind).ap()
    xT = dt("dxT", [384, S]); cw_d = dt("dcw", [128, 14]); bl_d = dt("dbl", [1, S]); al_d = dt("dal", [1, S])
    c64_d = dt("dc64", [64, 256]); id_d = dt("did", [128, 128]); rm_d = dt("drm", [128, 512])
    o_d = dt("dobT", [128, S], kind="ExternalOutput")
    fin = []
    C.nrot = 7
    cw = P.sbuf("dcw_sb", [128, 14], F32); P.dma(cw, cw_d)
    c64 = P.sbuf("dc64_sb", [64, 256], F32); P.dma(c64, c64_d)
    I64 = c64[:, 0:64]; mLs = c64[:, 64:128]; mUi = c64[:, 128:192]; mUs = c64[:, 192:256]
    ident = P.sbuf("did_sb", [128, 128], F32); P.dma(ident, id_d)
    rmask = P.sbuf("drm_sb", [128, 512], F32); P.dma(rmask, rm_d)
    negA = P.sbuf("dnegA", [128, 1], F32)
    P.act(negA, cw[:, 12:13], AF.Exp)
    St = P.sbuf("dS", [128, 128], F32); P.memset(St, 0.0)
    Sb = P.sbuf("dSb", [128, 128], BF16); P.memset(Sb, 0.0)
    T3 = lambda v: v.rearrange("p (c j) -> p c j", j=64)
    def sb(name, shape, d=F32, n=2):
        return [P.sbuf(f"{name}{i}", shape, d) for i in range(n)]
    xin = [sb("dxq", [128, TILE+3]), sb("dxk", [128, TILE+3]), sb("dxv", [128, TILE+3])]
    y = sb("dy", [128, TILE], n=3); sq = sb("dsq", [128, TILE], n=2); rstd = sb("drstd", [128, TILE], n=1)[0]
    qn = sb("dqn", [128, TILE]); kn = sb("dkn", [128, TILE]); vv = sb("dvv", [128, TILE])
    bl = sb("dblb", [128, TILE]); al = sb("dalb", [128, TILE])
    beta = sb("dbeta", [128, TILE]); g = sb("dg", [128, TILE], n=1)[0]; gc = sb("dgc", [128, TILE]); egc = sb("degc", [128, TILE])
    ekd = sb("dekd", [128, TILE]); egl = sb("degl", [128, CPT])
    knb = sb("dknb", [128, TILE], BF16); kbb = sb("dkbb", [128, TILE], BF16); qnb = sb("dqnb", [128, TILE], BF16)
    qdb = sb("dqdb", [128, TILE], BF16)
    kb = sb("dkb", [128, TILE], n=1)[0]; kbg = sb("dkbg", [128, TILE], n=1)[0]; kdc = sb("dkdc", [128, TILE], n=1)[0]
    vb = sb("dvb", [128, TILE], n=1)[0]
    kbg_t = sb("dkbgt", [64, CPT, 128]); vb_t = sb("dvbt", [64, CPT, 128]); kdc_t = sb("dkdct", [64, CPT, 128], BF16)
    gtok = sb("dgtok", [64, CPT], n=1)[0]; tmp64 = sb("dtmp64", [64, TILE], n=2)
    D1 = sb("dD1", [64, TILE], n=1)[0]
    dec = sb("ddec", [64, TILE], n=1)[0]; decT = sb("ddecT", [64, TILE], n=1)[0]
    Lm = sb("dL", [64, TILE]); LTm = sb("dLT", [64, TILE]); Ym = sb("dY", [64, TILE]); Pn = sb("dPn", [64, TILE]); PTn = sb("dPTn", [64, TILE])
    intraT = sb("dintraT", [64, TILE], BF16)
    u_t = sb("du", [64, CPT, 128]); wT = sb("dwT", [128, TILE], BF16)
    vnew = sb("dvnew", [64, 128], BF16, n=3)
    ost = sb("dost", [128, TILE])

    NT = S // TILE
    for t in range(NT):
        i = t % 2
        sl = slice(t*TILE, (t+1)*TILE)
        for kind in range(3):
            xi = xin[kind][i]
            if t == 0:
                P.memset(xi[:, 0:3], 0.0, eng="dve")
                P.dma(xi[:, 3:TILE+3], xT[kind*128:(kind+1)*128, 0:TILE])
            else:
                P.dma(xi, xT[kind*128:(kind+1)*128, t*TILE-3:(t+1)*TILE])
            yy = y[kind]
            P.ts(yy, xi[:, 0:TILE], cw[:, kind*4:kind*4+1], None, ALU.mult)
            for j in range(1, 4):
                P.stt(yy, xi[:, j:j+TILE], cw[:, kind*4+j:kind*4+j+1], yy, ALU.mult, ALU.add)
            if kind == 2:
                P.act(vv[i], yy, AF.Silu)
            else:
                P.act(yy, yy, AF.Silu)
                P.act(sq[kind], yy, AF.Square)
                ps = bank(C)
                P.matmul(ps, C.ones, sq[kind])
                P.act(rstd, ps, AF.Sqrt, bias=EPS)
                P.recip(rstd, rstd)
                if kind == 0:
                    P.stt(qn[i], yy, float(1.0/np.sqrt(128.0)), rstd, ALU.mult, ALU.mult)
                else:
                    P.tt(kn[i], yy, rstd, ALU.mult)
        P.dma(bl[i], bl_d[:, sl].partition_broadcast(128))
        P.dma(al[i], al_d[:, sl].partition_broadcast(128))
        P.act(beta[i], bl[i], AF.Sigmoid)
        P.act(g, al[i], AF.Exp, bias=cw[:, 13:14])
        P.act(g, g, AF.Ln, bias=1.0)
        P.ts(g, g, negA, -1.0, ALU.mult, ALU.mult)
        P.add("dve", (lambda o, m, gg: (lambda e: e.tensor_tensor_scan(o.ap, m.ap, gg.ap, 0.0, ALU.mult, ALU.add)))(gc[i], rmask, g),
              reads=[rmask, g], writes=[gc[i]])
        P.act(egc[i], gc[i], AF.Exp)
        gl3 = T3(gc[i])[:, :, 63:64]
        P.tt(T3(ekd[i]), gl3.bcast([128, CPT, 64]), T3(gc[i]), ALU.subtract)
        P.act(ekd[i], ekd[i], AF.Exp)
        P.act(egl[i], T3(gc[i])[:, :, 63], AF.Exp)
        P.copy(knb[i], kn[i], eng="act")
        P.copy(qnb[i], qn[i], eng="act")
        P.tt(kb, kn[i], beta[i], ALU.mult)
        P.copy(kbb[i], kb, eng="act")
        P.tt(kbg, kb, egc[i], ALU.mult)
        P.tt(kdc, kn[i], ekd[i], ALU.mult)
        P.tt(qdb[i], qn[i], egc[i], ALU.mult)
        P.tt(vb, vv[i], beta[i], ALU.mult)
        for src, dst in ((kbg, kbg_t[i]), (vb, vb_t[i]), (kdc, kdc_t[i])):
            for half in range(2):
                ps = bank(C)
                for cq in range(4):
                    cc = half*4 + cq
                    P.transpose(ps[0:64, cq*128:(cq+1)*128], src[:, cc*64:(cc+1)*64], ident)
                P.copy(dst[:, half*4:half*4+4, :], ps[0:64, :].rearrange("p (c d) -> p c d", c=4), eng="act" if half else "dve")
        P.tt(T3(tmp64[0]), T3(gc[i][0:64, :]), I64.rearrange("p (o j) -> p o j", o=1).bcast([64, CPT, 64]), ALU.mult)
        P.reduce(gtok, T3(tmp64[0]), ALU.add)
        P.tt(T3(D1), T3(gc[i][0:64, :]), gtok.rearrange("p (c o) -> p c o", o=1).bcast([64, CPT, 64]), ALU.subtract)
        P.ts(tmp64[0], D1, 0.0, None, ALU.max)
        P.act(dec, tmp64[0], AF.Exp, scale=-1.0)
        P.ts(tmp64[1], D1, 0.0, None, ALU.min)
        P.act(decT, tmp64[1], AF.Exp)
        psL = bank(C); psLT = bank(C); psQ = bank(C)
        for cc in range(CPT):
            cl = slice(cc*64, (cc+1)*64)
            P.matmul(psL[0:64, cl], kbb[i][:, cl], knb[i][:, cl])
            P.matmul(psLT[0:64, cl], knb[i][:, cl], kbb[i][:, cl])
            P.matmul(psQ[0:64, cl], knb[i][:, cl], qnb[i][:, cl])
        bc64 = lambda m: m.rearrange("p (o j) -> p o j", o=1).bcast([64, CPT, 64])
        P.tt(T3(tmp64[0]), T3(dec), bc64(mLs), ALU.mult)
        P.stt(Pn[0], psL[0:64, :], -1.0, tmp64[0], ALU.mult, ALU.mult)
        P.tt(T3(tmp64[1]), T3(decT), bc64(mUs), ALU.mult)
        P.stt(PTn[0], psLT[0:64, :], -1.0, tmp64[1], ALU.mult, ALU.mult)
        P.tt(T3(tmp64[1]), T3(decT), bc64(mUi), ALU.mult)
        P.tt(intraT[i], psQ[0:64, :], tmp64[1], ALU.mult)
        Y = Ym[0]
        P.tt(T3(Y), T3(PTn[0]), bc64(I64), ALU.add)
        cur = 0
        for n in range(1, 6):
            nxt = 1 - cur
            psP = bank(C); psPT = bank(C)
            for cc in range(CPT):
                cl = slice(cc*64, (cc+1)*64)
                P.matmul(psP[0:64, cl], PTn[cur][:, cl], Pn[cur][:, cl])
                if n < 5:
                    P.matmul(psPT[0:64, cl], Pn[cur][:, cl], PTn[cur][:, cl])
            P.copy(Pn[nxt], psP[0:64, :], eng="act")
            if n < 5:
                P.copy(PTn[nxt], psPT[0:64, :], eng="dve")
            psY = bank(C)
            for cc in range(CPT):
                cl = slice(cc*64, (cc+1)*64)
                P.matmul(psY[0:64, cl], Pn[nxt][:, cl], Y[:, cl])
            Y2 = Ym[1] if Y is Ym[0] else Ym[0]
            P.tt(Y2, Y, psY[0:64, :], ALU.add)
            Y = Y2
            cur = nxt
        for half in range(2):
            psu = bank(C); psw = bank(C)
            for cq in range(4):
                cc = half*4 + cq
                cl = slice(cc*64, (cc+1)*64)
                P.matmul(psu[0:64, cq*128:(cq+1)*128], Y[:, cl], vb_t[i][:, cc, :])
                P.matmul(psw[:, cq*64:(cq+1)*64], kbg_t[i][:, cc, :], Y[:, cl])
            P.copy(u_t[i][:, half*4:half*4+4, :], psu[0:64, :].rearrange("p (c d) -> p c d", c=4), eng="act")
            P.copy(wT[i][:, half*256:(half+1)*256], psw[:, 0:256], eng="dve")
        if t == 0 and DEBUG:
            for nm, v_ in (("qn", qn[i]), ("kn", kn[i]), ("vv", vv[i]), ("beta", beta[i]), ("gc", gc[i]), ("egc", egc[i]), ("ekd", ekd[i])):
                dd = nc.dram_tensor("dbg_" + nm, [128, TILE], F32, kind="ExternalOutput").ap()
                fin.append(P.dma(dd, v_))
            for nm, v_ in (("Y", Y), ("dec", dec), ("decT", decT), ("P0", Pn[0]), ("D1", D1)):
                dd = nc.dram_tensor("dbg_" + nm, [64, TILE], F32, kind="ExternalOutput").ap()
                fin.append(P.dma(dd, v_))
            for nm, v_ in (("u", u_t[i]), ("vbt", vb_t[i]), ("kbgt", kbg_t[i])):
                dd = nc.dram_tensor("dbg_" + nm, [64, CPT, 128], F32, kind="ExternalOutput").ap()
                fin.append(P.dma(dd, v_))
            dd = nc.dram_tensor("dbg_gtok", [64, CPT], F32, kind="ExternalOutput").ap()
            fin.append(P.dma(dd, gtok))
        C.nrot = 7
        pso = C.banks[7]
        for cc in range(CPT):
            c = t*CPT + cc
            cl = slice(cc*64, (cc+1)*64)
            vn = vnew[c % 3]
            p1 = bank(C)
            P.matmul(p1[0:64, 0:128], wT[i][:, cl], Sb)
            P.tt(vn, u_t[i][:, cc, :], p1[0:64, 0:128], ALU.subtract)
            P.matmul(pso[:, cl], Sb, qdb[i][:, cl], start=True, stop=False)
            P.matmul(pso[:, cl], vn, intraT[i][:, cl], start=False, stop=True)
            p2 = bank(C)
            P.matmul(p2[:, 0:128], kdc_t[i][:, cc, :], vn)
            P.stt(St, St, egl[i][:, cc:cc+1], p2[:, 0:128], ALU.mult, ALU.add)
            P.copy(Sb, St, eng="act")
        P.copy(ost[i], pso, eng="act")
        fin.append(P.dma(o_d[:, sl], ost[i]))
    return fin

def build_gdn():
    nc = new_nc()
    P = Prog(nc)
    C = setup_common(P, nc)
    fin = gdn_phase(P, C, nc)
    P.emit(final_wait_ops=fin)
    return nc

def colv(v):
    v = np.asarray(v, np.float32)
    return np.ascontiguousarray(v.reshape(-1, 128).T)
def make_cols(gate_mix=None, ffn=None, pre=None, mixnorm=None):
    cols = np.zeros((128, 136), np.float32)
    if gate_mix is not None: cols[:, 0:16] = colv(gate_mix)
    if ffn is not None:
        ng, sh, sc, gt = ffn
        cols[:, 16:32] = colv(ng); cols[:, 32:48] = colv(sh); cols[:, 48:64] = colv(sc); cols[:, 64:80] = colv(gt)
    if pre is not None:
        ng, sh, sc = pre
        cols[:, 80:96] = colv(ng)
        if sh is not None:
            cols[:, 96:112] = colv(sh); cols[:, 112:128] = colv(sc)
    if mixnorm is not None:
        m = colv(mixnorm)
        cols[:, 128:128+m.shape[1]] = m
    return cols
def w_in_even(w):
    kr = w[:, 1024:1088]
    return np.ascontiguousarray(np.concatenate([w, kr[:, 32:], kr[:, :32]], axis=1))
def mla_consts():
    half = 32
    inv = (10000.0 ** (-np.arange(half, dtype=np.float32) / half)).astype(np.float32)
    invf = np.zeros(128, np.float32); invf[:64] = np.concatenate([inv, inv])
    sign = np.zeros(128, np.float32); sign[:32] = -1; sign[32:64] = 1
    k = np.arange(128)[:, None, None]; r = np.arange(4)[None, :, None]; q = np.arange(512)[None, None, :]
    masks = (q >= k + 128*r).astype(np.float32)
    return invf, sign, np.ascontiguousarray(masks)
def mla_inputs(proj, positions, q_norm, kv_norm, w_uq, w_ukv, h):
    invf, sign, masks = mla_consts()
    cols = np.zeros((128, 10), np.float32)
    cols[:, 0:4] = colv(q_norm); cols[:, 4:8] = colv(kv_norm); cols[:, 8] = invf; cols[:, 9] = sign
    wqh = w_uq[:, h*192:(h+1)*192]
    wq = np.concatenate([wqh[:, :128], wqh[:, 128:192], wqh[:, 160:192], wqh[:, 128:160]], axis=1)
    wkv = w_ukv[:, h*256:(h+1)*256]
    return {"cqT": np.ascontiguousarray(proj[:, 0:512].T), "ckvT": np.ascontiguousarray(proj[:, 512:1024].T),
            "krT": np.ascontiguousarray(proj[:, 1024:1088].T), "krsT": np.ascontiguousarray(proj[:, 5200:5264].T),
            "mla_cols": cols, "wq": np.ascontiguousarray(wq), "wkv": np.ascontiguousarray(wkv),
            "pos": np.ascontiguousarray(positions.reshape(1, -1).astype(np.int32)), "masks": masks}
def gla_consts():
    tp = np.arange(64)[:, None]; t = np.arange(64)[None, :]
    triS = np.where(tp <= t, -1.0/16.0, 0.0); triR = np.where(tp > t, -1.0/16.0, 0.0)
    maskT = np.where(t >= tp, 1.0, 0.0)
    return np.ascontiguousarray(np.concatenate([triS, triR, maskT], axis=1).astype(np.float32))
def gla_inputs(proj, w_gk2, b_gk2, core):
    h, e = core // 2, core % 2
    q = proj[:, h*256:(h+1)*256]; k = proj[:, 1024+h*256:1024+(h+1)*256]
    v = proj[:, 2048+h*512+e*256:2048+h*512+(e+1)*256]
    gl = proj[:, 6144:6160]
    waug = np.concatenate([w_gk2[:, h*256:(h+1)*256], b_gk2[None, h*256:(h+1)*256]], axis=0)
    return {"gqT": np.ascontiguousarray(q.T), "gkT": np.ascontiguousarray(k.T), "gktok": np.ascontiguousarray(k),
            "gvtok": np.ascontiguousarray(v), "glowT": np.ascontiguousarray(gl.T), "gwaug": np.ascontiguousarray(waug),
            "gcst": gla_consts()}
def gdn_consts():
    p = np.arange(64)[:, None]; f = np.arange(64)[None, :]
    I64 = (p == f); mLs = (p > f); mUi = (f >= p); mUs = (f > p)
    c64 = np.concatenate([I64, mLs, mUi, mUs], axis=1).astype(np.float32)
    rm = np.ones((128, 512), np.float32); rm[:, ::64] = 0.0
    return np.ascontiguousarray(c64), np.eye(128, dtype=np.float32), rm
def gdn_inputs(proj, conv_w, a_log, dt_bias, h):
    q0 = 1088
    x = np.concatenate([proj[:, q0+h*128:q0+(h+1)*128], proj[:, q0+1024+h*128:q0+1024+(h+1)*128],
                        proj[:, q0+2048+h*128:q0+2048+(h+1)*128]], axis=1)
    cw = np.zeros((128, 14), np.float32)
    for kind in range(3):
        cw[:, kind*4:(kind+1)*4] = conv_w[:, kind*1024+h*128:kind*1024+(h+1)*128].T
    cw[:, 12] = a_log[h]; cw[:, 13] = dt_bias[h]
    c64, ident, rm = gdn_consts()
    return {"dxT": np.ascontiguousarray(x.T), "dcw": cw, "dbl": np.ascontiguousarray(proj[:, 5184+h][None]),
            "dal": np.ascontiguousarray(proj[:, 5192+h][None]), "dc64": c64, "did": ident, "drm": rm}

_DBG = {}
_NC_CACHE = {}

def _get(key, fn):
    if key not in _NC_CACHE:
        _NC_CACHE[key] = fn()
    return _NC_CACHE[key]

def _run(nc, maps):
    return run_bass_kernel_spmd(nc, maps, core_ids=list(range(8))).results

def kernel(x, c, positions, norm_g, ada_w, ada_b, ab_w_in, mla_q_norm, mla_w_uq, mla_kv_norm, mla_w_ukv,
           gdn_conv_w, gdn_a_log, gdn_dt_bias, gdn_norm, ab_w_out, gla_w_in, gla_w_gk2, gla_b_gk2, gla_norm,
           gla_w_out, ffn_w1, ffn_w3, ffn_w2, final_norm):
    f32 = lambda a: np.asarray(a, np.float32)
    x = f32(x)[0]; c = f32(c)[0]; positions = np.asarray(positions)
    norm_g = f32(norm_g); NL = 4
    c_col = np.ascontiguousarray(c.reshape(16, 128).T)
    aw = f32(ada_w).reshape(8, 2048, 6144); ab = f32(ada_b).reshape(8, 6144)
    maps = []
    for j in range(8):
        maps.append({"c_col": c_col, "w": np.ascontiguousarray(aw[:, :, j*768:(j+1)*768]),
                     "b": np.ascontiguousarray(ab[:, j*768:(j+1)*768].reshape(8, 6, 128).transpose(2, 0, 1).reshape(128, 48))})
    res = _run(_get("ada", build_ada), maps)
    mods = np.zeros((8, 6144), np.float32)
    for j in range(8):
        mods[:, j*768:(j+1)*768] = res[j]["o"].reshape(128, 8, 6).transpose(1, 2, 0).reshape(8, 768)
    del aw, maps
    SH, SC, GT = slice(0, 2048), slice(2048, 4096), slice(4096, 6144)
    def w_in_for(l):
        return w_in_even(f32(ab_w_in[l // 2])) if l % 2 == 0 else np.ascontiguousarray(f32(gla_w_in[l // 2]))
    xT = [np.ascontiguousarray(x[j*1024:(j+1)*1024].T) for j in range(8)]
    cols = make_cols(pre=(norm_g[0, 0], mods[0][SH], mods[0][SC]))
    w_in = w_in_for(0)
    res = _run(_get(("t", None, 5264, False), lambda: build_t(None, True, 5264, False)),
               [{"xT": xT[j], "cols": cols, "w_in": w_in} for j in range(8)])
    proj = np.concatenate([res[j]["projT"].T for j in range(8)], axis=0)
    for l in range(NL):
        i = l // 2
        if l % 2 == 0:
            res = _run(_get("mla", build_mla), [mla_inputs(proj, positions, f32(mla_q_norm[i]), f32(mla_kv_norm[i]),
                                                             f32(mla_w_uq[i]), f32(mla_w_ukv[i]), h) for h in range(8)])
            o_a = np.concatenate([res[h]["oaT"].T for h in range(8)], axis=1)
            res = _run(_get("gdn", build_gdn), [gdn_inputs(proj, f32(gdn_conv_w[i]), f32(gdn_a_log[i]), f32(gdn_dt_bias[i]), h)
                                                 for h in range(8)])
            o_b = np.concatenate([res[h]["dobT"].T for h in range(8)], axis=1)
            o = np.concatenate([o_a, o_b], axis=1)
            gsrc = proj[:, 4160:5184]
            mixnorm = f32(gdn_norm[i]); w_out = f32(ab_w_out[i]); kind = "ab"
        else:
            res = _run(_get("gla", build_gla), [gla_inputs(proj, f32(gla_w_gk2[i]), f32(gla_b_gk2[i]), cc) for cc in range(8)])
            o = np.concatenate([res[cc]["go"] for cc in range(8)], axis=1)
            gsrc = proj[:, 4096:6144]
            mixnorm = f32(gla_norm[i]); w_out = f32(gla_w_out[i]); kind = "gla"
        _DBG[f"o{l}"] = o
        last = (l == NL - 1)
        if last:
            pre = (f32(final_norm), None, None); f_next = 0
        else:
            pre = (norm_g[l+1, 0], mods[2*(l+1)][SH], mods[2*(l+1)][SC]); f_next = 5264 if (l+1) % 2 == 0 else 6160
        cols = make_cols(gate_mix=mods[2*l][GT], ffn=(norm_g[l, 1], mods[2*l+1][SH], mods[2*l+1][SC], mods[2*l+1][GT]),
                         pre=pre, mixnorm=mixnorm)
        w1 = np.ascontiguousarray(f32(ffn_w1[l])); w3 = np.ascontiguousarray(f32(ffn_w3[l])); w2 = np.ascontiguousarray(f32(ffn_w2[l]))
        w_out = np.ascontiguousarray(w_out)
        maps = []
        w_in = None if last else w_in_for(l + 1)
        for j in range(8):
            tk = slice(j*1024, (j+1)*1024)
            m = {"xT": xT[j], "cols": cols, "oT": np.ascontiguousarray(o[tk].T), "gT": np.ascontiguousarray(gsrc[tk].T),
                 "w_out": w_out, "w1": w1, "w3": w3, "w2": w2}
            if not last:
                m["w_in"] = w_in
            maps.append(m)
        key = ("t", kind, f_next, last)
        res = _run(_get(key, (lambda kind=kind, f_next=f_next, last=last: build_t(kind, not last, f_next, last))), maps)
        xT = [res[j]["xT_out"] for j in range(8)]
        if not last:
            proj = np.concatenate([res[j]["projT"].T for j in range(8)], axis=0)
        _DBG[f"x{l}"] = xT
    out = np.concatenate([xT[j].T for j in range(8)], axis=0)
    return np.ascontiguousarray(out[None]).astype(np.float32)
```

```python
import numpy as np
import concourse.bass as bass
import concourse.mybir as mybir
from concourse.bass_utils import run_bass_kernel_spmd
from contextlib import ExitStack


F32 = mybir.dt.float32
BF16 = mybir.dt.bfloat16
I32 = mybir.dt.int32
AF = mybir.ActivationFunctionType
ALU = mybir.AluOpType
AX = mybir.AxisListType


class Buf:
    __slots__ = ("name", "last_w", "readers")

    def __init__(self, name=""):
        self.name = name
        self.last_w = None
        self.readers = []


class V:
    __slots__ = ("ap", "buf")

    def __init__(self, ap, buf):
        self.ap = ap
        self.buf = buf

    def __getitem__(self, k):
        return V(self.ap[k], self.buf)

    def rearrange(self, s, **kw):
        return V(self.ap.rearrange(s, **kw), self.buf)

    def bitcast(self, dt):
        return V(self.ap.bitcast(dt), self.buf)

    def bcast(self, shape):
        return V(self.ap.broadcast_to(shape), self.buf)

    def sub(self, k, name=""):
        return V(self.ap[k], Buf(name))

    @property
    def shape(self):
        return self.ap.shape


class Op:
    __slots__ = ("eng", "fn", "deps", "signals", "count", "is_dma", "lane", "idx")


ENGS = ("pe", "act", "dve", "pool", "sp")


def _ap(x):
    return x.ap if isinstance(x, V) else x


class Prog:
    N_LANES = 6
    SAME_ENG_WINDOW = 6

    def __init__(self, nc):
        self.nc = nc
        self.ops = {e: [] for e in ENGS}
        self.stack = ExitStack()
        self.nbytes = 0

    def sbuf(self, name, shape, dtype):
        t = self.stack.enter_context(self.nc.sbuf_tensor(name, list(shape), dtype))
        return V(t[:], Buf(name))

    def psum(self, name, shape, dtype=F32):
        t = self.stack.enter_context(self.nc.psum_tensor(name, list(shape), dtype))
        return V(t[:], Buf(name))

    def add(self, eng, fn, reads=(), writes=(), is_dma=False):
        op = Op()
        op.eng = eng
        op.fn = fn
        op.signals = False
        op.count = None
        op.is_dma = is_dma
        op.lane = None
        lst = self.ops[eng]
        op.idx = len(lst)
        deps = []
        for r in reads:
            b = r.buf if isinstance(r, V) else r
            if b is None:
                continue
            if b.last_w is not None:
                deps.append(b.last_w)
        for w in writes:
            b = w.buf if isinstance(w, V) else w
            if b is None:
                continue
            if b.last_w is not None:
                deps.append(b.last_w)
            deps.extend(b.readers)
        fd = []
        seen = set()
        for d in deps:
            if d is op or id(d) in seen:
                continue
            seen.add(id(d))
            if (not d.is_dma) and d.eng == eng and not is_dma and (op.idx - d.idx) > self.SAME_ENG_WINDOW:
                continue
            if (not d.is_dma) and (not is_dma) and d.eng == eng == "pe":
                continue
            if (not d.is_dma) and d.eng == eng and is_dma:
                pass
            fd.append(d)
            d.signals = True
        op.deps = fd
        for r in reads:
            b = r.buf if isinstance(r, V) else r
            if b is None:
                continue
            b.readers = [x for x in b.readers if not (x.eng == eng and not x.is_dma and not is_dma)] + [op]
        for w in writes:
            b = w.buf if isinstance(w, V) else w
            if b is None:
                continue
            b.last_w = op
            b.readers = []
        lst.append(op)
        return op

    def matmul(self, out, lhsT, rhs, start=True, stop=True, **kw):
        reads = [lhsT, rhs] + ([] if start else [])
        return self.add("pe", lambda e: e.matmul(out.ap, lhsT.ap, rhs.ap, start=start, stop=stop, **kw),
                        reads=reads, writes=[out])

    def transpose(self, out, in_, ident):
        return self.add("pe", lambda e: e.transpose(out.ap, in_.ap, ident.ap), reads=[in_, ident], writes=[out])

    def act(self, out, in_, func, bias=None, scale=None, accum_out=None, eng="act"):
        reads = [in_]
        kw = {}
        if bias is not None:
            kw["bias"] = _ap(bias)
            if isinstance(bias, V):
                reads.append(bias)
        if scale is not None:
            kw["scale"] = _ap(scale)
            if isinstance(scale, V):
                reads.append(scale)
        writes = [out]
        if accum_out is not None:
            kw["accum_out"] = accum_out.ap
            writes.append(accum_out)
        return self.add(eng, lambda e: e.activation(out.ap, in_.ap, func, **kw), reads=reads, writes=writes)

    def tt(self, out, in0, in1, op, eng="dve"):
        return self.add(eng, lambda e: e.tensor_tensor(out.ap, in0.ap, in1.ap, op), reads=[in0, in1], writes=[out])

    def ts(self, out, in0, s1, s2, op0, op1=None, eng="dve", accum_out=None):
        reads = [in0] + [s for s in (s1, s2) if isinstance(s, V)]
        writes = [out]
        kw = {}
        if op1 is not None:
            kw["op1"] = op1
        if accum_out is not None:
            kw["accum_out"] = accum_out.ap
            writes.append(accum_out)
        return self.add(eng, lambda e: e.tensor_scalar(out.ap, in0.ap, _ap(s1), _ap(s2), op0, **kw),
                        reads=reads, writes=writes)

    def stt(self, out, in0, scalar, in1, op0, op1, eng="dve"):
        reads = [in0, in1] + ([scalar] if isinstance(scalar, V) else [])
        return self.add(eng, lambda e: e.scalar_tensor_tensor(out.ap, in0.ap, _ap(scalar), in1.ap, op0, op1),
                        reads=reads, writes=[out])

    def copy(self, out, in_, eng="dve"):
        if eng == "act":
            return self.add("act", lambda e: e.copy(out.ap, in_.ap), reads=[in_], writes=[out])
        return self.add(eng, lambda e: e.tensor_copy(out.ap, in_.ap), reads=[in_], writes=[out])

    def memset(self, out, val, eng="pool"):
        return self.add(eng, lambda e: e.memset(out.ap, val), writes=[out])

    def recip(self, out, in_):
        return self.add("dve", lambda e: e.reciprocal(out.ap, in_.ap), reads=[in_], writes=[out])

    def reduce(self, out, in_, op, axis=AX.X, eng="dve"):
        return self.add(eng, lambda e: e.tensor_reduce(out.ap, in_.ap, axis, op), reads=[in_], writes=[out])

    def dma(self, out, in_, q="sp", **kw):
        reads = [in_] if isinstance(in_, V) else []
        writes = [out] if isinstance(out, V) else []
        return self.add(q, lambda e: e.dma_start(out=_ap(out), in_=_ap(in_), **kw), reads=reads, writes=writes,
                        is_dma=True)

    def emit(self, final_wait_ops=()):
        nc = self.nc
        st = self.stack
        sem = {e: st.enter_context(nc.semaphore("s_" + e)) for e in ENGS}
        lanes = {e: [st.enter_context(nc.semaphore(f"l_{e}{i}")) for i in range(self.N_LANES)] for e in ENGS}
        for e in ENGS:
            c = 0
            lc = [0] * self.N_LANES
            nd = 0
            for op in self.ops[e]:
                if op.is_dma:
                    op.lane = nd % self.N_LANES
                    nd += 1
                    lc[op.lane] += 16
                    op.count = lc[op.lane]
                    op.signals = True
                elif op.signals:
                    c += 1
                    op.count = c
        for op in final_wait_ops:
            assert op.is_dma
        self.stats = {e: len(self.ops[e]) for e in ENGS}

        def run(e, engobj):
            waited = {}
            for op in self.ops[e]:
                need = {}
                for d in op.deps:
                    s = lanes[d.eng][d.lane] if d.is_dma else sem[d.eng]
                    key = id(s)
                    if need.get(key, (None, 0))[1] < d.count:
                        need[key] = (s, d.count)
                for key, (s, cnt) in need.items():
                    if waited.get(key, 0) >= cnt:
                        continue
                    engobj.wait_ge(s, cnt)
                    waited[key] = cnt
                ins = op.fn(engobj)
                if op.is_dma:
                    ins.then_inc(lanes[e][op.lane], 16)
                elif op.signals:
                    ins.then_inc(sem[e], 1)
            if e == "sp":
                for op in final_wait_ops:
                    engobj.wait_ge(lanes[op.eng][op.lane], op.count)

        with nc.Block() as block:
            @block.tensor
            def _(eng):
                run("pe", eng)

            @block.scalar
            def _(eng):
                run("act", eng)

            @block.vector
            def _(eng):
                run("dve", eng)

            @block.gpsimd
            def _(eng):
                run("pool", eng)

            @block.sync
            def _(eng):
                run("sp", eng)
        st.close()


D = 2048; KC = 16; TOK = 1024; NTT = 2; DFF = 5632; EPS = 1e-6

def new_nc():
    return bass.Bass("TRN2", target_bir_lowering=False)

class Ctx:
    pass

def setup_common(P, nc):
    C = Ctx()
    C.banks = [P.psum(f"pb{i}", [128, 512], F32) for i in range(8)]
    C.bi = 0
    C.ones = P.sbuf("ones_f", [128, 128], F32)
    P.memset(C.ones, 1.0)
    C.ev = 0
    return C

def bank(C):
    b = C.banks[C.bi % getattr(C, "nrot", 8)]
    C.bi += 1
    return b

def rms_rstd(P, C, src_chunks, nfeat, rstd_out, sqtmp, T=TOK):
    n = len(src_chunks)
    for tt in range(T // 512):
        ps = bank(C)
        for k, s in enumerate(src_chunks):
            sq = sqtmp[k % len(sqtmp)]
            P.act(sq[:, 0:512], s[:, tt*512:(tt+1)*512], AF.Square)
            P.matmul(ps, C.ones, sq[:, 0:512], start=(k == 0), stop=(k == n-1))
        P.act(rstd_out[:, tt*512:(tt+1)*512], ps, AF.Sqrt, bias=EPS, scale=1.0/nfeat)
    P.recip(rstd_out, rstd_out)

def load_w(P, wbuf, w_dram, k0, kn, f0, fw):
    dst = wbuf[:, 0:kn*fw].rearrange("p (k f) -> p k f", f=fw)
    src = w_dram[k0*128:(k0+kn)*128, f0:f0+fw].rearrange("(k p) f -> p k f", p=128)
    P.dma(dst, src, q="pool")
    return dst

def linear(P, C, w_dram, k0, kn, f_lo, f_hi, act, wbufs, consume, T=TOK, BW=256):
    bidx = getattr(C, "wrot", 0)
    for fb in range(f_lo, f_hi, BW):
        fw = min(BW, f_hi - fb)
        wb = load_w(P, wbufs[bidx % len(wbufs)], w_dram, k0, kn, fb, fw)
        bidx += 1
        for fc in range(0, fw, 128):
            fsz = min(128, fw - fc)
            for tt in range(T // 512):
                ps = bank(C)
                for k in range(kn):
                    P.matmul(ps[0:fsz, :], wb[:, k, fc:fc+fsz], act[:, k, tt*512:(tt+1)*512],
                             start=(k == 0), stop=(k == kn-1))
                consume(fb + fc, fsz, tt, ps[0:fsz, :])
    C.wrot = bidx

def build_t(has_post, has_pre, f_next, is_last):
    nc = new_nc()
    dt = lambda name, shape, kind="ExternalInput", d=F32: nc.dram_tensor(name, list(shape), d, kind=kind).ap()
    xT = dt("xT", [D, TOK])
    NCOL = 16 * 8 + 8
    cols_d = dt("cols", [128, NCOL])
    if has_post:
        oT = dt("oT", [D, TOK])
        gT = dt("gT", [1024 if has_post == "ab" else 2048, TOK])
        w_out = dt("w_out", [D, D])
        w1 = dt("w1", [D, DFF]); w3 = dt("w3", [D, DFF]); w2 = dt("w2", [DFF, D])
    if has_pre:
        w_in = dt("w_in", [D, f_next])
        projT = dt("projT", [f_next, TOK], kind="ExternalOutput")
    xT_out = dt("xT_out", [D, TOK], kind="ExternalOutput")

    P = Prog(nc)
    C = setup_common(P, nc)
    x = P.sbuf("x_sb", [128, KC, TOK], F32)
    xs = [x.sub((slice(None), k, slice(None)), f"x{k}") for k in range(KC)]
    hb = P.sbuf("hb", [128, KC, TOK], BF16)
    cols = P.sbuf("cols_sb", [128, NCOL], F32)
    P.dma(cols, cols_d)
    for k in range(KC):
        P.dma(xs[k], xT[k*128:(k+1)*128, :])
    wA = [P.sbuf(f"wA{i}", [128, 4096], BF16) for i in range(3)]
    wB = [P.sbuf(f"wB{i}", [128, 4096], BF16) for i in range(3)]
    tmp = [P.sbuf(f"tmp{i}", [128, TOK], F32) for i in range(3)]
    rstd = P.sbuf("rstd", [128, TOK], F32)
    stg = [P.sbuf(f"stg{i}", [128, 512], F32) for i in range(3)]
    gmod = P.sbuf("gmod", [128, 16], F32)
    fin = []
    def prenorm(ng0, sh0, sc0):
        rms_rstd(P, C, xs, D, rstd, tmp[0:2])
        P.ts(gmod, cols[:, sc0:sc0+16], 1.0, None, ALU.add)
        P.tt(gmod, gmod, cols[:, ng0:ng0+16], ALU.mult)
        for k in range(KC):
            t = tmp[k % 2]
            P.stt(t, xs[k], gmod[:, k:k+1], rstd, ALU.mult, ALU.mult)
            P.act(hb[:, k, :], t, AF.Identity, bias=cols[:, sh0+k:sh0+k+1])

    def resid_consumer(gate0):
        def consume(f0, fsz, tt, ps):
            k = f0 // 128
            assert f0 % 128 == 0 and fsz == 128
            xv = xs[k][:, tt*512:(tt+1)*512]
            P.stt(xv, ps, cols[:, gate0+k:gate0+k+1], xv, ALU.mult, ALU.add)
        return consume

    if has_post:
        if has_post == "ab":
            for k in range(8):
                t = tmp[k % 2]
                P.dma(t, oT[k*128:(k+1)*128, :])
                P.copy(hb[:, k, :], t, eng="act" if k % 2 else "dve")
            groups = [[8 + k] for k in range(8)]
            ncol0 = 128
        else:
            groups = [[4*h + j for j in range(4)] for h in range(4)]
            ncol0 = 128
        ot = [P.sbuf(f"ot{i}", [128, TOK], F32) for i in range(4)]
        for grp in groups:
            srcs = []
            for j, k in enumerate(grp):
                P.dma(ot[j], oT[k*128:(k+1)*128, :])
                srcs.append(ot[j])
            rms_rstd(P, C, srcs, 128 * len(grp), rstd, tmp[0:2])
            for j, k in enumerate(grp):
                gk = k - 8 if has_post == "ab" else k
                zt = tmp[2]
                P.dma(zt, gT[gk*128:(gk+1)*128, :])
                P.act(zt, zt, AF.Silu)
                t = tmp[j % 2]
                P.stt(t, ot[j], cols[:, ncol0+j:ncol0+j+1], rstd, ALU.mult, ALU.mult)
                P.tt(hb[:, k, :], t, zt, ALU.mult)
        linear(P, C, w_out, 0, KC, 0, D, hb, wA, resid_consumer(0))
        prenorm(16, 32, 48)
        gblk = P.sbuf("gblk", [128, 11, TOK], BF16)
        for bi in range(4):
            c0 = bi * 11
            bidx = 0
            for fb in range(c0*128, (c0+11)*128, 256):
                fw = min(256, (c0+11)*128 - fb)
                wa = load_w(P, wA[bidx % 3], w1, 0, KC, fb, fw)
                wb = load_w(P, wB[bidx % 3], w3, 0, KC, fb, fw)
                bidx += 1
                for fc in range(0, fw, 128):
                    kk = (fb + fc) // 128 - c0
                    for tt in range(NTT):
                        pa = bank(C); pb = bank(C)
                        for k in range(KC):
                            P.matmul(pa, wa[:, k, fc:fc+128], hb[:, k, tt*512:(tt+1)*512], start=(k == 0), stop=(k == KC-1))
                        for k in range(KC):
                            P.matmul(pb, wb[:, k, fc:fc+128], hb[:, k, tt*512:(tt+1)*512], start=(k == 0), stop=(k == KC-1))
                        s = stg[C.ev % 3]; C.ev += 1
                        P.act(s, pa, AF.Silu)
                        P.tt(gblk[:, kk, tt*512:(tt+1)*512], s, pb, ALU.mult)
            linear(P, C, w2, c0, 11, 0, D, gblk, wA, resid_consumer(64))
    if has_pre:
        prenorm(80, 96, 112)
        def consume(f0, fsz, tt, ps):
            s = stg[C.ev % 3]
            if C.ev % 2:
                P.copy(s[0:fsz, :], ps, eng="act")
            else:
                P.copy(s[0:fsz, :], ps, eng="dve")
            C.ev += 1
            fin.append(P.dma(projT[f0:f0+fsz, tt*512:(tt+1)*512], s[0:fsz, :]))
        linear(P, C, w_in, 0, KC, 0, f_next, hb, wA, consume)
    if is_last:
        rms_rstd(P, C, xs, D, rstd, tmp[0:2])
        for k in range(KC):
            P.stt(xs[k], xs[k], cols[:, 80+k:80+k+1], rstd, ALU.mult, ALU.mult)
    for k in range(KC):
        fin.append(P.dma(xT_out[k*128:(k+1)*128, :], xs[k]))
    P.emit(final_wait_ops=fin)
    return nc

def build_ada():
    nc = new_nc()
    c_col = nc.dram_tensor("c_col", [128, 16], F32, kind="ExternalInput").ap()
    w = nc.dram_tensor("w", [8, D, 768], F32, kind="ExternalInput").ap()
    b = nc.dram_tensor("b", [128, 48], F32, kind="ExternalInput").ap()
    o = nc.dram_tensor("o", [128, 48], F32, kind="ExternalOutput").ap()
    P = Prog(nc)
    cc = P.sbuf("cc", [128, 16], F32)
    sc = P.sbuf("sc", [128, 16], F32)
    bb = P.sbuf("bb", [128, 48], F32)
    ob = P.sbuf("ob", [128, 48], F32)
    P.dma(cc, c_col); P.dma(bb, b)
    P.act(sc, cc, AF.Silu)
    wb = [P.sbuf(f"w{i}", [128, 16, 768], F32) for i in range(2)]
    ps = P.psum("ps", [128, 512], F32)
    for m in range(8):
        wt = wb[m % 2]
        for q in range(4):
            P.dma(wt[:, 4*q:4*q+4, :], w[m, q*512:(q+1)*512, :].rearrange("(k p) f -> p k f", p=128), q="sp" if q % 2 else "act")
        for fc in range(6):
            for k in range(16):
                P.matmul(ps[:, m*6+fc:m*6+fc+1], wt[:, k, fc*128:(fc+1)*128], sc[:, k:k+1], start=(k == 0), stop=(k == 15))
    P.tt(ob, ps[:, 0:48], bb, ALU.add)
    f = P.dma(o, ob)
    P.emit(final_wait_ops=[f])
    return nc


S = 8192; NT = 16; EPS = 1e-6

def mla_phase(P, C, nc, pfx=""):
    dt = lambda name, shape, kind="ExternalInput", d=F32: nc.dram_tensor(pfx + name, list(shape), d, kind=kind).ap()
    cqT = dt("cqT", [512, S]); ckvT = dt("ckvT", [512, S]); krT = dt("krT", [64, S]); krsT = dt("krsT", [64, S])
    ncols = dt("mla_cols", [128, 10])
    wq = dt("wq", [512, 256])
    wkv = dt("wkv", [512, 256])
    pos = dt("pos", [1, S], d=I32)
    masks_d = dt("masks", [128, 4, 512])
    oT = dt("oaT", [128, S], kind="ExternalOutput")
    fin = []
    SCALE = 1.0 / np.sqrt(192.0)

    cols = P.sbuf("mcols", [128, 10], F32); P.dma(cols, ncols)
    wq_b = P.sbuf("wq_b", [128, 4, 256], BF16); P.dma(wq_b, wq.rearrange("(k p) f -> p k f", p=128), q="pool")
    wkv_b = P.sbuf("wkv_b", [128, 4, 256], BF16); P.dma(wkv_b, wkv.rearrange("(k p) f -> p k f", p=128), q="pool")
    mk = P.sbuf("mk", [128, 4, 512], BF16); P.dma(mk, masks_d, q="pool")
    ones_b = P.sbuf("ones_b", [128, 128], BF16); P.memset(ones_b, 1.0)

    qa = P.sbuf("qa", [128, S], BF16); qb = P.sbuf("qb", [64, S], BF16)
    ka = P.sbuf("ka", [128, S], BF16); kb = P.sbuf("kb", [64, S], BF16)
    vt = P.sbuf("vt", [128, 64, 128], BF16)
    cos2 = P.sbuf("cos2", [64, S], F32); sinpm = P.sbuf("sinpm", [64, S], F32)
    tl = lambda t: slice(t*512, (t+1)*512)
    qa_t = [qa.sub((slice(None), tl(t))) for t in range(NT)]; qb_t = [qb.sub((slice(None), tl(t))) for t in range(NT)]
    ka_t = [ka.sub((slice(None), tl(t))) for t in range(NT)]; kb_t = [kb.sub((slice(None), tl(t))) for t in range(NT)]
    vt_t = [vt.sub((slice(None), slice(4*t, 4*t+4), slice(None))) for t in range(NT)]

    SEG = 512
    pi_ = P.sbuf("pos_i", [64, SEG], I32); u = P.sbuf("rp_u", [64, SEG], F32); kf = P.sbuf("rp_kf", [64, SEG], F32)
    ki = P.sbuf("rp_ki", [64, SEG], I32)
    TWO_PI = float(2*np.pi)
    for sg in range(S // SEG):
        sl = slice(sg*SEG, (sg+1)*SEG)
        P.dma(pi_, pos[:, sl].partition_broadcast(64))
        P.copy(kf, pi_)
        for which, off, dst in (("sin", 0.5, sinpm), ("cos", 0.75, cos2)):
            P.ts(u, kf, cols[0:64, 8:9], 1.0/TWO_PI, ALU.mult, ALU.mult)
            P.ts(u, u, off, None, ALU.add)
            P.copy(ki, u)
            ang = dst[:, sl]
            P.copy(ang, ki)
            P.tt(u, u, ang, ALU.subtract)
            P.ts(ang, u, 0.0, None, ALU.is_lt)
            P.tt(u, u, ang, ALU.add)
            P.act(ang, u, AF.Sin, bias=-float(np.pi), scale=TWO_PI)
        P.ts(sinpm[:, sl], sinpm[:, sl], cols[0:64, 9:10], None, ALU.mult)

    cq0 = P.sbuf("cq0", [128, 4, 512], F32); cq = [cq0, cq0]
    cn = [P.sbuf(f"cn{i}", [128, 4, 512], BF16) for i in range(2)]
    sqt = [P.sbuf(f"sqt{i}", [128, 512], F32) for i in range(2)]
    rst = P.sbuf("rst", [128, 512], F32)
    kr = [P.sbuf(f"kr{i}", [64, 512], F32) for i in range(2)]
    krs = [P.sbuf(f"krs{i}", [64, 512], F32) for i in range(2)]
    r1 = P.sbuf("r1", [64, 512], F32); r2 = P.sbuf("r2", [64, 512], F32)
    tmpn = P.sbuf("tmpn", [128, 512], F32)

    def load_norm(src, t, ncol0, i):
        P.dma(cq[i], src[:, tl(t)].rearrange("(k p) t -> p k t", p=128))
        rms_rstd(P, C, [cq[i][:, k, :] for k in range(4)], 512, rst, sqt, T=512)
        for k in range(4):
            P.stt(tmpn, cq[i][:, k, :], cols[:, ncol0+k:ncol0+k+1], rst, ALU.mult, ALU.mult)
            P.copy(cn[i][:, k, :], tmpn, eng="act")

    def rope(dst, a, b, t):
        P.tt(r1, a, cos2[:, tl(t)], ALU.mult)
        P.tt(r2, b, sinpm[:, tl(t)], ALU.mult)
        P.tt(dst, r1, r2, ALU.add)

    for t in range(NT):
        load_norm(cqT, t, 0, 0)
        ps = bank(C)
        for k in range(4):
            P.matmul(ps, wq_b[:, k, 0:128], cn[0][:, k, :], start=(k == 0), stop=(k == 3))
        P.copy(qa_t[t], ps, eng="act")
        ps1 = bank(C)
        for k in range(4):
            P.matmul(ps1[0:64, :], wq_b[:, k, 128:192], cn[0][:, k, :], start=(k == 0), stop=(k == 3))
        ps2 = bank(C)
        for k in range(4):
            P.matmul(ps2[0:64, :], wq_b[:, k, 192:256], cn[0][:, k, :], start=(k == 0), stop=(k == 3))
        rope(qb_t[t], ps1[0:64, :], ps2[0:64, :], t)
        load_norm(ckvT, t, 4, 1)
        ps = bank(C)
        for k in range(4):
            P.matmul(ps, wkv_b[:, k, 0:128], cn[1][:, k, :], start=(k == 0), stop=(k == 3))
        P.copy(ka_t[t], ps, eng="act")
        ps = bank(C)
        for blk in range(4):
            for k in range(4):
                P.matmul(ps[:, blk*128:(blk+1)*128], cn[1][:, k, blk*128:(blk+1)*128], wkv_b[:, k, 128:256],
                         start=(k == 0), stop=(k == 3))
        P.copy(vt_t[t], ps.rearrange("p (b d) -> p b d", b=4), eng="act")
        i = t % 2
        P.dma(kr[i], krT[:, tl(t)]); P.dma(krs[i], krsT[:, tl(t)])
        rope(kb_t[t], kr[i], krs[i], t)

    sb = [C.banks[0], C.banks[1], C.banks[2], C.banks[3]]
    oacc = [C.banks[4], C.banks[5]]; sbank = C.banks[6]
    pb = [P.sbuf(f"pexp{i}", [128, 512], BF16) for i in range(4)]
    accs = [P.sbuf(f"pacc{i}", [128, 512], F32) for i in range(2)]
    rs = rst
    ost = [tmpn, sqt[0]]
    pairs = [(j, b) for j in range(NT) for b in range(4*j + 4)]
    LA = 2
    def emit_qk(n):
        j, b = pairs[n]
        ps = sb[n % 4]
        kt, ko = b // 4, (b % 4) * 128
        P.matmul(ps, ka_t[kt][:, ko:ko+128], qa_t[j], start=True, stop=False)
        P.matmul(ps, kb_t[kt][:, ko:ko+128], qb_t[j], start=False, stop=True)
    def emit_rest(n):
        j, b = pairs[n]
        ps = sb[n % 4]; pe = pb[n % 4]
        kt = b // 4
        nkb = 4*j + 4
        oa = oacc[j % 2]; acc = accs[j % 2]
        P.act(pe, ps, AF.Exp, scale=SCALE)
        if b >= 4*j:
            P.tt(pe, pe, mk[:, b - 4*j, :], ALU.mult, eng="pool")
        P.matmul(oa, vt_t[kt][:, b % 4, :], pe, start=(b == 0), stop=(b == nkb-1))
        if b == 0:
            P.copy(acc, pe, eng="dve")
        else:
            P.tt(acc, acc, pe, ALU.add)
        if b == nkb - 1:
            P.matmul(sbank, C.ones, acc)
            P.recip(rs, sbank)
            o = ost[j % 2]
            P.tt(o, oa, rs, ALU.mult)
            fin.append(P.dma(oT[:, tl(j)], o))
    for n in range(len(pairs) + LA):
        if n < len(pairs):
            emit_qk(n)
        if n - LA >= 0:
            emit_rest(n - LA)
    return fin

def build_mla():
    nc = new_nc()
    P = Prog(nc)
    C = setup_common(P, nc)
    fin = mla_phase(P, C, nc)
    P.emit(final_wait_ops=fin)
    return nc


S = 8192; NCH = 128; CH = 64; TILE = 512; CPT = 8

def gla_phase(P, C, nc, pfx=""):
    dt = lambda name, shape, kind="ExternalInput", d=F32: nc.dram_tensor(pfx + name, list(shape), d, kind=kind).ap()
    qT = dt("gqT", [256, S]); kT = dt("gkT", [256, S])
    ktok = dt("gktok", [S, 256]); vtok = dt("gvtok", [S, 256])
    glow = dt("glowT", [16, S]); waug = dt("gwaug", [17, 256])
    cst = dt("gcst", [64, 192])
    o_d = dt("go", [S, 256], kind="ExternalOutput")
    fin = []
    cs = P.sbuf("gcst_sb", [64, 192], F32); P.dma(cs, cst)
    triS = cs[:, 0:64]; triR = cs[:, 64:128]; maskT = cs[:, 128:192]
    wa = P.sbuf("gwa", [17, 256], F32); P.dma(wa, waug)
    st = P.sbuf("gstate", [128, 2, 256], F32); P.memset(st, 0.0)
    stb = P.sbuf("gstate_b", [128, 2, 256], BF16); P.memset(stb, 0.0)
    NB = 3
    q_in = [P.sbuf(f"gq_in{i}", [128, 2, TILE], F32) for i in range(NB)]
    k_in = [P.sbuf(f"gk_in{i}", [128, 2, TILE], F32) for i in range(NB)]
    kt_in = [P.sbuf(f"gkt_in{i}", [64, CPT, 256], F32) for i in range(NB)]
    v_in = [P.sbuf(f"gv_in{i}", [64, CPT, 256], BF16) for i in range(NB)]
    gl_in = [P.sbuf(f"ggl_in{i}", [17, TILE], F32) for i in range(NB)]
    for g in gl_in:
        P.memset(g, 1.0)
    o_st = [P.sbuf(f"go_st{i}", [64, CPT, 256], F32) for i in range(NB)]
    R = 4
    e1 = [P.sbuf(f"ge1{i}", [64, 256], F32) for i in range(R)]
    lnv = [P.sbuf(f"glnv{i}", [64, 256], F32) for i in range(R)]
    E = [P.sbuf(f"gE{i}", [128, 128], F32) for i in range(R+1)]
    Ei = [P.sbuf(f"gEi{i}", [128, 128], F32) for i in range(R)]
    Er = [P.sbuf(f"gEr{i}", [64, 256], F32) for i in range(R)]
    qt = [P.sbuf(f"gqt{i}", [128, 2, 64], BF16) for i in range(R)]
    kt = [P.sbuf(f"gkt{i}", [128, 2, 64], BF16) for i in range(R)]
    kd = [P.sbuf(f"gkd{i}", [64, 256], BF16) for i in range(R)]
    atm = [P.sbuf(f"gatm{i}", [64, 64], BF16) for i in range(R)]
    NTL = S // TILE
    def load(t):
        i = t % NB
        sl = slice(t*TILE, (t+1)*TILE)
        P.dma(q_in[i], qT[:, sl].rearrange("(k p) t -> p k t", p=128))
        P.dma(k_in[i], kT[:, sl].rearrange("(k p) t -> p k t", p=128))
        P.dma(kt_in[i], ktok[sl, :].rearrange("(c p) d -> p c d", p=64))
        P.dma(v_in[i], vtok[sl, :].rearrange("(c p) d -> p c d", p=64), q="pool")
        P.dma(gl_in[i][0:16, :], glow[:, sl])
    def prep(c):
        t, cc = c // CPT, c % CPT
        if cc == 0:
            load(t)
        i = t % NB
        r = c % R
        cl = slice(cc*64, (cc+1)*64)
        ps = bank(C)
        P.matmul(ps[0:64, 0:256], gl_in[i][:, cl], wa)
        P.act(e1[r], ps[0:64, 0:256], AF.Exp, scale=-1.0)
        P.act(lnv[r], e1[r], AF.Ln, bias=1.0)
        ps2 = bank(C)
        for dc in range(2):
            P.matmul(ps2[:, dc*64:(dc+1)*64], lnv[r][:, dc*128:(dc+1)*128], triS)
        ps3 = bank(C)
        P.matmul(ps3[0:64, 0:256], triR, lnv[r])
        Ec = E[c % (R+1)]
        P.act(Ec, ps2[:, 0:128], AF.Exp)
        P.act(Ei[r], ps2[:, 0:128], AF.Exp, scale=-1.0)
        P.act(Er[r], ps3[0:64, 0:256], AF.Exp)
        P.stt(qt[r], q_in[i][:, :, cl], 1.0/16.0, Ec.rearrange("p (k j) -> p k j", k=2), ALU.mult, ALU.mult)
        P.tt(kt[r], k_in[i][:, :, cl], Ei[r].rearrange("p (k j) -> p k j", k=2), ALU.mult)
        P.tt(kd[r], kt_in[i][:, cc, :], Er[r], ALU.mult)
        ps4 = bank(C)
        for dc in range(2):
            P.matmul(ps4[0:64, 0:64], kt[r][:, dc, :], qt[r][:, dc, :], start=(dc == 0), stop=(dc == 1))
        P.tt(atm[r], ps4[0:64, 0:64], maskT, ALU.mult)
    def scan(c):
        t, cc = c // CPT, c % CPT
        i = t % NB
        r = c % R
        Ec = E[c % (R+1)]
        ps5 = bank(C)
        for dc in range(2):
            P.matmul(ps5[0:64, 0:256], qt[r][:, dc, :], stb[:, dc, :], start=(dc == 0), stop=False)
        P.matmul(ps5[0:64, 0:256], atm[r], v_in[i][:, cc, :], start=False, stop=True)
        P.copy(o_st[i][:, cc, :], ps5[0:64, 0:256], eng="act")
        ps6 = bank(C)
        for dc in range(2):
            P.matmul(ps6[:, dc*256:(dc+1)*256], kd[r][:, dc*128:(dc+1)*128], v_in[i][:, cc, :])
        for dc in range(2):
            P.stt(st[:, dc, :], st[:, dc, :], Ec[:, dc*64+63:dc*64+64], ps6[:, dc*256:(dc+1)*256], ALU.mult, ALU.add)
        P.copy(stb, st, eng="act")
        if cc == CPT - 1:
            sl = slice(t*TILE, (t+1)*TILE)
            fin.append(P.dma(o_d[sl, :].rearrange("(c p) e -> p c e", p=64), o_st[i]))
    NC_ = NTL * CPT
    LA = 2
    for c in range(NC_ + LA):
        if c < NC_:
            prep(c)
        if c - LA >= 0:
            scan(c - LA)
    return fin

def build_gla():
    nc = new_nc()
    P = Prog(nc)
    C = setup_common(P, nc)
    fin = gla_phase(P, C, nc)
    P.emit(final_wait_ops=fin)
    return nc


DEBUG = False
S = 8192; TILE = 512; CPT = 8; EPS = 1e-6

def gdn_phase(P, C, nc, pfx=""):
    dt = lambda name, shape, kind="ExternalInput", d=F32: nc.dram_tensor(pfx + name, list(shape), d, kind=kind).ap()
    xT = dt("dxT", [384, S]); cw_d = dt("dcw", [128, 14]); bl_d = dt("dbl", [1, S]); al_d = dt("dal", [1, S])
    c64_d = dt("dc64", [64, 256]); id_d = dt("did", [128, 128]); rm_d = dt("drm", [128, 512])
    o_d = dt("dobT", [128, S], kind="ExternalOutput")
    fin = []
    C.nrot = 7
    cw = P.sbuf("dcw_sb", [128, 14], F32); P.dma(cw, cw_d)
    c64 = P.sbuf("dc64_sb", [64, 256], F32); P.dma(c64, c64_d)
    I64 = c64[:, 0:64]; mLs = c64[:, 64:128]; mUi = c64[:, 128:192]; mUs = c64[:, 192:256]
    ident = P.sbuf("did_sb", [128, 128], F32); P.dma(ident, id_d)
    rmask = P.sbuf("drm_sb", [128, 512], F32); P.dma(rmask, rm_d)
    negA = P.sbuf("dnegA", [128, 1], F32)
    P.act(negA, cw[:, 12:13], AF.Exp)
    St = P.sbuf("dS", [128, 128], F32); P.memset(St, 0.0)
    Sb = P.sbuf("dSb", [128, 128], BF16); P.memset(Sb, 0.0)
    T3 = lambda v: v.rearrange("p (c j) -> p c j", j=64)
    def sb(name, shape, d=F32, n=2):
        return [P.sbuf(f"{name}{i}", shape, d) for i in range(n)]
    xin = [sb("dxq", [128, TILE+3]), sb("dxk", [128, TILE+3]), sb("dxv", [128, TILE+3])]
    y = sb("dy", [128, TILE], n=3); sq = sb("dsq", [128, TILE], n=2); rstd = sb("drstd", [128, TILE], n=1)[0]
    qn = sb("dqn", [128, TILE]); kn = sb("dkn", [128, TILE]); vv = sb("dvv", [128, TILE])
    bl = sb("dblb", [128, TILE]); al = sb("dalb", [128, TILE])
    beta = sb("dbeta", [128, TILE]); g = sb("dg", [128, TILE], n=1)[0]; gc = sb("dgc", [128, TILE]); egc = sb("degc", [128, TILE])
    ekd = sb("dekd", [128, TILE]); egl = sb("degl", [128, CPT])
    knb = sb("dknb", [128, TILE], BF16); kbb = sb("dkbb", [128, TILE], BF16); qnb = sb("dqnb", [128, TILE], BF16)
    qdb = sb("dqdb", [128, TILE], BF16)
    kb = sb("dkb", [128, TILE], n=1)[0]; kbg = sb("dkbg", [128, TILE], n=1)[0]; kdc = sb("dkdc", [128, TILE], n=1)[0]
    vb = sb("dvb", [128, TILE], n=1)[0]
    kbg_t = sb("dkbgt", [64, CPT, 128]); vb_t = sb("dvbt", [64, CPT, 128]); kdc_t = sb("dkdct", [64, CPT, 128], BF16)
    gtok = sb("dgtok", [64, CPT], n=1)[0]; tmp64 = sb("dtmp64", [64, TILE], n=2)
    D1 = sb("dD1", [64, TILE], n=1)[0]
    dec = sb("ddec", [64, TILE], n=1)[0]; decT = sb("ddecT", [64, TILE], n=1)[0]
    Lm = sb("dL", [64, TILE]); LTm = sb("dLT", [64, TILE]); Ym = sb("dY", [64, TILE]); Pn = sb("dPn", [64, TILE]); PTn = sb("dPTn", [64, TILE])
    intraT = sb("dintraT", [64, TILE], BF16)
    u_t = sb("du", [64, CPT, 128]); wT = sb("dwT", [128, TILE], BF16)
    vnew = sb("dvnew", [64, 128], BF16, n=3)
    ost = sb("dost", [128, TILE])

    NT = S // TILE
    def prepass(t):
        i = t % 2
        sl = slice(t*TILE, (t+1)*TILE)
        for kind in range(3):
            xi = xin[kind][i]
            if t == 0:
                P.memset(xi[:, 0:3], 0.0, eng="dve")
                P.dma(xi[:, 3:TILE+3], xT[kind*128:(kind+1)*128, 0:TILE])
            else:
                P.dma(xi, xT[kind*128:(kind+1)*128, t*TILE-3:(t+1)*TILE])
            yy = y[kind]
            P.ts(yy, xi[:, 0:TILE], cw[:, kind*4:kind*4+1], None, ALU.mult)
            for j in range(1, 4):
                P.stt(yy, xi[:, j:j+TILE], cw[:, kind*4+j:kind*4+j+1], yy, ALU.mult, ALU.add)
            if kind == 2:
                P.act(vv[i], yy, AF.Silu)
            else:
                P.act(yy, yy, AF.Silu)
                P.act(sq[kind], yy, AF.Square)
                ps = bank(C)
                P.matmul(ps, C.ones, sq[kind])
                P.act(rstd, ps, AF.Sqrt, bias=EPS)
                P.recip(rstd, rstd)
                if kind == 0:
                    P.stt(qn[i], yy, float(1.0/np.sqrt(128.0)), rstd, ALU.mult, ALU.mult)
                else:
                    P.tt(kn[i], yy, rstd, ALU.mult)
            yield
        P.dma(bl[i], bl_d[:, sl].partition_broadcast(128))
        P.dma(al[i], al_d[:, sl].partition_broadcast(128))
        P.act(beta[i], bl[i], AF.Sigmoid)
        P.act(g, al[i], AF.Exp, bias=cw[:, 13:14])
        P.act(g, g, AF.Ln, bias=1.0)
        P.ts(g, g, negA, -1.0, ALU.mult, ALU.mult)
        P.add("dve", (lambda o, m, gg: (lambda e: e.tensor_tensor_scan(o.ap, m.ap, gg.ap, 0.0, ALU.mult, ALU.add)))(gc[i], rmask, g),
              reads=[rmask, g], writes=[gc[i]])
        P.act(egc[i], gc[i], AF.Exp)
        gl3 = T3(gc[i])[:, :, 63:64]
        P.tt(T3(ekd[i]), gl3.bcast([128, CPT, 64]), T3(gc[i]), ALU.subtract)
        P.act(ekd[i], ekd[i], AF.Exp)
        P.act(egl[i], T3(gc[i])[:, :, 63], AF.Exp)
        yield
        P.copy(knb[i], kn[i], eng="act")
        P.copy(qnb[i], qn[i], eng="act")
        P.tt(kb, kn[i], beta[i], ALU.mult)
        P.copy(kbb[i], kb, eng="act")
        P.tt(kbg, kb, egc[i], ALU.mult)
        P.tt(kdc, kn[i], ekd[i], ALU.mult)
        P.tt(qdb[i], qn[i], egc[i], ALU.mult)
        P.tt(vb, vv[i], beta[i], ALU.mult)
        yield
        for src, dst in ((kbg, kbg_t[i]), (vb, vb_t[i]), (kdc, kdc_t[i])):
            for half in range(2):
                ps = bank(C)
                for cq in range(4):
                    cc = half*4 + cq
                    P.transpose(ps[0:64, cq*128:(cq+1)*128], src[:, cc*64:(cc+1)*64], ident)
                P.copy(dst[:, half*4:half*4+4, :], ps[0:64, :].rearrange("p (c d) -> p c d", c=4), eng="act" if half else "dve")
                yield
        P.tt(T3(tmp64[0]), T3(gc[i][0:64, :]), I64.rearrange("p (o j) -> p o j", o=1).bcast([64, CPT, 64]), ALU.mult)
        P.reduce(gtok, T3(tmp64[0]), ALU.add)
        P.tt(T3(D1), T3(gc[i][0:64, :]), gtok.rearrange("p (c o) -> p c o", o=1).bcast([64, CPT, 64]), ALU.subtract)
        P.ts(tmp64[0], D1, 0.0, None, ALU.max)
        P.act(dec, tmp64[0], AF.Exp, scale=-1.0)
        P.ts(tmp64[1], D1, 0.0, None, ALU.min)
        P.act(decT, tmp64[1], AF.Exp)
        yield
        psL = bank(C); psLT = bank(C); psQ = bank(C)
        for cc in range(CPT):
            cl = slice(cc*64, (cc+1)*64)
            P.matmul(psL[0:64, cl], kbb[i][:, cl], knb[i][:, cl])
            P.matmul(psLT[0:64, cl], knb[i][:, cl], kbb[i][:, cl])
            P.matmul(psQ[0:64, cl], knb[i][:, cl], qnb[i][:, cl])
        bc64 = lambda m: m.rearrange("p (o j) -> p o j", o=1).bcast([64, CPT, 64])
        P.tt(T3(tmp64[0]), T3(dec), bc64(mLs), ALU.mult)
        P.stt(Pn[0], psL[0:64, :], -1.0, tmp64[0], ALU.mult, ALU.mult)
        P.tt(T3(tmp64[1]), T3(decT), bc64(mUs), ALU.mult)
        P.stt(PTn[0], psLT[0:64, :], -1.0, tmp64[1], ALU.mult, ALU.mult)
        P.tt(T3(tmp64[1]), T3(decT), bc64(mUi), ALU.mult)
        P.tt(intraT[i], psQ[0:64, :], tmp64[1], ALU.mult)
        Y = Ym[0]
        P.tt(T3(Y), T3(PTn[0]), bc64(I64), ALU.add)
        yield
        cur = 0
        for n in range(1, 6):
            nxt = 1 - cur
            psP = bank(C); psPT = bank(C)
            for cc in range(CPT):
                cl = slice(cc*64, (cc+1)*64)
                P.matmul(psP[0:64, cl], PTn[cur][:, cl], Pn[cur][:, cl])
                if n < 5:
                    P.matmul(psPT[0:64, cl], Pn[cur][:, cl], PTn[cur][:, cl])
            P.copy(Pn[nxt], psP[0:64, :], eng="act")
            if n < 5:
                P.copy(PTn[nxt], psPT[0:64, :], eng="dve")
            yield
            psY = bank(C)
            for cc in range(CPT):
                cl = slice(cc*64, (cc+1)*64)
                P.matmul(psY[0:64, cl], Pn[nxt][:, cl], Y[:, cl])
            Y2 = Ym[1] if Y is Ym[0] else Ym[0]
            P.tt(Y2, Y, psY[0:64, :], ALU.add)
            Y = Y2
            yield
            cur = nxt
        for half in range(2):
            psu = bank(C); psw = bank(C)
            for cq in range(4):
                cc = half*4 + cq
                cl = slice(cc*64, (cc+1)*64)
                P.matmul(psu[0:64, cq*128:(cq+1)*128], Y[:, cl], vb_t[i][:, cc, :])
                P.matmul(psw[:, cq*64:(cq+1)*64], kbg_t[i][:, cc, :], Y[:, cl])
            P.copy(u_t[i][:, half*4:half*4+4, :], psu[0:64, :].rearrange("p (c d) -> p c d", c=4), eng="act")
            P.copy(wT[i][:, half*256:(half+1)*256], psw[:, 0:256], eng="dve")
            yield

    def scan_tile(t, gen):
        i = t % 2
        sl = slice(t*TILE, (t+1)*TILE)
        C.nrot = 7
        pso = C.banks[7]
        for cc in range(CPT):
            c = t*CPT + cc
            cl = slice(cc*64, (cc+1)*64)
            vn = vnew[c % 3]
            p1 = bank(C)
            P.matmul(p1[0:64, 0:128], wT[i][:, cl], Sb)
            P.tt(vn, u_t[i][:, cc, :], p1[0:64, 0:128], ALU.subtract)
            P.matmul(pso[:, cl], Sb, qdb[i][:, cl], start=True, stop=False)
            P.matmul(pso[:, cl], vn, intraT[i][:, cl], start=False, stop=True)
            p2 = bank(C)
            P.matmul(p2[:, 0:128], kdc_t[i][:, cc, :], vn)
            P.stt(St, St, egl[i][:, cc:cc+1], p2[:, 0:128], ALU.mult, ALU.add)
            P.copy(Sb, St, eng="act")
            if gen is not None:
                for _ in range(5):
                    next(gen, None)
        P.copy(ost[i], pso, eng="act")
        fin.append(P.dma(o_d[:, sl], ost[i]))

    for _ in prepass(0):
        pass
    for t in range(NT):
        gen = prepass(t + 1) if t + 1 < NT else None
        scan_tile(t, gen)
        if gen is not None:
            for _ in gen:
                pass
    return fin

def build_gdn():
    nc = new_nc()
    P = Prog(nc)
    C = setup_common(P, nc)
    fin = gdn_phase(P, C, nc)
    P.emit(final_wait_ops=fin)
    return nc

def colv(v):
    v = np.asarray(v, np.float32)
    return np.ascontiguousarray(v.reshape(-1, 128).T)
def make_cols(gate_mix=None, ffn=None, pre=None, mixnorm=None):
    cols = np.zeros((128, 136), np.float32)
    if gate_mix is not None: cols[:, 0:16] = colv(gate_mix)
    if ffn is not None:
        ng, sh, sc, gt = ffn
        cols[:, 16:32] = colv(ng); cols[:, 32:48] = colv(sh); cols[:, 48:64] = colv(sc); cols[:, 64:80] = colv(gt)
    if pre is not None:
        ng, sh, sc = pre
        cols[:, 80:96] = colv(ng)
        if sh is not None:
            cols[:, 96:112] = colv(sh); cols[:, 112:128] = colv(sc)
    if mixnorm is not None:
        m = colv(mixnorm)
        cols[:, 128:128+m.shape[1]] = m
    return cols
def w_in_even(w):
    kr = w[:, 1024:1088]
    return np.ascontiguousarray(np.concatenate([w, kr[:, 32:], kr[:, :32]], axis=1))
def mla_consts():
    half = 32
    inv = (10000.0 ** (-np.arange(half, dtype=np.float32) / half)).astype(np.float32)
    invf = np.zeros(128, np.float32); invf[:64] = np.concatenate([inv, inv])
    sign = np.zeros(128, np.float32); sign[:32] = -1; sign[32:64] = 1
    k = np.arange(128)[:, None, None]; r = np.arange(4)[None, :, None]; q = np.arange(512)[None, None, :]
    masks = (q >= k + 128*r).astype(np.float32)
    return invf, sign, np.ascontiguousarray(masks)
def mla_inputs(proj, positions, q_norm, kv_norm, w_uq, w_ukv, h):
    invf, sign, masks = mla_consts()
    cols = np.zeros((128, 10), np.float32)
    cols[:, 0:4] = colv(q_norm); cols[:, 4:8] = colv(kv_norm); cols[:, 8] = invf; cols[:, 9] = sign
    wqh = w_uq[:, h*192:(h+1)*192]
    wq = np.concatenate([wqh[:, :128], wqh[:, 128:192], wqh[:, 160:192], wqh[:, 128:160]], axis=1)
    wkv = w_ukv[:, h*256:(h+1)*256]
    return {"cqT": np.ascontiguousarray(proj[:, 0:512].T), "ckvT": np.ascontiguousarray(proj[:, 512:1024].T),
            "krT": np.ascontiguousarray(proj[:, 1024:1088].T), "krsT": np.ascontiguousarray(proj[:, 5200:5264].T),
            "mla_cols": cols, "wq": np.ascontiguousarray(wq), "wkv": np.ascontiguousarray(wkv),
            "pos": np.ascontiguousarray(positions.reshape(1, -1).astype(np.int32)), "masks": masks}
def gla_consts():
    tp = np.arange(64)[:, None]; t = np.arange(64)[None, :]
    triS = np.where(tp <= t, -1.0/16.0, 0.0); triR = np.where(tp > t, -1.0/16.0, 0.0)
    maskT = np.where(t >= tp, 1.0, 0.0)
    return np.ascontiguousarray(np.concatenate([triS, triR, maskT], axis=1).astype(np.float32))
def gla_inputs(proj, w_gk2, b_gk2, core):
    h, e = core // 2, core % 2
    q = proj[:, h*256:(h+1)*256]; k = proj[:, 1024+h*256:1024+(h+1)*256]
    v = proj[:, 2048+h*512+e*256:2048+h*512+(e+1)*256]
    gl = proj[:, 6144:6160]
    waug = np.concatenate([w_gk2[:, h*256:(h+1)*256], b_gk2[None, h*256:(h+1)*256]], axis=0)
    return {"gqT": np.ascontiguousarray(q.T), "gkT": np.ascontiguousarray(k.T), "gktok": np.ascontiguousarray(k),
            "gvtok": np.ascontiguousarray(v), "glowT": np.ascontiguousarray(gl.T), "gwaug": np.ascontiguousarray(waug),
            "gcst": gla_consts()}
def gdn_consts():
    p = np.arange(64)[:, None]; f = np.arange(64)[None, :]
    I64 = (p == f); mLs = (p > f); mUi = (f >= p); mUs = (f > p)
    c64 = np.concatenate([I64, mLs, mUi, mUs], axis=1).astype(np.float32)
    rm = np.ones((128, 512), np.float32); rm[:, ::64] = 0.0
    return np.ascontiguousarray(c64), np.eye(128, dtype=np.float32), rm
def gdn_inputs(proj, conv_w, a_log, dt_bias, h):
    q0 = 1088
    x = np.concatenate([proj[:, q0+h*128:q0+(h+1)*128], proj[:, q0+1024+h*128:q0+1024+(h+1)*128],
                        proj[:, q0+2048+h*128:q0+2048+(h+1)*128]], axis=1)
    cw = np.zeros((128, 14), np.float32)
    for kind in range(3):
        cw[:, kind*4:(kind+1)*4] = conv_w[:, kind*1024+h*128:kind*1024+(h+1)*128].T
    cw[:, 12] = a_log[h]; cw[:, 13] = dt_bias[h]
    c64, ident, rm = gdn_consts()
    return {"dxT": np.ascontiguousarray(x.T), "dcw": cw, "dbl": np.ascontiguousarray(proj[:, 5184+h][None]),
            "dal": np.ascontiguousarray(proj[:, 5192+h][None]), "dc64": c64, "did": ident, "drm": rm}

_DBG = {}
_NC_CACHE = {}

def _get(key, fn):
    if key not in _NC_CACHE:
        _NC_CACHE[key] = fn()
    return _NC_CACHE[key]

def _run(nc, maps):
    return run_bass_kernel_spmd(nc, maps, core_ids=list(range(8))).results

def kernel(x, c, positions, norm_g, ada_w, ada_b, ab_w_in, mla_q_norm, mla_w_uq, mla_kv_norm, mla_w_ukv,
           gdn_conv_w, gdn_a_log, gdn_dt_bias, gdn_norm, ab_w_out, gla_w_in, gla_w_gk2, gla_b_gk2, gla_norm,
           gla_w_out, ffn_w1, ffn_w3, ffn_w2, final_norm):
    f32 = lambda a: np.asarray(a, np.float32)
    x = f32(x)[0]; c = f32(c)[0]; positions = np.asarray(positions)
    norm_g = f32(norm_g); NL = 4
    c_col = np.ascontiguousarray(c.reshape(16, 128).T)
    aw = f32(ada_w).reshape(8, 2048, 6144); ab = f32(ada_b).reshape(8, 6144)
    maps = []
    for j in range(8):
        maps.append({"c_col": c_col, "w": np.ascontiguousarray(aw[:, :, j*768:(j+1)*768]),
                     "b": np.ascontiguousarray(ab[:, j*768:(j+1)*768].reshape(8, 6, 128).transpose(2, 0, 1).reshape(128, 48))})
    res = _run(_get("ada", build_ada), maps)
    mods = np.zeros((8, 6144), np.float32)
    for j in range(8):
        mods[:, j*768:(j+1)*768] = res[j]["o"].reshape(128, 8, 6).transpose(1, 2, 0).reshape(8, 768)
    del aw, maps
    SH, SC, GT = slice(0, 2048), slice(2048, 4096), slice(4096, 6144)
    def w_in_for(l):
        return w_in_even(f32(ab_w_in[l // 2])) if l % 2 == 0 else np.ascontiguousarray(f32(gla_w_in[l // 2]))
    xT = [np.ascontiguousarray(x[j*1024:(j+1)*1024].T) for j in range(8)]
    cols = make_cols(pre=(norm_g[0, 0], mods[0][SH], mods[0][SC]))
    w_in = w_in_for(0)
    res = _run(_get(("t", None, 5264, False), lambda: build_t(None, True, 5264, False)),
               [{"xT": xT[j], "cols": cols, "w_in": w_in} for j in range(8)])
    proj = np.concatenate([res[j]["projT"].T for j in range(8)], axis=0)
    for l in range(NL):
        i = l // 2
        if l % 2 == 0:
            res = _run(_get("mla", build_mla), [mla_inputs(proj, positions, f32(mla_q_norm[i]), f32(mla_kv_norm[i]),
                                                             f32(mla_w_uq[i]), f32(mla_w_ukv[i]), h) for h in range(8)])
            o_a = np.concatenate([res[h]["oaT"].T for h in range(8)], axis=1)
            res = _run(_get("gdn", build_gdn), [gdn_inputs(proj, f32(gdn_conv_w[i]), f32(gdn_a_log[i]), f32(gdn_dt_bias[i]), h)
                                                 for h in range(8)])
            o_b = np.concatenate([res[h]["dobT"].T for h in range(8)], axis=1)
            o = np.concatenate([o_a, o_b], axis=1)
            gsrc = proj[:, 4160:5184]
            mixnorm = f32(gdn_norm[i]); w_out = f32(ab_w_out[i]); kind = "ab"
        else:
            res = _run(_get("gla", build_gla), [gla_inputs(proj, f32(gla_w_gk2[i]), f32(gla_b_gk2[i]), cc) for cc in range(8)])
            o = np.concatenate([res[cc]["go"] for cc in range(8)], axis=1)
            gsrc = proj[:, 4096:6144]
            mixnorm = f32(gla_norm[i]); w_out = f32(gla_w_out[i]); kind = "gla"
        _DBG[f"o{l}"] = o
        last = (l == NL - 1)
        if last:
            pre = (f32(final_norm), None, None); f_next = 0
        else:
            pre = (norm_g[l+1, 0], mods[2*(l+1)][SH], mods[2*(l+1)][SC]); f_next = 5264 if (l+1) % 2 == 0 else 6160
        cols = make_cols(gate_mix=mods[2*l][GT], ffn=(norm_g[l, 1], mods[2*l+1][SH], mods[2*l+1][SC], mods[2*l+1][GT]),
                         pre=pre, mixnorm=mixnorm)
        w1 = np.ascontiguousarray(f32(ffn_w1[l])); w3 = np.ascontiguousarray(f32(ffn_w3[l])); w2 = np.ascontiguousarray(f32(ffn_w2[l]))
        w_out = np.ascontiguousarray(w_out)
        maps = []
        w_in = None if last else w_in_for(l + 1)
        for j in range(8):
            tk = slice(j*1024, (j+1)*1024)
            m = {"xT": xT[j], "cols": cols, "oT": np.ascontiguousarray(o[tk].T), "gT": np.ascontiguousarray(gsrc[tk].T),
                 "w_out": w_out, "w1": w1, "w3": w3, "w2": w2}
            if not last:
                m["w_in"] = w_in
            maps.append(m)
        key = ("t", kind, f_next, last)
        res = _run(_get(key, (lambda kind=kind, f_next=f_next, last=last: build_t(kind, not last, f_next, last))), maps)
        xT = [res[j]["xT_out"] for j in range(8)]
        if not last:
            proj = np.concatenate([res[j]["projT"].T for j in range(8)], axis=0)
        _DBG[f"x{l}"] = xT
    out = np.concatenate([xT[j].T for j in range(8)], axis=0)
    return np.ascontiguousarray(out[None]).astype(np.float32)
```

```python
import numpy as np
import concourse.bass as bass
import concourse.mybir as mybir
from concourse.bass_utils import run_bass_kernel_spmd
from contextlib import ExitStack


F32 = mybir.dt.float32
BF16 = mybir.dt.bfloat16
I32 = mybir.dt.int32
AF = mybir.ActivationFunctionType
ALU = mybir.AluOpType
AX = mybir.AxisListType


class Buf:
    __slots__ = ("name", "last_w", "readers")

    def __init__(self, name=""):
        self.name = name
        self.last_w = None
        self.readers = []


class V:
    __slots__ = ("ap", "buf")

    def __init__(self, ap, buf):
        self.ap = ap
        self.buf = buf

    def __getitem__(self, k):
        return V(self.ap[k], self.buf)

    def rearrange(self, s, **kw):
        return V(self.ap.rearrange(s, **kw), self.buf)

    def bitcast(self, dt):
        return V(self.ap.bitcast(dt), self.buf)

    def bcast(self, shape):
        return V(self.ap.broadcast_to(shape), self.buf)

    def sub(self, k, name=""):
        return V(self.ap[k], Buf(name))

    @property
    def shape(self):
        return self.ap.shape


class Op:
    __slots__ = ("eng", "fn", "deps", "signals", "count", "is_dma", "lane", "idx")


ENGS = ("pe", "act", "dve", "pool", "sp")


def _ap(x):
    return x.ap if isinstance(x, V) else x


class Prog:
    N_LANES = 6
    SAME_ENG_WINDOW = 6

    def __init__(self, nc):
        self.nc = nc
        self.ops = {e: [] for e in ENGS}
        self.stack = ExitStack()
        self.nbytes = 0

    def sbuf(self, name, shape, dtype):
        t = self.stack.enter_context(self.nc.sbuf_tensor(name, list(shape), dtype))
        return V(t[:], Buf(name))

    def psum(self, name, shape, dtype=F32):
        t = self.stack.enter_context(self.nc.psum_tensor(name, list(shape), dtype))
        return V(t[:], Buf(name))

    def add(self, eng, fn, reads=(), writes=(), is_dma=False):
        op = Op()
        op.eng = eng
        op.fn = fn
        op.signals = False
        op.count = None
        op.is_dma = is_dma
        op.lane = None
        lst = self.ops[eng]
        op.idx = len(lst)
        deps = []
        for r in reads:
            b = r.buf if isinstance(r, V) else r
            if b is None:
                continue
            if b.last_w is not None:
                deps.append(b.last_w)
        for w in writes:
            b = w.buf if isinstance(w, V) else w
            if b is None:
                continue
            if b.last_w is not None:
                deps.append(b.last_w)
            deps.extend(b.readers)
        fd = []
        seen = set()
        for d in deps:
            if d is op or id(d) in seen:
                continue
            seen.add(id(d))
            if (not d.is_dma) and d.eng == eng and not is_dma and (op.idx - d.idx) > self.SAME_ENG_WINDOW:
                continue
            if (not d.is_dma) and (not is_dma) and d.eng == eng == "pe":
                continue
            if (not d.is_dma) and d.eng == eng and is_dma:
                pass
            fd.append(d)
            d.signals = True
        op.deps = fd
        for r in reads:
            b = r.buf if isinstance(r, V) else r
            if b is None:
                continue
            b.readers = [x for x in b.readers if not (x.eng == eng and not x.is_dma and not is_dma)] + [op]
        for w in writes:
            b = w.buf if isinstance(w, V) else w
            if b is None:
                continue
            b.last_w = op
            b.readers = []
        lst.append(op)
        return op

    F32R = False

    def matmul(self, out, lhsT, rhs, start=True, stop=True, **kw):
        if self.F32R and lhsT.ap.dtype == F32 and rhs.ap.dtype == F32:
            lhsT = lhsT.bitcast(mybir.dt.float32r); rhs = rhs.bitcast(mybir.dt.float32r)
        reads = [lhsT, rhs] + ([] if start else [])
        return self.add("pe", lambda e: e.matmul(out.ap, lhsT.ap, rhs.ap, start=start, stop=stop, **kw),
                        reads=reads, writes=[out])

    def transpose(self, out, in_, ident):
        return self.add("pe", lambda e: e.transpose(out.ap, in_.ap, ident.ap), reads=[in_, ident], writes=[out])

    def act(self, out, in_, func, bias=None, scale=None, accum_out=None, eng="act"):
        reads = [in_]
        kw = {}
        if bias is not None:
            kw["bias"] = _ap(bias)
            if isinstance(bias, V):
                reads.append(bias)
        if scale is not None:
            kw["scale"] = _ap(scale)
            if isinstance(scale, V):
                reads.append(scale)
        writes = [out]
        if accum_out is not None:
            kw["accum_out"] = accum_out.ap
            writes.append(accum_out)
        return self.add(eng, lambda e: e.activation(out.ap, in_.ap, func, **kw), reads=reads, writes=writes)

    def tt(self, out, in0, in1, op, eng="dve"):
        return self.add(eng, lambda e: e.tensor_tensor(out.ap, in0.ap, in1.ap, op), reads=[in0, in1], writes=[out])

    def ts(self, out, in0, s1, s2, op0, op1=None, eng="dve", accum_out=None):
        reads = [in0] + [s for s in (s1, s2) if isinstance(s, V)]
        writes = [out]
        kw = {}
        if op1 is not None:
            kw["op1"] = op1
        if accum_out is not None:
            kw["accum_out"] = accum_out.ap
            writes.append(accum_out)
        return self.add(eng, lambda e: e.tensor_scalar(out.ap, in0.ap, _ap(s1), _ap(s2), op0, **kw),
                        reads=reads, writes=writes)

    def stt(self, out, in0, scalar, in1, op0, op1, eng="dve"):
        reads = [in0, in1] + ([scalar] if isinstance(scalar, V) else [])
        return self.add(eng, lambda e: e.scalar_tensor_tensor(out.ap, in0.ap, _ap(scalar), in1.ap, op0, op1),
                        reads=reads, writes=[out])

    def copy(self, out, in_, eng="dve"):
        if eng == "act":
            return self.add("act", lambda e: e.copy(out.ap, in_.ap), reads=[in_], writes=[out])
        return self.add(eng, lambda e: e.tensor_copy(out.ap, in_.ap), reads=[in_], writes=[out])

    def memset(self, out, val, eng="pool"):
        return self.add(eng, lambda e: e.memset(out.ap, val), writes=[out])

    def recip(self, out, in_):
        return self.add("dve", lambda e: e.reciprocal(out.ap, in_.ap), reads=[in_], writes=[out])

    def reduce(self, out, in_, op, axis=AX.X, eng="dve"):
        return self.add(eng, lambda e: e.tensor_reduce(out.ap, in_.ap, axis, op), reads=[in_], writes=[out])

    def dma(self, out, in_, q="sp", **kw):
        reads = [in_] if isinstance(in_, V) else []
        writes = [out] if isinstance(out, V) else []
        return self.add(q, lambda e: e.dma_start(out=_ap(out), in_=_ap(in_), **kw), reads=reads, writes=writes,
                        is_dma=True)

    def emit(self, final_wait_ops=()):
        nc = self.nc
        st = self.stack
        sem = {e: st.enter_context(nc.semaphore("s_" + e)) for e in ENGS}
        lanes = {e: [st.enter_context(nc.semaphore(f"l_{e}{i}")) for i in range(self.N_LANES)] for e in ENGS}
        for e in ENGS:
            c = 0
            lc = [0] * self.N_LANES
            nd = 0
            for op in self.ops[e]:
                if op.is_dma:
                    op.lane = nd % self.N_LANES
                    nd += 1
                    lc[op.lane] += 16
                    op.count = lc[op.lane]
                    op.signals = True
                elif op.signals:
                    c += 1
                    op.count = c
        for op in final_wait_ops:
            assert op.is_dma
        self.stats = {e: len(self.ops[e]) for e in ENGS}

        def run(e, engobj):
            waited = {}
            for op in self.ops[e]:
                need = {}
                for d in op.deps:
                    s = lanes[d.eng][d.lane] if d.is_dma else sem[d.eng]
                    key = id(s)
                    if need.get(key, (None, 0))[1] < d.count:
                        need[key] = (s, d.count)
                for key, (s, cnt) in need.items():
                    if waited.get(key, 0) >= cnt:
                        continue
                    engobj.wait_ge(s, cnt)
                    waited[key] = cnt
                ins = op.fn(engobj)
                if op.is_dma:
                    ins.then_inc(lanes[e][op.lane], 16)
                elif op.signals:
                    ins.then_inc(sem[e], 1)
            if e == "sp":
                for op in final_wait_ops:
                    engobj.wait_ge(lanes[op.eng][op.lane], op.count)

        with nc.Block() as block:
            @block.tensor
            def _(eng):
                run("pe", eng)

            @block.scalar
            def _(eng):
                run("act", eng)

            @block.vector
            def _(eng):
                run("dve", eng)

            @block.gpsimd
            def _(eng):
                run("pool", eng)

            @block.sync
            def _(eng):
                run("sp", eng)
        st.close()


D = 2048; KC = 16; TOK = 1024; NTT = 2; DFF = 5632; EPS = 1e-6

def new_nc():
    return bass.Bass("TRN2", target_bir_lowering=False)

class Ctx:
    pass

def setup_common(P, nc):
    C = Ctx()
    C.banks = [P.psum(f"pb{i}", [128, 512], F32) for i in range(8)]
    C.bi = 0
    C.ones = P.sbuf("ones_f", [128, 128], F32)
    P.memset(C.ones, 1.0)
    C.ones_b = P.sbuf("ones_bf", [128, 128], BF16)
    P.memset(C.ones_b, 1.0)
    C.sqb = [P.sbuf(f"sqb{i}", [128, 512], BF16) for i in range(2)]
    C.ev = 0
    return C

def bank(C):
    b = C.banks[C.bi % getattr(C, "nrot", 8)]
    C.bi += 1
    return b

def rms_rstd(P, C, src_chunks, nfeat, rstd_out, sqtmp, T=TOK):
    n = len(src_chunks)
    sqtmp = C.sqb
    for tt in range(T // 512):
        ps = bank(C)
        for k, s in enumerate(src_chunks):
            sq = sqtmp[k % len(sqtmp)]
            P.act(sq[:, 0:512], s[:, tt*512:(tt+1)*512], AF.Square)
            P.matmul(ps, C.ones_b, sq[:, 0:512], start=(k == 0), stop=(k == n-1))
        P.act(rstd_out[:, tt*512:(tt+1)*512], ps, AF.Sqrt, bias=EPS, scale=1.0/nfeat)
    P.recip(rstd_out, rstd_out)

def load_w(P, wbuf, w_dram, k0, kn, f0, fw):
    dst = wbuf[:, 0:kn*fw].rearrange("p (k f) -> p k f", f=fw)
    src = w_dram[k0*128:(k0+kn)*128, f0:f0+fw].rearrange("(k p) f -> p k f", p=128)
    P.dma(dst, src, q="pool")
    return dst

def linear(P, C, w_dram, k0, kn, f_lo, f_hi, act, wbufs, consume, T=TOK, BW=256):
    bidx = getattr(C, "wrot", 0)
    for fb in range(f_lo, f_hi, BW):
        fw = min(BW, f_hi - fb)
        wb = load_w(P, wbufs[bidx % len(wbufs)], w_dram, k0, kn, fb, fw)
        bidx += 1
        for fc in range(0, fw, 128):
            fsz = min(128, fw - fc)
            for tt in range(T // 512):
                ps = bank(C)
                for k in range(kn):
                    P.matmul(ps[0:fsz, :], wb[:, k, fc:fc+fsz], act[:, k, tt*512:(tt+1)*512],
                             start=(k == 0), stop=(k == kn-1))
                consume(fb + fc, fsz, tt, ps[0:fsz, :])
    C.wrot = bidx

def build_t(has_post, has_pre, f_next, is_last):
    nc = new_nc()
    dt = lambda name, shape, kind="ExternalInput", d=F32: nc.dram_tensor(name, list(shape), d, kind=kind).ap()
    xT = dt("xT", [D, TOK])
    NCOL = 16 * 8 + 8
    cols_d = dt("cols", [128, NCOL])
    if has_post:
        oT = dt("oT", [D, TOK])
        gT = dt("gT", [1024 if has_post == "ab" else 2048, TOK])
        w_out = dt("w_out", [D, D])
        w1 = dt("w1", [D, DFF]); w3 = dt("w3", [D, DFF]); w2 = dt("w2", [DFF, D])
    if has_pre:
        w_in = dt("w_in", [D, f_next])
        projT = dt("projT", [f_next, TOK], kind="ExternalOutput")
    xT_out = dt("xT_out", [D, TOK], kind="ExternalOutput")

    P = Prog(nc)
    C = setup_common(P, nc)
    x = P.sbuf("x_sb", [128, KC, TOK], F32)
    xs = [x.sub((slice(None), k, slice(None)), f"x{k}") for k in range(KC)]
    hb = P.sbuf("hb", [128, KC, TOK], BF16)
    cols = P.sbuf("cols_sb", [128, NCOL], F32)
    P.dma(cols, cols_d)
    def load_x():
        for k in range(KC):
            P.dma(xs[k], xT[k*128:(k+1)*128, :], q="act")
    if not has_post:
        load_x()
    wA = [P.sbuf(f"wA{i}", [128, 4096], BF16) for i in range(3)]
    wB = [P.sbuf(f"wB{i}", [128, 4096], BF16) for i in range(3)]
    tmp = [P.sbuf(f"tmp{i}", [128, TOK], F32) for i in range(3)]
    rstd = P.sbuf("rstd", [128, TOK], F32)
    stg = [P.sbuf(f"stg{i}", [128, 512], F32) for i in range(3)]
    gmod = P.sbuf("gmod", [128, 16], F32)
    fin = []
    def prenorm(ng0, sh0, sc0):
        rms_rstd(P, C, xs, D, rstd, tmp[0:2])
        P.ts(gmod, cols[:, sc0:sc0+16], 1.0, None, ALU.add)
        P.tt(gmod, gmod, cols[:, ng0:ng0+16], ALU.mult)
        for k in range(KC):
            t = tmp[k % 2]
            P.stt(t, xs[k], gmod[:, k:k+1], rstd, ALU.mult, ALU.mult)
            P.act(hb[:, k, :], t, AF.Identity, bias=cols[:, sh0+k:sh0+k+1])

    def resid_consumer(gate0):
        def consume(f0, fsz, tt, ps):
            k = f0 // 128
            assert f0 % 128 == 0 and fsz == 128
            xv = xs[k][:, tt*512:(tt+1)*512]
            P.stt(xv, ps, cols[:, gate0+k:gate0+k+1], xv, ALU.mult, ALU.add)
        return consume

    if has_post:
        if has_post == "ab":
            for k in range(8):
                t = tmp[k % 2]
                P.dma(t, oT[k*128:(k+1)*128, :])
                P.copy(hb[:, k, :], t, eng="act" if k % 2 else "dve")
            groups = [[8 + k] for k in range(8)]
            ncol0 = 128
        else:
            groups = [[4*h + j for j in range(4)] for h in range(4)]
            ncol0 = 128
        ot = [P.sbuf(f"ot{i}", [128, TOK], F32) for i in range(4)]
        for grp in groups:
            srcs = []
            for j, k in enumerate(grp):
                P.dma(ot[j], oT[k*128:(k+1)*128, :])
                srcs.append(ot[j])
            rms_rstd(P, C, srcs, 128 * len(grp), rstd, tmp[0:2])
            for j, k in enumerate(grp):
                gk = k - 8 if has_post == "ab" else k
                zt = tmp[2]
                P.dma(zt, gT[gk*128:(gk+1)*128, :])
                P.act(zt, zt, AF.Silu)
                t = tmp[j % 2]
                P.stt(t, ot[j], cols[:, ncol0+j:ncol0+j+1], rstd, ALU.mult, ALU.mult)
                P.tt(hb[:, k, :], t, zt, ALU.mult)
        load_x()
        linear(P, C, w_out, 0, KC, 0, D, hb, wA, resid_consumer(0))
        prenorm(16, 32, 48)
        gblk = P.sbuf("gblk", [128, 11, TOK], BF16)
        for bi in range(4):
            c0 = bi * 11
            bidx = 0
            for fb in range(c0*128, (c0+11)*128, 256):
                fw = min(256, (c0+11)*128 - fb)
                wa = load_w(P, wA[bidx % 3], w1, 0, KC, fb, fw)
                wb = load_w(P, wB[bidx % 3], w3, 0, KC, fb, fw)
                bidx += 1
                for fc in range(0, fw, 128):
                    kk = (fb + fc) // 128 - c0
                    for tt in range(NTT):
                        pa = bank(C); pb = bank(C)
                        for k in range(KC):
                            P.matmul(pa, wa[:, k, fc:fc+128], hb[:, k, tt*512:(tt+1)*512], start=(k == 0), stop=(k == KC-1))
                        for k in range(KC):
                            P.matmul(pb, wb[:, k, fc:fc+128], hb[:, k, tt*512:(tt+1)*512], start=(k == 0), stop=(k == KC-1))
                        s = stg[C.ev % 3]; C.ev += 1
                        P.act(s, pa, AF.Silu)
                        P.tt(gblk[:, kk, tt*512:(tt+1)*512], s, pb, ALU.mult)
            linear(P, C, w2, c0, 11, 0, D, gblk, wA, resid_consumer(64))
    if has_pre:
        prenorm(80, 96, 112)
        def consume(f0, fsz, tt, ps):
            s = stg[C.ev % 3]
            if C.ev % 2:
                P.copy(s[0:fsz, :], ps, eng="act")
            else:
                P.copy(s[0:fsz, :], ps, eng="dve")
            C.ev += 1
            fin.append(P.dma(projT[f0:f0+fsz, tt*512:(tt+1)*512], s[0:fsz, :]))
        linear(P, C, w_in, 0, KC, 0, f_next, hb, wA, consume)
    if is_last:
        rms_rstd(P, C, xs, D, rstd, tmp[0:2])
        for k in range(KC):
            P.stt(xs[k], xs[k], cols[:, 80+k:80+k+1], rstd, ALU.mult, ALU.mult)
    for k in range(KC):
        fin.append(P.dma(xT_out[k*128:(k+1)*128, :], xs[k]))
    P.emit(final_wait_ops=fin)
    return nc

def build_ada():
    nc = new_nc()
    c_col = nc.dram_tensor("c_col", [128, 16], F32, kind="ExternalInput").ap()
    w = nc.dram_tensor("w", [8, D, 768], F32, kind="ExternalInput").ap()
    b = nc.dram_tensor("b", [128, 48], F32, kind="ExternalInput").ap()
    o = nc.dram_tensor("o", [128, 48], F32, kind="ExternalOutput").ap()
    P = Prog(nc)
    cc = P.sbuf("cc", [128, 16], F32)
    sc = P.sbuf("sc", [128, 16], F32)
    bb = P.sbuf("bb", [128, 48], F32)
    ob = P.sbuf("ob", [128, 48], F32)
    P.dma(cc, c_col); P.dma(bb, b)
    P.act(sc, cc, AF.Silu)
    wb = [P.sbuf(f"w{i}", [128, 16, 768], F32) for i in range(2)]
    ps = P.psum("ps", [128, 512], F32)
    for m in range(8):
        wt = wb[m % 2]
        for q in range(4):
            P.dma(wt[:, 4*q:4*q+4, :], w[m, q*512:(q+1)*512, :].rearrange("(k p) f -> p k f", p=128), q="sp" if q % 2 else "act")
        for fc in range(6):
            for k in range(16):
                P.matmul(ps[:, m*6+fc:m*6+fc+1], wt[:, k, fc*128:(fc+1)*128], sc[:, k:k+1], start=(k == 0), stop=(k == 15))
    P.tt(ob, ps[:, 0:48], bb, ALU.add)
    f = P.dma(o, ob)
    P.emit(final_wait_ops=[f])
    return nc


S = 8192; NT = 16; EPS = 1e-6

def mla_phase(P, C, nc, pfx=""):
    dt = lambda name, shape, kind="ExternalInput", d=F32: nc.dram_tensor(pfx + name, list(shape), d, kind=kind).ap()
    cqT = dt("cqT", [512, S]); ckvT = dt("ckvT", [512, S]); krT = dt("krT", [64, S]); krsT = dt("krsT", [64, S])
    ncols = dt("mla_cols", [128, 10])
    wq = dt("wq", [512, 256])
    wkv = dt("wkv", [512, 256])
    pos = dt("pos", [1, S], d=I32)
    masks_d = dt("masks", [128, 4, 512])
    oT = dt("oaT", [128, S], kind="ExternalOutput")
    fin = []
    SCALE = 1.0 / np.sqrt(192.0)

    cols = P.sbuf("mcols", [128, 10], F32); P.dma(cols, ncols)
    wq_b = P.sbuf("wq_b", [128, 4, 256], BF16); P.dma(wq_b, wq.rearrange("(k p) f -> p k f", p=128), q="pool")
    wkv_b = P.sbuf("wkv_b", [128, 4, 256], BF16); P.dma(wkv_b, wkv.rearrange("(k p) f -> p k f", p=128), q="pool")
    mk = P.sbuf("mk", [128, 4, 512], BF16); P.dma(mk, masks_d, q="pool")
    ones_b = P.sbuf("ones_b", [128, 128], BF16); P.memset(ones_b, 1.0)

    qa = P.sbuf("qa", [128, S], BF16); qb = P.sbuf("qb", [64, S], BF16)
    ka = P.sbuf("ka", [128, S], BF16); kb = P.sbuf("kb", [64, S], BF16)
    vt = P.sbuf("vt", [128, 64, 128], BF16)
    cos2 = P.sbuf("cos2", [64, S], F32); sinpm = P.sbuf("sinpm", [64, S], F32)
    tl = lambda t: slice(t*512, (t+1)*512)
    qa_t = [qa.sub((slice(None), tl(t))) for t in range(NT)]; qb_t = [qb.sub((slice(None), tl(t))) for t in range(NT)]
    ka_t = [ka.sub((slice(None), tl(t))) for t in range(NT)]; kb_t = [kb.sub((slice(None), tl(t))) for t in range(NT)]
    vt_t = [vt.sub((slice(None), slice(4*t, 4*t+4), slice(None))) for t in range(NT)]

    SEG = 512
    pi_ = P.sbuf("pos_i", [64, SEG], I32); u = P.sbuf("rp_u", [64, SEG], F32); kf = P.sbuf("rp_kf", [64, SEG], F32)
    ki = P.sbuf("rp_ki", [64, SEG], I32)
    TWO_PI = float(2*np.pi)
    for sg in range(S // SEG):
        sl = slice(sg*SEG, (sg+1)*SEG)
        P.dma(pi_, pos[:, sl].partition_broadcast(64))
        P.copy(kf, pi_)
        for which, off, dst in (("sin", 0.5, sinpm), ("cos", 0.75, cos2)):
            P.ts(u, kf, cols[0:64, 8:9], 1.0/TWO_PI, ALU.mult, ALU.mult)
            P.ts(u, u, off, None, ALU.add)
            P.copy(ki, u)
            ang = dst[:, sl]
            P.copy(ang, ki)
            P.tt(u, u, ang, ALU.subtract)
            P.ts(ang, u, 0.0, None, ALU.is_lt)
            P.tt(u, u, ang, ALU.add)
            P.act(ang, u, AF.Sin, bias=-float(np.pi), scale=TWO_PI)
        P.ts(sinpm[:, sl], sinpm[:, sl], cols[0:64, 9:10], None, ALU.mult)

    cq0 = P.sbuf("cq0", [128, 4, 512], F32); cq = [cq0, cq0]
    cn = [P.sbuf(f"cn{i}", [128, 4, 512], BF16) for i in range(2)]
    sqt = [P.sbuf(f"sqt{i}", [128, 512], F32) for i in range(1)]
    rst = P.sbuf("rst", [128, 512], F32)
    kr = [P.sbuf(f"kr{i}", [64, 512], F32) for i in range(2)]
    krs = [P.sbuf(f"krs{i}", [64, 512], F32) for i in range(2)]
    r1 = P.sbuf("r1", [64, 512], F32); r2 = P.sbuf("r2", [64, 512], F32)
    tmpn = P.sbuf("tmpn", [128, 512], F32)

    def load_norm(src, t, ncol0, i):
        P.dma(cq[i], src[:, tl(t)].rearrange("(k p) t -> p k t", p=128))
        rms_rstd(P, C, [cq[i][:, k, :] for k in range(4)], 512, rst, sqt, T=512)
        for k in range(4):
            P.stt(tmpn, cq[i][:, k, :], cols[:, ncol0+k:ncol0+k+1], rst, ALU.mult, ALU.mult)
            P.copy(cn[i][:, k, :], tmpn, eng="act")

    def rope(dst, a, b, t):
        P.tt(r1, a, cos2[:, tl(t)], ALU.mult)
        P.tt(r2, b, sinpm[:, tl(t)], ALU.mult)
        P.tt(dst, r1, r2, ALU.add)

    for t in range(NT):
        load_norm(cqT, t, 0, 0)
        ps = bank(C)
        for k in range(4):
            P.matmul(ps, wq_b[:, k, 0:128], cn[0][:, k, :], start=(k == 0), stop=(k == 3))
        P.copy(qa_t[t], ps, eng="act")
        ps1 = bank(C)
        for k in range(4):
            P.matmul(ps1[0:64, :], wq_b[:, k, 128:192], cn[0][:, k, :], start=(k == 0), stop=(k == 3))
        ps2 = bank(C)
        for k in range(4):
            P.matmul(ps2[0:64, :], wq_b[:, k, 192:256], cn[0][:, k, :], start=(k == 0), stop=(k == 3))
        rope(qb_t[t], ps1[0:64, :], ps2[0:64, :], t)
        load_norm(ckvT, t, 4, 1)
        ps = bank(C)
        for k in range(4):
            P.matmul(ps, wkv_b[:, k, 0:128], cn[1][:, k, :], start=(k == 0), stop=(k == 3))
        P.copy(ka_t[t], ps, eng="act")
        ps = bank(C)
        for blk in range(4):
            for k in range(4):
                P.matmul(ps[:, blk*128:(blk+1)*128], cn[1][:, k, blk*128:(blk+1)*128], wkv_b[:, k, 128:256],
                         start=(k == 0), stop=(k == 3))
        P.copy(vt_t[t], ps.rearrange("p (b d) -> p b d", b=4), eng="act")
        i = t % 2
        P.dma(kr[i], krT[:, tl(t)]); P.dma(krs[i], krsT[:, tl(t)])
        rope(kb_t[t], kr[i], krs[i], t)

    sb = [C.banks[0], C.banks[1], C.banks[2], C.banks[3], C.banks[4]]
    oacc = [C.banks[5], C.banks[6]]; sbank = C.banks[7]
    pb = [P.sbuf(f"pexp{i}", [128, 512], BF16) for i in range(5)]
    rs = rst
    ost = [tmpn, sqt[0]]
    pairs = [(j, b) for j in range(NT) for b in range(4*j + 4)]
    LA = 3
    def emit_qk(n):
        j, b = pairs[n]
        ps = sb[n % 5]
        kt, ko = b // 4, (b % 4) * 128
        P.matmul(ps, ka_t[kt][:, ko:ko+128], qa_t[j], start=True, stop=False)
        P.matmul(ps, kb_t[kt][:, ko:ko+128], qb_t[j], start=False, stop=True)
    def emit_rest(n):
        j, b = pairs[n]
        ps = sb[n % 5]; pe = pb[n % 5]
        kt = b // 4
        nkb = 4*j + 4
        oa = oacc[j % 2]
        P.act(pe, ps, AF.Exp, scale=SCALE)
        if b >= 4*j:
            P.tt(pe, pe, mk[:, b - 4*j, :], ALU.mult, eng="pool")
        P.matmul(oa, vt_t[kt][:, b % 4, :], pe, start=(b == 0), stop=(b == nkb-1))
        P.matmul(sbank, ones_b, pe, start=(b == 0), stop=(b == nkb-1))
        if b == nkb - 1:
            P.recip(rs, sbank)
            o = ost[j % 2]
            P.tt(o, oa, rs, ALU.mult)
            fin.append(P.dma(oT[:, tl(j)], o))
    for n in range(len(pairs) + LA):
        if n < len(pairs):
            emit_qk(n)
        if n - LA >= 0:
            emit_rest(n - LA)
    return fin

def build_mla():
    nc = new_nc()
    P = Prog(nc)
    C = setup_common(P, nc)
    fin = mla_phase(P, C, nc)
    P.emit(final_wait_ops=fin)
    return nc


S = 8192; NCH = 128; CH = 64; TILE = 512; CPT = 8

def gla_phase(P, C, nc, pfx=""):
    dt = lambda name, shape, kind="ExternalInput", d=F32: nc.dram_tensor(pfx + name, list(shape), d, kind=kind).ap()
    qT = dt("gqT", [256, S]); kT = dt("gkT", [256, S])
    ktok = dt("gktok", [S, 256]); vtok = dt("gvtok", [S, 256])
    glow = dt("glowT", [16, S]); waug = dt("gwaug", [17, 256])
    cst = dt("gcst", [64, 192])
    o_d = dt("go", [S, 256], kind="ExternalOutput")
    fin = []
    cs = P.sbuf("gcst_sb", [64, 192], F32); P.dma(cs, cst)
    triS = cs[:, 0:64]; triR = cs[:, 64:128]; maskT = cs[:, 128:192]
    wa = P.sbuf("gwa", [17, 256], F32); P.dma(wa, waug)
    st = P.sbuf("gstate", [128, 2, 256], F32); P.memset(st, 0.0)
    stb = P.sbuf("gstate_b", [128, 2, 256], BF16); P.memset(stb, 0.0)
    NB = 3
    q_in = [P.sbuf(f"gq_in{i}", [128, 2, TILE], F32) for i in range(NB)]
    k_in = [P.sbuf(f"gk_in{i}", [128, 2, TILE], F32) for i in range(NB)]
    kt_in = [P.sbuf(f"gkt_in{i}", [64, CPT, 256], F32) for i in range(NB)]
    v_in = [P.sbuf(f"gv_in{i}", [64, CPT, 256], BF16) for i in range(NB)]
    gl_in = [P.sbuf(f"ggl_in{i}", [17, TILE], F32) for i in range(NB)]
    for g in gl_in:
        P.memset(g, 1.0)
    o_st = [P.sbuf(f"go_st{i}", [64, CPT, 256], F32) for i in range(NB)]
    R = 4
    e1 = [P.sbuf(f"ge1{i}", [64, 256], F32) for i in range(R)]
    lnv = [P.sbuf(f"glnv{i}", [64, 256], F32) for i in range(R)]
    E = [P.sbuf(f"gE{i}", [128, 128], F32) for i in range(R+1)]
    Ei = [P.sbuf(f"gEi{i}", [128, 128], F32) for i in range(R)]
    Er = [P.sbuf(f"gEr{i}", [64, 256], F32) for i in range(R)]
    qt = [P.sbuf(f"gqt{i}", [128, 2, 64], BF16) for i in range(R)]
    kt = [P.sbuf(f"gkt{i}", [128, 2, 64], BF16) for i in range(R)]
    kd = [P.sbuf(f"gkd{i}", [64, 256], BF16) for i in range(R)]
    atm = [P.sbuf(f"gatm{i}", [64, 64], BF16) for i in range(R)]
    NTL = S // TILE
    def load(t):
        i = t % NB
        sl = slice(t*TILE, (t+1)*TILE)
        P.dma(q_in[i], qT[:, sl].rearrange("(k p) t -> p k t", p=128))
        P.dma(k_in[i], kT[:, sl].rearrange("(k p) t -> p k t", p=128))
        P.dma(kt_in[i], ktok[sl, :].rearrange("(c p) d -> p c d", p=64))
        P.dma(v_in[i], vtok[sl, :].rearrange("(c p) d -> p c d", p=64), q="pool")
        P.dma(gl_in[i][0:16, :], glow[:, sl])
    def prep(c):
        t, cc = c // CPT, c % CPT
        if cc == 0:
            load(t)
        i = t % NB
        r = c % R
        cl = slice(cc*64, (cc+1)*64)
        ps = bank(C)
        P.matmul(ps[0:64, 0:256], gl_in[i][:, cl], wa)
        P.act(e1[r], ps[0:64, 0:256], AF.Exp, scale=-1.0)
        P.act(lnv[r], e1[r], AF.Ln, bias=1.0)
        ps2 = bank(C)
        for dc in range(2):
            P.matmul(ps2[:, dc*64:(dc+1)*64], lnv[r][:, dc*128:(dc+1)*128], triS)
        ps3 = bank(C)
        P.matmul(ps3[0:64, 0:256], triR, lnv[r])
        Ec = E[c % (R+1)]
        P.act(Ec, ps2[:, 0:128], AF.Exp)
        P.act(Ei[r], ps2[:, 0:128], AF.Exp, scale=-1.0)
        P.act(Er[r], ps3[0:64, 0:256], AF.Exp)
        P.stt(qt[r], q_in[i][:, :, cl], 1.0/16.0, Ec.rearrange("p (k j) -> p k j", k=2), ALU.mult, ALU.mult)
        P.tt(kt[r], k_in[i][:, :, cl], Ei[r].rearrange("p (k j) -> p k j", k=2), ALU.mult)
        P.tt(kd[r], kt_in[i][:, cc, :], Er[r], ALU.mult)
        ps4 = bank(C)
        for dc in range(2):
            P.matmul(ps4[0:64, 0:64], kt[r][:, dc, :], qt[r][:, dc, :], start=(dc == 0), stop=(dc == 1))
        P.tt(atm[r], ps4[0:64, 0:64], maskT, ALU.mult)
    def scan(c):
        t, cc = c // CPT, c % CPT
        i = t % NB
        r = c % R
        Ec = E[c % (R+1)]
        ps5 = bank(C)
        for dc in range(2):
            P.matmul(ps5[0:64, 0:256], qt[r][:, dc, :], stb[:, dc, :], start=(dc == 0), stop=False)
        P.matmul(ps5[0:64, 0:256], atm[r], v_in[i][:, cc, :], start=False, stop=True)
        P.copy(o_st[i][:, cc, :], ps5[0:64, 0:256], eng="act")
        ps6 = bank(C)
        for dc in range(2):
            P.matmul(ps6[:, dc*256:(dc+1)*256], kd[r][:, dc*128:(dc+1)*128], v_in[i][:, cc, :])
        for dc in range(2):
            P.stt(st[:, dc, :], st[:, dc, :], Ec[:, dc*64+63:dc*64+64], ps6[:, dc*256:(dc+1)*256], ALU.mult, ALU.add)
        P.copy(stb, st, eng="act")
        if cc == CPT - 1:
            sl = slice(t*TILE, (t+1)*TILE)
            fin.append(P.dma(o_d[sl, :].rearrange("(c p) e -> p c e", p=64), o_st[i]))
    NC_ = NTL * CPT
    LA = 2
    for c in range(NC_ + LA):
        if c < NC_:
            prep(c)
        if c - LA >= 0:
            scan(c - LA)
    return fin

def build_gla():
    nc = new_nc()
    P = Prog(nc)
    C = setup_common(P, nc)
    fin = gla_phase(P, C, nc)
    P.emit(final_wait_ops=fin)
    return nc


DEBUG = False
S = 8192; TILE = 512; CPT = 8; EPS = 1e-6

def gdn_phase(P, C, nc, pfx=""):
    dt = lambda name, shape, kind="ExternalInput", d=F32: nc.dram_tensor(pfx + name, list(shape), d, kind=kind).ap()
    xT = dt("dxT", [384, S]); cw_d = dt("dcw", [128, 14]); bl_d = dt("dbl", [1, S]); al_d = dt("dal", [1, S])
    c64_d = dt("dc64", [64, 256]); id_d = dt("did", [128, 128]); rm_d = dt("drm", [128, 512])
    o_d = dt("dobT", [128, S], kind="ExternalOutput")
    fin = []
    C.nrot = 7
    cw = P.sbuf("dcw_sb", [128, 14], F32); P.dma(cw, cw_d)
    c64 = P.sbuf("dc64_sb", [64, 256], F32); P.dma(c64, c64_d)
    I64 = c64[:, 0:64]; mLs = c64[:, 64:128]; mUi = c64[:, 128:192]; mUs = c64[:, 192:256]
    ident = P.sbuf("did_sb", [128, 128], F32); P.dma(ident, id_d)
    rmask = P.sbuf("drm_sb", [128, 512], F32); P.dma(rmask, rm_d)
    negA = P.sbuf("dnegA", [128, 1], F32)
    P.act(negA, cw[:, 12:13], AF.Exp)
    Ss = [P.sbuf(f"dS{i}", [128, 128], F32) for i in range(3)]
    P.memset(Ss[0], 0.0)
    T3 = lambda v: v.rearrange("p (c j) -> p c j", j=64)
    def sb(name, shape, d=F32, n=2):
        return [P.sbuf(f"{name}{i}", shape, d) for i in range(n)]
    xin = [sb("dxq", [128, TILE+3]), sb("dxk", [128, TILE+3]), sb("dxv", [128, TILE+3])]
    y = sb("dy", [128, TILE], n=3); sq = sb("dsq", [128, TILE], n=2); rstd = sb("drstd", [128, TILE], n=1)[0]
    qn = sb("dqn", [128, TILE]); kn = sb("dkn", [128, TILE]); vv = sb("dvv", [128, TILE])
    bl = sb("dblb", [128, TILE]); al = sb("dalb", [128, TILE])
    beta = sb("dbeta", [128, TILE]); g = sb("dg", [128, TILE], n=1)[0]; gc = sb("dgc", [128, TILE]); egc = sb("degc", [128, TILE])
    ekd = sb("dekd", [128, TILE]); egl = sb("degl", [128, CPT])
    knb = sb("dknb", [128, TILE], BF16); kbb = sb("dkbb", [128, TILE], BF16); qnb = sb("dqnb", [128, TILE], BF16)
    kb = sb("dkb", [128, TILE], n=1)[0]; kbg = sb("dkbg", [128, TILE], BF16, n=1)[0]; kdc = sb("dkdc", [128, TILE], BF16, n=1)[0]
    vb = sb("dvb", [128, TILE], BF16, n=1)[0]
    kbg_t = sb("dkbgt", [64, CPT, 128], BF16); vb_t = sb("dvbt", [64, CPT, 128], BF16); kdc_t = sb("dkdct", [64, CPT, 128], BF16)
    identb = P.sbuf("didb", [128, 128], BF16); P.copy(identb, ident)
    Sbs = [P.sbuf(f"dSb{i}", [128, 128], BF16) for i in range(3)]
    P.memset(Sbs[0], 0.0)
    gtok = sb("dgtok", [64, CPT], n=1)[0]; tmp64 = sb("dtmp64", [64, TILE], n=2)
    D1 = sb("dD1", [64, TILE], n=1)[0]
    dec = sb("ddec", [64, TILE], n=1)[0]; decT = sb("ddecT", [64, TILE], n=1)[0]
    Ym = sb("dY", [64, TILE]); Pn = sb("dPn", [64, TILE], BF16); PTn = sb("dPTn", [64, TILE], BF16); Yb = sb("dYb", [64, TILE], BF16)
    intraT = sb("dintraT", [64, TILE], BF16)
    u_t = sb("du", [64, CPT, 128], BF16); w_t = sb("dwt", [64, CPT, 128], BF16, n=1)[0]
    MT = sb("dMT", [128, CPT, 128]); Bc = sb("dBc", [128, CPT, 128]); qpT = sb("dqpT", [128, TILE], BF16)
    qdf = sb("dqdf", [128, TILE], n=1)[0]
    ost = sb("dost", [128, TILE])

    NT = S // TILE
    def prepass(t):
        i = t % 2
        sl = slice(t*TILE, (t+1)*TILE)
        for kind in range(3):
            xi = xin[kind][i]
            if t == 0:
                P.memset(xi[:, 0:3], 0.0, eng="dve")
                P.dma(xi[:, 3:TILE+3], xT[kind*128:(kind+1)*128, 0:TILE])
            else:
                P.dma(xi, xT[kind*128:(kind+1)*128, t*TILE-3:(t+1)*TILE])
            yy = y[kind]
            P.ts(yy, xi[:, 0:TILE], cw[:, kind*4:kind*4+1], None, ALU.mult)
            for j in range(1, 4):
                P.stt(yy, xi[:, j:j+TILE], cw[:, kind*4+j:kind*4+j+1], yy, ALU.mult, ALU.add)
            if kind == 2:
                P.act(vv[i], yy, AF.Silu)
            else:
                P.act(yy, yy, AF.Silu)
                P.act(C.sqb[kind], yy, AF.Square)
                ps = bank(C)
                P.matmul(ps, C.ones_b, C.sqb[kind])
                P.act(rstd, ps, AF.Sqrt, bias=EPS)
                P.recip(rstd, rstd)
                if kind == 0:
                    P.stt(qn[i], yy, float(1.0/np.sqrt(128.0)), rstd, ALU.mult, ALU.mult)
                else:
                    P.tt(kn[i], yy, rstd, ALU.mult)
            yield
        P.dma(bl[i], bl_d[:, sl].partition_broadcast(128))
        P.dma(al[i], al_d[:, sl].partition_broadcast(128))
        P.act(beta[i], bl[i], AF.Sigmoid)
        P.act(g, al[i], AF.Exp, bias=cw[:, 13:14])
        P.act(g, g, AF.Ln, bias=1.0)
        P.ts(g, g, negA, -1.0, ALU.mult, ALU.mult)
        P.add("dve", (lambda o, m, gg: (lambda e: e.tensor_tensor_scan(o.ap, m.ap, gg.ap, 0.0, ALU.mult, ALU.add)))(gc[i], rmask, g),
              reads=[rmask, g], writes=[gc[i]])
        P.act(egc[i], gc[i], AF.Exp)
        gl3 = T3(gc[i])[:, :, 63:64]
        P.tt(T3(ekd[i]), gl3.bcast([128, CPT, 64]), T3(gc[i]), ALU.subtract)
        P.act(ekd[i], ekd[i], AF.Exp)
        P.act(egl[i], T3(gc[i])[:, :, 63], AF.Exp)
        yield
        P.copy(knb[i], kn[i], eng="act")
        P.copy(qnb[i], qn[i], eng="act")
        P.tt(kb, kn[i], beta[i], ALU.mult)
        P.copy(kbb[i], kb, eng="act")
        P.tt(kbg, kb, egc[i], ALU.mult)
        P.tt(kdc, kn[i], ekd[i], ALU.mult)
        P.tt(qdf, qn[i], egc[i], ALU.mult)
        P.tt(vb, vv[i], beta[i], ALU.mult)
        yield
        for src, dst in ((kbg, kbg_t[i]), (vb, vb_t[i]), (kdc, kdc_t[i])):
            for half in range(2):
                ps = bank(C).bitcast(BF16)
                for cq in range(4):
                    cc = half*4 + cq
                    P.transpose(ps[0:64, cq*128:(cq+1)*128], src[:, cc*64:(cc+1)*64], identb)
                P.copy(dst[:, half*4:half*4+4, :], ps[0:64, 0:512].rearrange("p (c d) -> p c d", c=4), eng="act" if half else "dve")
                yield
        P.tt(T3(tmp64[0]), T3(gc[i][0:64, :]), I64.rearrange("p (o j) -> p o j", o=1).bcast([64, CPT, 64]), ALU.mult)
        P.reduce(gtok, T3(tmp64[0]), ALU.add)
        P.tt(T3(D1), T3(gc[i][0:64, :]), gtok.rearrange("p (c o) -> p c o", o=1).bcast([64, CPT, 64]), ALU.subtract)
        P.ts(tmp64[0], D1, 0.0, None, ALU.max)
        P.act(dec, tmp64[0], AF.Exp, scale=-1.0)
        P.ts(tmp64[1], D1, 0.0, None, ALU.min)
        P.act(decT, tmp64[1], AF.Exp)
        yield
        psL = bank(C); psLT = bank(C); psQ = bank(C)
        for cc in range(CPT):
            cl = slice(cc*64, (cc+1)*64)
            P.matmul(psL[0:64, cl], kbb[i][:, cl], knb[i][:, cl])
            P.matmul(psLT[0:64, cl], knb[i][:, cl], kbb[i][:, cl])
            P.matmul(psQ[0:64, cl], knb[i][:, cl], qnb[i][:, cl])
        bc64 = lambda m: m.rearrange("p (o j) -> p o j", o=1).bcast([64, CPT, 64])
        P.tt(T3(tmp64[0]), T3(dec), bc64(mLs), ALU.mult)
        P.stt(Pn[0], psL[0:64, :], -1.0, tmp64[0], ALU.mult, ALU.mult)
        P.tt(T3(tmp64[1]), T3(decT), bc64(mUs), ALU.mult)
        P.stt(PTn[0], psLT[0:64, :], -1.0, tmp64[1], ALU.mult, ALU.mult)
        P.tt(T3(tmp64[1]), T3(decT), bc64(mUi), ALU.mult)
        P.tt(intraT[i], psQ[0:64, :], tmp64[1], ALU.mult)
        Y = Ym[0]
        P.tt(T3(Y), T3(PTn[0]), bc64(I64), ALU.add)
        P.copy(Yb[0], Y, eng="act")
        ybi = 0
        yield
        cur = 0
        for n in range(1, 6):
            nxt = 1 - cur
            psP = bank(C); psPT = bank(C)
            for cc in range(CPT):
                cl = slice(cc*64, (cc+1)*64)
                P.matmul(psP[0:64, cl], PTn[cur][:, cl], Pn[cur][:, cl])
                if n < 5:
                    P.matmul(psPT[0:64, cl], Pn[cur][:, cl], PTn[cur][:, cl])
            P.copy(Pn[nxt], psP[0:64, :], eng="act")
            if n < 5:
                P.copy(PTn[nxt], psPT[0:64, :], eng="dve")
            yield
            psY = bank(C)
            for cc in range(CPT):
                cl = slice(cc*64, (cc+1)*64)
                P.matmul(psY[0:64, cl], Pn[nxt][:, cl], Yb[ybi][:, cl])
            Y2 = Ym[1] if Y is Ym[0] else Ym[0]
            P.tt(Y2, Y, psY[0:64, :], ALU.add)
            Y = Y2
            ybi = 1 - ybi
            P.copy(Yb[ybi], Y, eng="act")
            yield
            cur = nxt
        for half in range(2):
            psu = bank(C); psw = bank(C)
            for cq in range(4):
                cc = half*4 + cq
                cl = slice(cc*64, (cc+1)*64)
                P.matmul(psu[0:64, cq*128:(cq+1)*128], Yb[ybi][:, cl], vb_t[i][:, cc, :])
                P.matmul(psw[0:64, cq*128:(cq+1)*128], Yb[ybi][:, cl], kbg_t[i][:, cc, :])
            P.copy(u_t[i][:, half*4:half*4+4, :], psu[0:64, :].rearrange("p (c d) -> p c d", c=4), eng="act")
            P.copy(w_t[:, half*4:half*4+4, :], psw[0:64, :].rearrange("p (c d) -> p c d", c=4), eng="dve")
            yield
        psq = bank(C)
        for cc in range(CPT):
            cl = slice(cc*64, (cc+1)*64)
            P.matmul(psq[:, cl], w_t[:, cc, :], intraT[i][:, cl])
        P.tt(qpT[i], qdf, psq, ALU.subtract)
        yield
        for half in range(2):
            psm = bank(C); psb = bank(C)
            for cq in range(4):
                cc = half*4 + cq
                P.matmul(psm[:, cq*128:(cq+1)*128], w_t[:, cc, :], kdc_t[i][:, cc, :])
                P.matmul(psb[:, cq*128:(cq+1)*128], kdc_t[i][:, cc, :], u_t[i][:, cc, :])
            for cq in range(4):
                cc = half*4 + cq
                P.stt(MT[i][:, cc, :], ident, egl[i][:, cc:cc+1], psm[:, cq*128:(cq+1)*128], ALU.mult, ALU.subtract)
            P.copy(Bc[i][:, half*4:half*4+4, :], psb.rearrange("p (c d) -> p c d", c=4), eng="act")
            yield

    def scan_tile(t, gen):
        i = t % 2
        sl = slice(t*TILE, (t+1)*TILE)
        C.nrot = 7
        pso = C.banks[7]
        for cc in range(CPT):
            c = t*CPT + cc
            cl = slice(cc*64, (cc+1)*64)
            S_cur = Ss[c % 3]; S_nxt = Ss[(c + 1) % 3]
            p1 = bank(C)
            P.matmul(p1[:, 0:128], MT[i][:, cc, :], S_cur)
            P.tt(S_nxt, p1[:, 0:128], Bc[i][:, cc, :], ALU.add)
            P.copy(Sbs[(c + 1) % 3], S_nxt, eng="act")
            P.matmul(pso[:, cl], Sbs[c % 3], qpT[i][:, cl], start=True, stop=False)
            P.matmul(pso[:, cl], u_t[i][:, cc, :], intraT[i][:, cl], start=False, stop=True)
            if gen is not None:
                for _ in range(5):
                    next(gen, None)
        P.copy(ost[i], pso, eng="act")
        fin.append(P.dma(o_d[:, sl], ost[i]))

    for _ in prepass(0):
        pass
    for t in range(NT):
        gen = prepass(t + 1) if t + 1 < NT else None
        scan_tile(t, gen)
        if gen is not None:
            for _ in gen:
                pass
    return fin

def build_gdn():
    nc = new_nc()
    P = Prog(nc)
    C = setup_common(P, nc)
    fin = gdn_phase(P, C, nc)
    P.emit(final_wait_ops=fin)
    return nc

def colv(v):
    v = np.asarray(v, np.float32)
    return np.ascontiguousarray(v.reshape(-1, 128).T)
def make_cols(gate_mix=None, ffn=None, pre=None, mixnorm=None):
    cols = np.zeros((128, 136), np.float32)
    if gate_mix is not None: cols[:, 0:16] = colv(gate_mix)
    if ffn is not None:
        ng, sh, sc, gt = ffn
        cols[:, 16:32] = colv(ng); cols[:, 32:48] = colv(sh); cols[:, 48:64] = colv(sc); cols[:, 64:80] = colv(gt)
    if pre is not None:
        ng, sh, sc = pre
        cols[:, 80:96] = colv(ng)
        if sh is not None:
            cols[:, 96:112] = colv(sh); cols[:, 112:128] = colv(sc)
    if mixnorm is not None:
        m = colv(mixnorm)
        cols[:, 128:128+m.shape[1]] = m
    return cols
def w_in_even(w):
    kr = w[:, 1024:1088]
    return np.ascontiguousarray(np.concatenate([w, kr[:, 32:], kr[:, :32]], axis=1))
def mla_consts():
    half = 32
    inv = (10000.0 ** (-np.arange(half, dtype=np.float32) / half)).astype(np.float32)
    invf = np.zeros(128, np.float32); invf[:64] = np.concatenate([inv, inv])
    sign = np.zeros(128, np.float32); sign[:32] = -1; sign[32:64] = 1
    k = np.arange(128)[:, None, None]; r = np.arange(4)[None, :, None]; q = np.arange(512)[None, None, :]
    masks = (q >= k + 128*r).astype(np.float32)
    return invf, sign, np.ascontiguousarray(masks)
def mla_inputs(proj, positions, q_norm, kv_norm, w_uq, w_ukv, h):
    invf, sign, masks = mla_consts()
    cols = np.zeros((128, 10), np.float32)
    cols[:, 0:4] = colv(q_norm); cols[:, 4:8] = colv(kv_norm); cols[:, 8] = invf; cols[:, 9] = sign
    wqh = w_uq[:, h*192:(h+1)*192]
    wq = np.concatenate([wqh[:, :128], wqh[:, 128:192], wqh[:, 160:192], wqh[:, 128:160]], axis=1)
    wkv = w_ukv[:, h*256:(h+1)*256]
    return {"cqT": np.ascontiguousarray(proj[:, 0:512].T), "ckvT": np.ascontiguousarray(proj[:, 512:1024].T),
            "krT": np.ascontiguousarray(proj[:, 1024:1088].T), "krsT": np.ascontiguousarray(proj[:, 5200:5264].T),
            "mla_cols": cols, "wq": np.ascontiguousarray(wq), "wkv": np.ascontiguousarray(wkv),
            "pos": np.ascontiguousarray(positions.reshape(1, -1).astype(np.int32)), "masks": masks}
def gla_consts():
    tp = np.arange(64)[:, None]; t = np.arange(64)[None, :]
    triS = np.where(tp <= t, -1.0/16.0, 0.0); triR = np.where(tp > t, -1.0/16.0, 0.0)
    maskT = np.where(t >= tp, 1.0, 0.0)
    return np.ascontiguousarray(np.concatenate([triS, triR, maskT], axis=1).astype(np.float32))
def gla_inputs(proj, w_gk2, b_gk2, core):
    h, e = core // 2, core % 2
    q = proj[:, h*256:(h+1)*256]; k = proj[:, 1024+h*256:1024+(h+1)*256]
    v = proj[:, 2048+h*512+e*256:2048+h*512+(e+1)*256]
    gl = proj[:, 6144:6160]
    waug = np.concatenate([w_gk2[:, h*256:(h+1)*256], b_gk2[None, h*256:(h+1)*256]], axis=0)
    return {"gqT": np.ascontiguousarray(q.T), "gkT": np.ascontiguousarray(k.T), "gktok": np.ascontiguousarray(k),
            "gvtok": np.ascontiguousarray(v), "glowT": np.ascontiguousarray(gl.T), "gwaug": np.ascontiguousarray(waug),
            "gcst": gla_consts()}
def gdn_consts():
    p = np.arange(64)[:, None]; f = np.arange(64)[None, :]
    I64 = (p == f); mLs = (p > f); mUi = (f >= p); mUs = (f > p)
    c64 = np.concatenate([I64, mLs, mUi, mUs], axis=1).astype(np.float32)
    rm = np.ones((128, 512), np.float32); rm[:, ::64] = 0.0
    return np.ascontiguousarray(c64), np.eye(128, dtype=np.float32), rm
def gdn_inputs(proj, conv_w, a_log, dt_bias, h):
    q0 = 1088
    x = np.concatenate([proj[:, q0+h*128:q0+(h+1)*128], proj[:, q0+1024+h*128:q0+1024+(h+1)*128],
                        proj[:, q0+2048+h*128:q0+2048+(h+1)*128]], axis=1)
    cw = np.zeros((128, 14), np.float32)
    for kind in range(3):
        cw[:, kind*4:(kind+1)*4] = conv_w[:, kind*1024+h*128:kind*1024+(h+1)*128].T
    cw[:, 12] = a_log[h]; cw[:, 13] = dt_bias[h]
    c64, ident, rm = gdn_consts()
    return {"dxT": np.ascontiguousarray(x.T), "dcw": cw, "dbl": np.ascontiguousarray(proj[:, 5184+h][None]),
            "dal": np.ascontiguousarray(proj[:, 5192+h][None]), "dc64": c64, "did": ident, "drm": rm}

_DBG = {}
_NC_CACHE = {}

def _get(key, fn):
    if key not in _NC_CACHE:
        _NC_CACHE[key] = fn()
    return _NC_CACHE[key]

def _run(nc, maps):
    return run_bass_kernel_spmd(nc, maps, core_ids=list(range(8))).results

def kernel(x, c, positions, norm_g, ada_w, ada_b, ab_w_in, mla_q_norm, mla_w_uq, mla_kv_norm, mla_w_ukv,
           gdn_conv_w, gdn_a_log, gdn_dt_bias, gdn_norm, ab_w_out, gla_w_in, gla_w_gk2, gla_b_gk2, gla_norm,
           gla_w_out, ffn_w1, ffn_w3, ffn_w2, final_norm):
    f32 = lambda a: np.asarray(a, np.float32)
    x = f32(x)[0]; c = f32(c)[0]; positions = np.asarray(positions)
    norm_g = f32(norm_g); NL = 4
    c_col = np.ascontiguousarray(c.reshape(16, 128).T)
    aw = f32(ada_w).reshape(8, 2048, 6144); ab = f32(ada_b).reshape(8, 6144)
    maps = []
    for j in range(8):
        maps.append({"c_col": c_col, "w": np.ascontiguousarray(aw[:, :, j*768:(j+1)*768]),
                     "b": np.ascontiguousarray(ab[:, j*768:(j+1)*768].reshape(8, 6, 128).transpose(2, 0, 1).reshape(128, 48))})
    res = _run(_get("ada", build_ada), maps)
    mods = np.zeros((8, 6144), np.float32)
    for j in range(8):
        mods[:, j*768:(j+1)*768] = res[j]["o"].reshape(128, 8, 6).transpose(1, 2, 0).reshape(8, 768)
    del aw, maps
    SH, SC, GT = slice(0, 2048), slice(2048, 4096), slice(4096, 6144)
    def w_in_for(l):
        return w_in_even(f32(ab_w_in[l // 2])) if l % 2 == 0 else np.ascontiguousarray(f32(gla_w_in[l // 2]))
    xT = [np.ascontiguousarray(x[j*1024:(j+1)*1024].T) for j in range(8)]
    cols = make_cols(pre=(norm_g[0, 0], mods[0][SH], mods[0][SC]))
    w_in = w_in_for(0)
    res = _run(_get(("t", None, 5264, False), lambda: build_t(None, True, 5264, False)),
               [{"xT": xT[j], "cols": cols, "w_in": w_in} for j in range(8)])
    proj = np.concatenate([res[j]["projT"].T for j in range(8)], axis=0)
    for l in range(NL):
        i = l // 2
        if l % 2 == 0:
            res = _run(_get("mla", build_mla), [mla_inputs(proj, positions, f32(mla_q_norm[i]), f32(mla_kv_norm[i]),
                                                             f32(mla_w_uq[i]), f32(mla_w_ukv[i]), h) for h in range(8)])
            o_a = np.concatenate([res[h]["oaT"].T for h in range(8)], axis=1)
            res = _run(_get("gdn", build_gdn), [gdn_inputs(proj, f32(gdn_conv_w[i]), f32(gdn_a_log[i]), f32(gdn_dt_bias[i]), h)
                                                 for h in range(8)])
            o_b = np.concatenate([res[h]["dobT"].T for h in range(8)], axis=1)
            o = np.concatenate([o_a, o_b], axis=1)
            gsrc = proj[:, 4160:5184]
            mixnorm = f32(gdn_norm[i]); w_out = f32(ab_w_out[i]); kind = "ab"
        else:
            res = _run(_get("gla", build_gla), [gla_inputs(proj, f32(gla_w_gk2[i]), f32(gla_b_gk2[i]), cc) for cc in range(8)])
            o = np.concatenate([res[cc]["go"] for cc in range(8)], axis=1)
            gsrc = proj[:, 4096:6144]
            mixnorm = f32(gla_norm[i]); w_out = f32(gla_w_out[i]); kind = "gla"
        _DBG[f"o{l}"] = o
        last = (l == NL - 1)
        if last:
            pre = (f32(final_norm), None, None); f_next = 0
        else:
            pre = (norm_g[l+1, 0], mods[2*(l+1)][SH], mods[2*(l+1)][SC]); f_next = 5264 if (l+1) % 2 == 0 else 6160
        cols = make_cols(gate_mix=mods[2*l][GT], ffn=(norm_g[l, 1], mods[2*l+1][SH], mods[2*l+1][SC], mods[2*l+1][GT]),
                         pre=pre, mixnorm=mixnorm)
        w1 = np.ascontiguousarray(f32(ffn_w1[l])); w3 = np.ascontiguousarray(f32(ffn_w3[l])); w2 = np.ascontiguousarray(f32(ffn_w2[l]))
        w_out = np.ascontiguousarray(w_out)
        maps = []
        w_in = None if last else w_in_for(l + 1)
        for j in range(8):
            tk = slice(j*1024, (j+1)*1024)
            m = {"xT": xT[j], "cols": cols, "oT": np.ascontiguousarray(o[tk].T), "gT": np.ascontiguousarray(gsrc[tk].T),
                 "w_out": w_out, "w1": w1, "w3": w3, "w2": w2}
            if not last:
                m["w_in"] = w_in
            maps.append(m)
        key = ("t", kind, f_next, last)
        res = _run(_get(key, (lambda kind=kind, f_next=f_next, last=last: build_t(kind, not last, f_next, last))), maps)
        xT = [res[j]["xT_out"] for j in range(8)]
        if not last:
            proj = np.concatenate([res[j]["projT"].T for j in range(8)], axis=0)
        _DBG[f"x{l}"] = xT
    out = np.concatenate([xT[j].T for j in range(8)], axis=0)
    return np.ascontiguousarray(out[None]).astype(np.float32)
```

```python
import numpy as np
import concourse.bass as bass
import concourse.mybir as mybir
from concourse.bass_utils import run_bass_kernel_spmd
from contextlib import ExitStack


F32 = mybir.dt.float32
BF16 = mybir.dt.bfloat16
I32 = mybir.dt.int32
AF = mybir.ActivationFunctionType
ALU = mybir.AluOpType
AX = mybir.AxisListType


class Buf:
    __slots__ = ("name", "last_w", "readers")

    def __init__(self, name=""):
        self.name = name
        self.last_w = None
        self.readers = []


class V:
    __slots__ = ("ap", "buf")

    def __init__(self, ap, buf):
        self.ap = ap
        self.buf = buf

    def __getitem__(self, k):
        return V(self.ap[k], self.buf)

    def rearrange(self, s, **kw):
        return V(self.ap.rearrange(s, **kw), self.buf)

    def bitcast(self, dt):
        return V(self.ap.bitcast(dt), self.buf)

    def bcast(self, shape):
        return V(self.ap.broadcast_to(shape), self.buf)

    def sub(self, k, name=""):
        return V(self.ap[k], Buf(name))

    @property
    def shape(self):
        return self.ap.shape


class Op:
    __slots__ = ("eng", "fn", "deps", "signals", "count", "is_dma", "lane", "idx")


ENGS = ("pe", "act", "dve", "pool", "sp")


def _ap(x):
    return x.ap if isinstance(x, V) else x


class Prog:
    N_LANES = 6
    SAME_ENG_WINDOW = 6

    def __init__(self, nc):
        self.nc = nc
        self.ops = {e: [] for e in ENGS}
        self.stack = ExitStack()
        self.nbytes = 0

    def sbuf(self, name, shape, dtype):
        t = self.stack.enter_context(self.nc.sbuf_tensor(name, list(shape), dtype))
        return V(t[:], Buf(name))

    def psum(self, name, shape, dtype=F32):
        t = self.stack.enter_context(self.nc.psum_tensor(name, list(shape), dtype))
        return V(t[:], Buf(name))

    def add(self, eng, fn, reads=(), writes=(), is_dma=False):
        op = Op()
        op.eng = eng
        op.fn = fn
        op.signals = False
        op.count = None
        op.is_dma = is_dma
        op.lane = None
        lst = self.ops[eng]
        op.idx = len(lst)
        deps = []
        for r in reads:
            b = r.buf if isinstance(r, V) else r
            if b is None:
                continue
            if b.last_w is not None:
                deps.append(b.last_w)
        for w in writes:
            b = w.buf if isinstance(w, V) else w
            if b is None:
                continue
            if b.last_w is not None:
                deps.append(b.last_w)
            deps.extend(b.readers)
        fd = []
        seen = set()
        for d in deps:
            if d is op or id(d) in seen:
                continue
            seen.add(id(d))
            if (not d.is_dma) and d.eng == eng and not is_dma and (op.idx - d.idx) > self.SAME_ENG_WINDOW:
                continue
            if (not d.is_dma) and (not is_dma) and d.eng == eng == "pe":
                continue
            if (not d.is_dma) and d.eng == eng and is_dma:
                pass
            fd.append(d)
            d.signals = True
        op.deps = fd
        for r in reads:
            b = r.buf if isinstance(r, V) else r
            if b is None:
                continue
            b.readers = [x for x in b.readers if not (x.eng == eng and not x.is_dma and not is_dma)] + [op]
        for w in writes:
            b = w.buf if isinstance(w, V) else w
            if b is None:
                continue
            b.last_w = op
            b.readers = []
        lst.append(op)
        return op

    F32R = False

    def matmul(self, out, lhsT, rhs, start=True, stop=True, **kw):
        if self.F32R and lhsT.ap.dtype == F32 and rhs.ap.dtype == F32:
            lhsT = lhsT.bitcast(mybir.dt.float32r); rhs = rhs.bitcast(mybir.dt.float32r)
        reads = [lhsT, rhs] + ([] if start else [])
        return self.add("pe", lambda e: e.matmul(out.ap, lhsT.ap, rhs.ap, start=start, stop=stop, **kw),
                        reads=reads, writes=[out])

    def transpose(self, out, in_, ident):
        return self.add("pe", lambda e: e.transpose(out.ap, in_.ap, ident.ap), reads=[in_, ident], writes=[out])

    def act(self, out, in_, func, bias=None, scale=None, accum_out=None, eng="act"):
        reads = [in_]
        kw = {}
        if bias is not None:
            kw["bias"] = _ap(bias)
            if isinstance(bias, V):
                reads.append(bias)
        if scale is not None:
            kw["scale"] = _ap(scale)
            if isinstance(scale, V):
                reads.append(scale)
        writes = [out]
        if accum_out is not None:
            kw["accum_out"] = accum_out.ap
            writes.append(accum_out)
        return self.add(eng, lambda e: e.activation(out.ap, in_.ap, func, **kw), reads=reads, writes=writes)

    def tt(self, out, in0, in1, op, eng="dve"):
        return self.add(eng, lambda e: e.tensor_tensor(out.ap, in0.ap, in1.ap, op), reads=[in0, in1], writes=[out])

    def ts(self, out, in0, s1, s2, op0, op1=None, eng="dve", accum_out=None):
        reads = [in0] + [s for s in (s1, s2) if isinstance(s, V)]
        writes = [out]
        kw = {}
        if op1 is not None:
            kw["op1"] = op1
        if accum_out is not None:
            kw["accum_out"] = accum_out.ap
            writes.append(accum_out)
        return self.add(eng, lambda e: e.tensor_scalar(out.ap, in0.ap, _ap(s1), _ap(s2), op0, **kw),
                        reads=reads, writes=writes)

    def stt(self, out, in0, scalar, in1, op0, op1, eng="dve"):
        reads = [in0, in1] + ([scalar] if isinstance(scalar, V) else [])
        return self.add(eng, lambda e: e.scalar_tensor_tensor(out.ap, in0.ap, _ap(scalar), in1.ap, op0, op1),
                        reads=reads, writes=[out])

    def copy(self, out, in_, eng="dve"):
        if eng == "act":
            return self.add("act", lambda e: e.copy(out.ap, in_.ap), reads=[in_], writes=[out])
        return self.add(eng, lambda e: e.tensor_copy(out.ap, in_.ap), reads=[in_], writes=[out])

    def memset(self, out, val, eng="pool"):
        return self.add(eng, lambda e: e.memset(out.ap, val), writes=[out])

    def recip(self, out, in_):
        return self.add("dve", lambda e: e.reciprocal(out.ap, in_.ap), reads=[in_], writes=[out])

    def reduce(self, out, in_, op, axis=AX.X, eng="dve"):
        return self.add(eng, lambda e: e.tensor_reduce(out.ap, in_.ap, axis, op), reads=[in_], writes=[out])

    def dma(self, out, in_, q="sp", **kw):
        reads = [in_] if isinstance(in_, V) else []
        writes = [out] if isinstance(out, V) else []
        return self.add(q, lambda e: e.dma_start(out=_ap(out), in_=_ap(in_), **kw), reads=reads, writes=writes,
                        is_dma=True)

    def emit(self, final_wait_ops=()):
        nc = self.nc
        st = self.stack
        sem = {e: st.enter_context(nc.semaphore("s_" + e)) for e in ENGS}
        lanes = {e: [st.enter_context(nc.semaphore(f"l_{e}{i}")) for i in range(self.N_LANES)] for e in ENGS}
        for e in ENGS:
            c = 0
            lc = [0] * self.N_LANES
            nd = 0
            for op in self.ops[e]:
                if op.is_dma:
                    op.lane = nd % self.N_LANES
                    nd += 1
                    lc[op.lane] += 16
                    op.count = lc[op.lane]
                    op.signals = True
                elif op.signals:
                    c += 1
                    op.count = c
        for op in final_wait_ops:
            assert op.is_dma
        self.stats = {e: len(self.ops[e]) for e in ENGS}

        def run(e, engobj):
            waited = {}
            for op in self.ops[e]:
                need = {}
                for d in op.deps:
                    s = lanes[d.eng][d.lane] if d.is_dma else sem[d.eng]
                    key = id(s)
                    if need.get(key, (None, 0))[1] < d.count:
                        need[key] = (s, d.count)
                for key, (s, cnt) in need.items():
                    if waited.get(key, 0) >= cnt:
                        continue
                    engobj.wait_ge(s, cnt)
                    waited[key] = cnt
                ins = op.fn(engobj)
                if op.is_dma:
                    ins.then_inc(lanes[e][op.lane], 16)
                elif op.signals:
                    ins.then_inc(sem[e], 1)
            if e == "sp":
                for op in final_wait_ops:
                    engobj.wait_ge(lanes[op.eng][op.lane], op.count)

        with nc.Block() as block:
            @block.tensor
            def _(eng):
                run("pe", eng)

            @block.scalar
            def _(eng):
                run("act", eng)

            @block.vector
            def _(eng):
                run("dve", eng)

            @block.gpsimd
            def _(eng):
                run("pool", eng)

            @block.sync
            def _(eng):
                run("sp", eng)
        st.close()


D = 2048; KC = 16; TOK = 1024; NTT = 2; DFF = 5632; EPS = 1e-6

def new_nc():
    return bass.Bass("TRN2", target_bir_lowering=False)

class Ctx:
    pass

def setup_common(P, nc):
    C = Ctx()
    C.banks = [P.psum(f"pb{i}", [128, 512], F32) for i in range(8)]
    C.bi = 0
    C.ones = P.sbuf("ones_f", [128, 128], F32)
    P.memset(C.ones, 1.0)
    C.ones_b = P.sbuf("ones_bf", [128, 128], BF16)
    P.memset(C.ones_b, 1.0)
    C.sqb = [P.sbuf(f"sqb{i}", [128, 512], BF16) for i in range(2)]
    C.ev = 0
    return C

def bank(C):
    rot = getattr(C, "rot", None)
    if rot is not None:
        b = rot[C.bi % len(rot)]
    else:
        b = C.banks[C.bi % getattr(C, "nrot", 8)]
    C.bi += 1
    return b

def rms_rstd(P, C, src_chunks, nfeat, rstd_out, sqtmp, T=TOK):
    n = len(src_chunks)
    sqtmp = C.sqb
    for tt in range(T // 512):
        ps = bank(C)
        for k, s in enumerate(src_chunks):
            sq = sqtmp[k % len(sqtmp)]
            P.act(sq[:, 0:512], s[:, tt*512:(tt+1)*512], AF.Square)
            P.matmul(ps, C.ones_b, sq[:, 0:512], start=(k == 0), stop=(k == n-1))
        P.act(rstd_out[:, tt*512:(tt+1)*512], ps, AF.Sqrt, bias=EPS, scale=1.0/nfeat)
    P.recip(rstd_out, rstd_out)

def load_w(P, wbuf, w_dram, k0, kn, f0, fw):
    dst = wbuf[:, 0:kn*fw].rearrange("p (k f) -> p k f", f=fw)
    src = w_dram[k0*128:(k0+kn)*128, f0:f0+fw].rearrange("(k p) f -> p k f", p=128)
    P.dma(dst, src, q="pool")
    return dst

def linear(P, C, w_dram, k0, kn, f_lo, f_hi, act, wbufs, consume, T=TOK, BW=256):
    bidx = getattr(C, "wrot", 0)
    for fb in range(f_lo, f_hi, BW):
        fw = min(BW, f_hi - fb)
        wb = load_w(P, wbufs[bidx % len(wbufs)], w_dram, k0, kn, fb, fw)
        bidx += 1
        for fc in range(0, fw, 128):
            fsz = min(128, fw - fc)
            for tt in range(T // 512):
                ps = bank(C)
                for k in range(kn):
                    P.matmul(ps[0:fsz, :], wb[:, k, fc:fc+fsz], act[:, k, tt*512:(tt+1)*512],
                             start=(k == 0), stop=(k == kn-1))
                consume(fb + fc, fsz, tt, ps[0:fsz, :])
    C.wrot = bidx

def build_t(has_post, has_pre, f_next, is_last):
    nc = new_nc()
    dt = lambda name, shape, kind="ExternalInput", d=F32: nc.dram_tensor(name, list(shape), d, kind=kind).ap()
    xT = dt("xT", [D, TOK])
    NCOL = 16 * 8 + 8
    cols_d = dt("cols", [128, NCOL])
    if has_post:
        oT = dt("oT", [D, TOK])
        gT = dt("gT", [1024 if has_post == "ab" else 2048, TOK])
        w_out = dt("w_out", [D, D])
        w1 = dt("w1", [D, DFF]); w3 = dt("w3", [D, DFF]); w2 = dt("w2", [DFF, D])
    if has_pre:
        w_in = dt("w_in", [D, f_next])
        projT = dt("projT", [f_next, TOK], kind="ExternalOutput")
    xT_out = dt("xT_out", [D, TOK], kind="ExternalOutput")

    P = Prog(nc)
    C = setup_common(P, nc)
    x = P.sbuf("x_sb", [128, KC, TOK], F32)
    xs = [x.sub((slice(None), k, slice(None)), f"x{k}") for k in range(KC)]
    hb = P.sbuf("hb", [128, KC, TOK], BF16)
    cols = P.sbuf("cols_sb", [128, NCOL], F32)
    P.dma(cols, cols_d)
    def load_x():
        for k in range(KC):
            P.dma(xs[k], xT[k*128:(k+1)*128, :], q="act")
    if not has_post:
        load_x()
    wA = [P.sbuf(f"wA{i}", [128, 4096], BF16) for i in range(3)]
    wB = [P.sbuf(f"wB{i}", [128, 4096], BF16) for i in range(3)]
    tmp = [P.sbuf(f"tmp{i}", [128, TOK], F32) for i in range(3)]
    rstd = P.sbuf("rstd", [128, TOK], F32)
    stg = [P.sbuf(f"stg{i}", [128, 512], F32) for i in range(3)]
    gmod = P.sbuf("gmod", [128, 16], F32)
    fin = []
    def prenorm(ng0, sh0, sc0):
        rms_rstd(P, C, xs, D, rstd, tmp[0:2])
        P.ts(gmod, cols[:, sc0:sc0+16], 1.0, None, ALU.add)
        P.tt(gmod, gmod, cols[:, ng0:ng0+16], ALU.mult)
        for k in range(KC):
            t = tmp[k % 2]
            P.stt(t, xs[k], gmod[:, k:k+1], rstd, ALU.mult, ALU.mult)
            P.act(hb[:, k, :], t, AF.Identity, bias=cols[:, sh0+k:sh0+k+1])

    def resid_consumer(gate0):
        def consume(f0, fsz, tt, ps):
            k = f0 // 128
            assert f0 % 128 == 0 and fsz == 128
            xv = xs[k][:, tt*512:(tt+1)*512]
            P.stt(xv, ps, cols[:, gate0+k:gate0+k+1], xv, ALU.mult, ALU.add)
        return consume

    if has_post:
        if has_post == "ab":
            for k in range(8):
                t = tmp[k % 2]
                P.dma(t, oT[k*128:(k+1)*128, :])
                P.copy(hb[:, k, :], t, eng="act" if k % 2 else "dve")
            groups = [[8 + k] for k in range(8)]
            ncol0 = 128
        else:
            groups = [[4*h + j for j in range(4)] for h in range(4)]
            ncol0 = 128
        ot = [P.sbuf(f"ot{i}", [128, TOK], F32) for i in range(4)]
        extra = []
        for i in range(3):
            f = wB[i].bitcast(F32)
            extra += [f.sub((slice(None), slice(0, 1024)), f"wBx{i}a"), f.sub((slice(None), slice(1024, 2048)), f"wBx{i}b")]
        obufs = ot + extra[0:4]
        rstds = [rstd, extra[4]]
        zts = [tmp[2], extra[5]]
        nz = 0
        for g, grp in enumerate(groups):
            srcs = []
            for j, k in enumerate(grp):
                ob = obufs[g % 8] if len(grp) == 1 else obufs[(g % 2) * 4 + j]
                P.dma(ob, oT[k*128:(k+1)*128, :])
                srcs.append(ob)
            rs_ = rstds[g % 2]
            rms_rstd(P, C, srcs, 128 * len(grp), rs_, tmp[0:2])
            for j, k in enumerate(grp):
                gk = k - 8 if has_post == "ab" else k
                zt = zts[nz % 2]; nz += 1
                P.dma(zt, gT[gk*128:(gk+1)*128, :])
                P.act(zt, zt, AF.Silu)
                t = tmp[j % 2]
                P.stt(t, srcs[j], cols[:, ncol0+j:ncol0+j+1], rs_, ALU.mult, ALU.mult)
                P.tt(hb[:, k, :], t, zt, ALU.mult)
        for i in range(3):
            P.add("pool", (lambda i: (lambda e: e.memset(wB[i].ap[:, 0:2], 0.0)))(i),
                  writes=[extra[2*i], extra[2*i+1], wB[i]])
        load_x()
        linear(P, C, w_out, 0, KC, 0, D, hb, wA, resid_consumer(0))
        prenorm(16, 32, 48)
        gblk = P.sbuf("gblk", [128, 11, TOK], BF16)
        for bi in range(4):
            c0 = bi * 11
            bidx = 0
            for fb in range(c0*128, (c0+11)*128, 256):
                fw = min(256, (c0+11)*128 - fb)
                wa = load_w(P, wA[bidx % 3], w1, 0, KC, fb, fw)
                wb = load_w(P, wB[bidx % 3], w3, 0, KC, fb, fw)
                bidx += 1
                for fc in range(0, fw, 128):
                    kk = (fb + fc) // 128 - c0
                    for tt in range(NTT):
                        pa = bank(C); pb = bank(C)
                        for k in range(KC):
                            P.matmul(pa, wa[:, k, fc:fc+128], hb[:, k, tt*512:(tt+1)*512], start=(k == 0), stop=(k == KC-1))
                        for k in range(KC):
                            P.matmul(pb, wb[:, k, fc:fc+128], hb[:, k, tt*512:(tt+1)*512], start=(k == 0), stop=(k == KC-1))
                        s = stg[C.ev % 3]; C.ev += 1
                        P.act(s, pa, AF.Silu)
                        P.tt(gblk[:, kk, tt*512:(tt+1)*512], s, pb, ALU.mult)
            linear(P, C, w2, c0, 11, 0, D, gblk, wA, resid_consumer(64))
    if has_pre:
        prenorm(80, 96, 112)
        def consume(f0, fsz, tt, ps):
            s = stg[C.ev % 3]
            if C.ev % 2:
                P.copy(s[0:fsz, :], ps, eng="act")
            else:
                P.copy(s[0:fsz, :], ps, eng="dve")
            C.ev += 1
            fin.append(P.dma(projT[f0:f0+fsz, tt*512:(tt+1)*512], s[0:fsz, :]))
        linear(P, C, w_in, 0, KC, 0, f_next, hb, wA, consume)
    if is_last:
        rms_rstd(P, C, xs, D, rstd, tmp[0:2])
        for k in range(KC):
            P.stt(xs[k], xs[k], cols[:, 80+k:80+k+1], rstd, ALU.mult, ALU.mult)
    for k in range(KC):
        fin.append(P.dma(xT_out[k*128:(k+1)*128, :], xs[k]))
    P.emit(final_wait_ops=fin)
    return nc

def build_ada():
    nc = new_nc()
    c_col = nc.dram_tensor("c_col", [128, 16], F32, kind="ExternalInput").ap()
    w = nc.dram_tensor("w", [8, D, 768], F32, kind="ExternalInput").ap()
    b = nc.dram_tensor("b", [128, 48], F32, kind="ExternalInput").ap()
    o = nc.dram_tensor("o", [128, 48], F32, kind="ExternalOutput").ap()
    P = Prog(nc)
    cc = P.sbuf("cc", [128, 16], F32)
    sc = P.sbuf("sc", [128, 16], F32)
    bb = P.sbuf("bb", [128, 48], F32)
    ob = P.sbuf("ob", [128, 48], F32)
    P.dma(cc, c_col); P.dma(bb, b)
    P.act(sc, cc, AF.Silu)
    wb = [P.sbuf(f"w{i}", [128, 16, 768], F32) for i in range(2)]
    ps = P.psum("ps", [128, 512], F32)
    for m in range(8):
        wt = wb[m % 2]
        for q in range(4):
            P.dma(wt[:, 4*q:4*q+4, :], w[m, q*512:(q+1)*512, :].rearrange("(k p) f -> p k f", p=128), q="sp" if q % 2 else "act")
        for fc in range(6):
            for k in range(16):
                P.matmul(ps[:, m*6+fc:m*6+fc+1], wt[:, k, fc*128:(fc+1)*128], sc[:, k:k+1], start=(k == 0), stop=(k == 15))
    P.tt(ob, ps[:, 0:48], bb, ALU.add)
    f = P.dma(o, ob)
    P.emit(final_wait_ops=[f])
    return nc


S = 8192; NT = 16; EPS = 1e-6

def mla_phase(P, C, nc, pfx=""):
    dt = lambda name, shape, kind="ExternalInput", d=F32: nc.dram_tensor(pfx + name, list(shape), d, kind=kind).ap()
    cqT = dt("cqT", [512, S]); ckvT = dt("ckvT", [512, S]); krT = dt("krT", [64, S]); krsT = dt("krsT", [64, S])
    ncols = dt("mla_cols", [128, 10])
    wq = dt("wq", [512, 256])
    wkv = dt("wkv", [512, 256])
    pos = dt("pos", [1, S], d=I32)
    masks_d = dt("masks", [128, 4, 512])
    oT = dt("oaT", [128, S], kind="ExternalOutput")
    fin = []
    SCALE = 1.0 / np.sqrt(192.0)

    cols = P.sbuf("mcols", [128, 10], F32); P.dma(cols, ncols)
    wq_b = P.sbuf("wq_b", [128, 4, 256], BF16); P.dma(wq_b, wq.rearrange("(k p) f -> p k f", p=128), q="pool")
    wkv_b = P.sbuf("wkv_b", [128, 4, 256], BF16); P.dma(wkv_b, wkv.rearrange("(k p) f -> p k f", p=128), q="pool")
    mk = P.sbuf("mk", [128, 4, 512], BF16); P.dma(mk, masks_d, q="pool")
    ones_b = P.sbuf("ones_b", [128, 128], BF16); P.memset(ones_b, 1.0)

    qa = P.sbuf("qa", [128, S], BF16); qb = P.sbuf("qb", [64, S], BF16)
    ka = P.sbuf("ka", [128, S], BF16); kb = P.sbuf("kb", [64, S], BF16)
    vt = P.sbuf("vt", [128, 64, 128], BF16)
    cos2 = P.sbuf("cos2", [64, S], F32); sinpm = P.sbuf("sinpm", [64, S], F32)
    tl = lambda t: slice(t*512, (t+1)*512)
    qa_t = [qa.sub((slice(None), tl(t))) for t in range(NT)]; qb_t = [qb.sub((slice(None), tl(t))) for t in range(NT)]
    ka_t = [ka.sub((slice(None), tl(t))) for t in range(NT)]; kb_t = [kb.sub((slice(None), tl(t))) for t in range(NT)]
    vt_t = [vt.sub((slice(None), slice(4*t, 4*t+4), slice(None))) for t in range(NT)]

    SEG = 512
    pi_ = P.sbuf("pos_i", [64, SEG], I32); u = P.sbuf("rp_u", [64, SEG], F32); kf = P.sbuf("rp_kf", [64, SEG], F32)
    ki = P.sbuf("rp_ki", [64, SEG], I32)
    TWO_PI = float(2*np.pi)
    for sg in range(S // SEG):
        sl = slice(sg*SEG, (sg+1)*SEG)
        P.dma(pi_, pos[:, sl].partition_broadcast(64))
        P.copy(kf, pi_)
        for which, off, dst in (("sin", 0.5, sinpm), ("cos", 0.75, cos2)):
            P.ts(u, kf, cols[0:64, 8:9], 1.0/TWO_PI, ALU.mult, ALU.mult)
            P.ts(u, u, off, None, ALU.add)
            P.copy(ki, u)
            ang = dst[:, sl]
            P.copy(ang, ki)
            P.tt(u, u, ang, ALU.subtract)
            P.ts(ang, u, 0.0, None, ALU.is_lt)
            P.tt(u, u, ang, ALU.add)
            P.act(ang, u, AF.Sin, bias=-float(np.pi), scale=TWO_PI)
        P.ts(sinpm[:, sl], sinpm[:, sl], cols[0:64, 9:10], None, ALU.mult)

    cq0 = P.sbuf("cq0", [128, 4, 512], F32); cq = [cq0, cq0]
    cn = [P.sbuf(f"cn{i}", [128, 4, 512], BF16) for i in range(2)]
    sqt = [P.sbuf(f"sqt{i}", [128, 512], F32) for i in range(1)]
    rst = P.sbuf("rst", [128, 512], F32)
    kr = [P.sbuf(f"kr{i}", [64, 512], F32) for i in range(2)]
    krs = [P.sbuf(f"krs{i}", [64, 512], F32) for i in range(2)]
    r1 = P.sbuf("r1", [64, 512], F32); r2 = P.sbuf("r2", [64, 512], F32)
    tmpn = P.sbuf("tmpn", [128, 512], F32)

    def load_norm(src, t, ncol0, i):
        P.dma(cq[i], src[:, tl(t)].rearrange("(k p) t -> p k t", p=128))
        rms_rstd(P, C, [cq[i][:, k, :] for k in range(4)], 512, rst, sqt, T=512)
        for k in range(4):
            P.stt(tmpn, cq[i][:, k, :], cols[:, ncol0+k:ncol0+k+1], rst, ALU.mult, ALU.mult)
            P.copy(cn[i][:, k, :], tmpn, eng="act")

    def rope(dst, a, b, t):
        P.tt(r1, a, cos2[:, tl(t)], ALU.mult)
        P.tt(r2, b, sinpm[:, tl(t)], ALU.mult)
        P.tt(dst, r1, r2, ALU.add)

    for t in range(NT):
        load_norm(cqT, t, 0, 0)
        ps = bank(C)
        for k in range(4):
            P.matmul(ps, wq_b[:, k, 0:128], cn[0][:, k, :], start=(k == 0), stop=(k == 3))
        P.copy(qa_t[t], ps, eng="act")
        ps1 = bank(C)
        for k in range(4):
            P.matmul(ps1[0:64, :], wq_b[:, k, 128:192], cn[0][:, k, :], start=(k == 0), stop=(k == 3))
        ps2 = bank(C)
        for k in range(4):
            P.matmul(ps2[0:64, :], wq_b[:, k, 192:256], cn[0][:, k, :], start=(k == 0), stop=(k == 3))
        rope(qb_t[t], ps1[0:64, :], ps2[0:64, :], t)
        load_norm(ckvT, t, 4, 1)
        ps = bank(C)
        for k in range(4):
            P.matmul(ps, wkv_b[:, k, 0:128], cn[1][:, k, :], start=(k == 0), stop=(k == 3))
        P.copy(ka_t[t], ps, eng="act")
        ps = bank(C)
        for blk in range(4):
            for k in range(4):
                P.matmul(ps[:, blk*128:(blk+1)*128], cn[1][:, k, blk*128:(blk+1)*128], wkv_b[:, k, 128:256],
                         start=(k == 0), stop=(k == 3))
        P.copy(vt_t[t], ps.rearrange("p (b d) -> p b d", b=4), eng="act")
        i = t % 2
        P.dma(kr[i], krT[:, tl(t)]); P.dma(krs[i], krsT[:, tl(t)])
        rope(kb_t[t], kr[i], krs[i], t)

    sb = [C.banks[0], C.banks[1], C.banks[2], C.banks[3], C.banks[4]]
    oacc = [C.banks[5], C.banks[6]]; sbank = C.banks[7]
    pb = [P.sbuf(f"pexp{i}", [128, 512], BF16) for i in range(5)]
    rs = rst
    ost = [tmpn, sqt[0]]
    pairs = [(j, b) for j in range(NT) for b in range(4*j + 4)]
    LA = 3
    def emit_qk(n):
        j, b = pairs[n]
        ps = sb[n % 5]
        kt, ko = b // 4, (b % 4) * 128
        P.matmul(ps, ka_t[kt][:, ko:ko+128], qa_t[j], start=True, stop=False)
        P.matmul(ps, kb_t[kt][:, ko:ko+128], qb_t[j], start=False, stop=True)
    def emit_rest(n):
        j, b = pairs[n]
        ps = sb[n % 5]; pe = pb[n % 5]
        kt = b // 4
        nkb = 4*j + 4
        oa = oacc[j % 2]
        P.act(pe, ps, AF.Exp, scale=SCALE)
        if b >= 4*j:
            P.tt(pe, pe, mk[:, b - 4*j, :], ALU.mult, eng="pool")
        P.matmul(oa, vt_t[kt][:, b % 4, :], pe, start=(b == 0), stop=(b == nkb-1))
        P.matmul(sbank, ones_b, pe, start=(b == 0), stop=(b == nkb-1))
        if b == nkb - 1:
            P.recip(rs, sbank)
            o = ost[j % 2]
            P.tt(o, oa, rs, ALU.mult)
            fin.append(P.dma(oT[:, tl(j)], o))
    for n in range(len(pairs) + LA):
        if n < len(pairs):
            emit_qk(n)
        if n - LA >= 0:
            emit_rest(n - LA)
    return fin

def build_mla():
    nc = new_nc()
    P = Prog(nc)
    C = setup_common(P, nc)
    fin = mla_phase(P, C, nc)
    P.emit(final_wait_ops=fin)
    return nc


S = 8192; NCH = 128; CH = 64; TILE = 512; CPT = 8

def gla_phase(P, C, nc, pfx=""):
    dt = lambda name, shape, kind="ExternalInput", d=F32: nc.dram_tensor(pfx + name, list(shape), d, kind=kind).ap()
    qT = dt("gqT", [256, S]); kT = dt("gkT", [256, S])
    ktok = dt("gktok", [S, 256]); vtok = dt("gvtok", [S, 256])
    glow = dt("glowT", [16, S]); waug = dt("gwaug", [17, 256])
    cst = dt("gcst", [64, 192])
    o_d = dt("go", [S, 256], kind="ExternalOutput")
    fin = []
    cs = P.sbuf("gcst_sb", [64, 192], F32); P.dma(cs, cst)
    triS = cs[:, 0:64]; triR = cs[:, 64:128]; maskT = cs[:, 128:192]
    wa = P.sbuf("gwa", [17, 256], F32); P.dma(wa, waug)
    st = P.sbuf("gstate", [128, 2, 256], F32); P.memset(st, 0.0)
    stb = P.sbuf("gstate_b", [128, 2, 256], BF16); P.memset(stb, 0.0)
    NB = 3
    q_in = [P.sbuf(f"gq_in{i}", [128, 2, TILE], F32) for i in range(NB)]
    k_in = [P.sbuf(f"gk_in{i}", [128, 2, TILE], F32) for i in range(NB)]
    kt_in = [P.sbuf(f"gkt_in{i}", [64, CPT, 256], F32) for i in range(NB)]
    v_in = [P.sbuf(f"gv_in{i}", [64, CPT, 256], BF16) for i in range(NB)]
    gl_in = [P.sbuf(f"ggl_in{i}", [17, TILE], F32) for i in range(NB)]
    for g in gl_in:
        P.memset(g, 1.0)
    o_st = [P.sbuf(f"go_st{i}", [64, CPT, 256], F32) for i in range(NB)]
    R = 4
    e1 = [P.sbuf(f"ge1{i}", [64, 256], F32) for i in range(R)]
    lnv = [P.sbuf(f"glnv{i}", [64, 256], F32) for i in range(R)]
    E = [P.sbuf(f"gE{i}", [128, 128], F32) for i in range(R+1)]
    Ei = [P.sbuf(f"gEi{i}", [128, 128], F32) for i in range(R)]
    Er = [P.sbuf(f"gEr{i}", [64, 256], F32) for i in range(R)]
    qt = [P.sbuf(f"gqt{i}", [128, 2, 64], BF16) for i in range(R)]
    kt = [P.sbuf(f"gkt{i}", [128, 2, 64], BF16) for i in range(R)]
    kd = [P.sbuf(f"gkd{i}", [64, 256], BF16) for i in range(R)]
    atm = [P.sbuf(f"gatm{i}", [64, 64], BF16) for i in range(R)]
    NTL = S // TILE
    def load(t):
        i = t % NB
        sl = slice(t*TILE, (t+1)*TILE)
        P.dma(q_in[i], qT[:, sl].rearrange("(k p) t -> p k t", p=128))
        P.dma(k_in[i], kT[:, sl].rearrange("(k p) t -> p k t", p=128))
        P.dma(kt_in[i], ktok[sl, :].rearrange("(c p) d -> p c d", p=64))
        P.dma(v_in[i], vtok[sl, :].rearrange("(c p) d -> p c d", p=64), q="pool")
        P.dma(gl_in[i][0:16, :], glow[:, sl])
    def prep(c):
        t, cc = c // CPT, c % CPT
        if cc == 0:
            load(t)
        i = t % NB
        r = c % R
        cl = slice(cc*64, (cc+1)*64)
        ps = bank(C)
        P.matmul(ps[0:64, 0:256], gl_in[i][:, cl], wa)
        P.act(e1[r], ps[0:64, 0:256], AF.Exp, scale=-1.0)
        P.act(lnv[r], e1[r], AF.Ln, bias=1.0)
        ps2 = bank(C)
        for dc in range(2):
            P.matmul(ps2[:, dc*64:(dc+1)*64], lnv[r][:, dc*128:(dc+1)*128], triS)
        ps3 = bank(C)
        P.matmul(ps3[0:64, 0:256], triR, lnv[r])
        Ec = E[c % (R+1)]
        P.act(Ec, ps2[:, 0:128], AF.Exp)
        P.act(Ei[r], ps2[:, 0:128], AF.Exp, scale=-1.0)
        P.act(Er[r], ps3[0:64, 0:256], AF.Exp)
        P.stt(qt[r], q_in[i][:, :, cl], 1.0/16.0, Ec.rearrange("p (k j) -> p k j", k=2), ALU.mult, ALU.mult)
        P.tt(kt[r], k_in[i][:, :, cl], Ei[r].rearrange("p (k j) -> p k j", k=2), ALU.mult)
        P.tt(kd[r], kt_in[i][:, cc, :], Er[r], ALU.mult)
        ps4 = bank(C)
        for dc in range(2):
            P.matmul(ps4[0:64, 0:64], kt[r][:, dc, :], qt[r][:, dc, :], start=(dc == 0), stop=(dc == 1))
        P.tt(atm[r], ps4[0:64, 0:64], maskT, ALU.mult)
    def scan(c):
        t, cc = c // CPT, c % CPT
        i = t % NB
        r = c % R
        Ec = E[c % (R+1)]
        ps5 = bank(C)
        for dc in range(2):
            P.matmul(ps5[0:64, 0:256], qt[r][:, dc, :], stb[:, dc, :], start=(dc == 0), stop=False)
        P.matmul(ps5[0:64, 0:256], atm[r], v_in[i][:, cc, :], start=False, stop=True)
        P.copy(o_st[i][:, cc, :], ps5[0:64, 0:256], eng="act")
        ps6 = bank(C)
        for dc in range(2):
            P.matmul(ps6[:, dc*256:(dc+1)*256], kd[r][:, dc*128:(dc+1)*128], v_in[i][:, cc, :])
        for dc in range(2):
            P.stt(st[:, dc, :], st[:, dc, :], Ec[:, dc*64+63:dc*64+64], ps6[:, dc*256:(dc+1)*256], ALU.mult, ALU.add)
        P.copy(stb, st, eng="act")
        if cc == CPT - 1:
            sl = slice(t*TILE, (t+1)*TILE)
            fin.append(P.dma(o_d[sl, :].rearrange("(c p) e -> p c e", p=64), o_st[i]))
    NC_ = NTL * CPT
    LA = 2
    for c in range(NC_ + LA):
        if c < NC_:
            prep(c)
        if c - LA >= 0:
            scan(c - LA)
    return fin

def build_gla():
    nc = new_nc()
    P = Prog(nc)
    C = setup_common(P, nc)
    fin = gla_phase(P, C, nc)
    P.emit(final_wait_ops=fin)
    return nc


DEBUG = False
S = 8192; TILE = 512; CPT = 8; EPS = 1e-6

def gdn_phase(P, C, nc, pfx=""):
    dt = lambda name, shape, kind="ExternalInput", d=F32: nc.dram_tensor(pfx + name, list(shape), d, kind=kind).ap()
    xT = dt("dxT", [384, S]); cw_d = dt("dcw", [128, 14]); bl_d = dt("dbl", [1, S]); al_d = dt("dal", [1, S])
    c64_d = dt("dc64", [64, 256]); id_d = dt("did", [128, 128]); rm_d = dt("drm", [128, 512])
    o_d = dt("dobT", [128, S], kind="ExternalOutput")
    fin = []
    C.nrot = 7
    cw = P.sbuf("dcw_sb", [128, 14], F32); P.dma(cw, cw_d)
    c64 = P.sbuf("dc64_sb", [64, 256], F32); P.dma(c64, c64_d)
    I64 = c64[:, 0:64]; mLs = c64[:, 64:128]; mUi = c64[:, 128:192]; mUs = c64[:, 192:256]
    ident = P.sbuf("did_sb", [128, 128], F32); P.dma(ident, id_d)
    rmask = P.sbuf("drm_sb", [128, 512], F32); P.dma(rmask, rm_d)
    negA = P.sbuf("dnegA", [128, 1], F32)
    P.act(negA, cw[:, 12:13], AF.Exp)
    Ss = [P.sbuf(f"dS{i}", [128, 128], F32) for i in range(3)]
    P.memset(Ss[0], 0.0)
    T3 = lambda v: v.rearrange("p (c j) -> p c j", j=64)
    def sb(name, shape, d=F32, n=2):
        return [P.sbuf(f"{name}{i}", shape, d) for i in range(n)]
    xin = [sb("dxq", [128, TILE+3]), sb("dxk", [128, TILE+3]), sb("dxv", [128, TILE+3])]
    y = sb("dy", [128, TILE], n=3); sq = sb("dsq", [128, TILE], n=2); rstd = sb("drstd", [128, TILE], n=1)[0]
    qn = sb("dqn", [128, TILE]); kn = sb("dkn", [128, TILE]); vv = sb("dvv", [128, TILE])
    bl = sb("dblb", [128, TILE]); al = sb("dalb", [128, TILE])
    beta = sb("dbeta", [128, TILE]); g = sb("dg", [128, TILE], n=1)[0]; gc = sb("dgc", [128, TILE]); egc = sb("degc", [128, TILE])
    ekd = sb("dekd", [128, TILE]); egl = sb("degl", [128, CPT])
    knb = sb("dknb", [128, TILE], BF16); kbb = sb("dkbb", [128, TILE], BF16); qnb = sb("dqnb", [128, TILE], BF16)
    kb = sb("dkb", [128, TILE], n=1)[0]; kbg = sb("dkbg", [128, TILE], BF16, n=1)[0]; kdc = sb("dkdc", [128, TILE], BF16, n=1)[0]
    vb = sb("dvb", [128, TILE], BF16, n=1)[0]
    kbg_t = sb("dkbgt", [64, CPT, 128], BF16); vb_t = sb("dvbt", [64, CPT, 128], BF16); kdc_t = sb("dkdct", [64, CPT, 128], BF16)
    identb = P.sbuf("didb", [128, 128], BF16); P.copy(identb, ident)
    Sbs = [P.sbuf(f"dSb{i}", [128, 128], BF16) for i in range(3)]
    P.memset(Sbs[0], 0.0)
    gtok = sb("dgtok", [64, CPT], n=1)[0]; tmp64 = sb("dtmp64", [64, TILE], n=2)
    D1 = sb("dD1", [64, TILE], n=1)[0]
    dec = sb("ddec", [64, TILE], n=1)[0]; decT = sb("ddecT", [64, TILE], n=1)[0]
    Ym = sb("dY", [64, TILE]); Pn = sb("dPn", [64, TILE], BF16); PTn = sb("dPTn", [64, TILE], BF16); Yb = sb("dYb", [64, TILE], BF16)
    intraT = sb("dintraT", [64, TILE], BF16)
    u_t = sb("du", [64, CPT, 128], BF16); w_t = sb("dwt", [64, CPT, 128], BF16, n=1)[0]
    MT = sb("dMT", [128, CPT, 128]); Bc = sb("dBc", [128, CPT, 128]); qpT = sb("dqpT", [128, TILE], BF16)
    qdf = sb("dqdf", [128, TILE], n=1)[0]
    ost = sb("dost", [128, TILE])

    NT = S // TILE
    def prepass(t):
        i = t % 2
        sl = slice(t*TILE, (t+1)*TILE)
        for kind in range(3):
            xi = xin[kind][i]
            if t == 0:
                P.memset(xi[:, 0:3], 0.0, eng="dve")
                P.dma(xi[:, 3:TILE+3], xT[kind*128:(kind+1)*128, 0:TILE])
            else:
                P.dma(xi, xT[kind*128:(kind+1)*128, t*TILE-3:(t+1)*TILE])
            yy = y[kind]
            P.ts(yy, xi[:, 0:TILE], cw[:, kind*4:kind*4+1], None, ALU.mult)
            for j in range(1, 4):
                P.stt(yy, xi[:, j:j+TILE], cw[:, kind*4+j:kind*4+j+1], yy, ALU.mult, ALU.add)
            if kind == 2:
                P.act(vv[i], yy, AF.Silu)
            else:
                P.act(yy, yy, AF.Silu)
                P.act(C.sqb[kind], yy, AF.Square)
                ps = bank(C)
                P.matmul(ps, C.ones_b, C.sqb[kind])
                P.act(rstd, ps, AF.Sqrt, bias=EPS)
                P.recip(rstd, rstd)
                if kind == 0:
                    P.stt(qn[i], yy, float(1.0/np.sqrt(128.0)), rstd, ALU.mult, ALU.mult)
                else:
                    P.tt(kn[i], yy, rstd, ALU.mult)
            yield
        P.dma(bl[i], bl_d[:, sl].partition_broadcast(128))
        P.dma(al[i], al_d[:, sl].partition_broadcast(128))
        P.act(beta[i], bl[i], AF.Sigmoid)
        P.act(g, al[i], AF.Exp, bias=cw[:, 13:14])
        P.act(g, g, AF.Ln, bias=1.0)
        P.ts(g, g, negA, -1.0, ALU.mult, ALU.mult)
        P.add("dve", (lambda o, m, gg: (lambda e: e.tensor_tensor_scan(o.ap, m.ap, gg.ap, 0.0, ALU.mult, ALU.add)))(gc[i], rmask, g),
              reads=[rmask, g], writes=[gc[i]])
        P.act(egc[i], gc[i], AF.Exp)
        gl3 = T3(gc[i])[:, :, 63:64]
        P.tt(T3(ekd[i]), gl3.bcast([128, CPT, 64]), T3(gc[i]), ALU.subtract)
        P.act(ekd[i], ekd[i], AF.Exp)
        P.act(egl[i], T3(gc[i])[:, :, 63], AF.Exp)
        yield
        P.copy(knb[i], kn[i], eng="act")
        P.copy(qnb[i], qn[i], eng="act")
        P.tt(kb, kn[i], beta[i], ALU.mult)
        P.copy(kbb[i], kb, eng="act")
        P.tt(kbg, kb, egc[i], ALU.mult)
        P.tt(kdc, kn[i], ekd[i], ALU.mult)
        P.tt(qdf, qn[i], egc[i], ALU.mult)
        P.tt(vb, vv[i], beta[i], ALU.mult)
        yield
        for src, dst in ((kbg, kbg_t[i]), (vb, vb_t[i]), (kdc, kdc_t[i])):
            for half in range(2):
                ps = bank(C).bitcast(BF16)
                for cq in range(4):
                    cc = half*4 + cq
                    P.transpose(ps[0:64, cq*128:(cq+1)*128], src[:, cc*64:(cc+1)*64], identb)
                P.copy(dst[:, half*4:half*4+4, :], ps[0:64, 0:512].rearrange("p (c d) -> p c d", c=4), eng="act" if half else "dve")
                yield
        P.tt(T3(tmp64[0]), T3(gc[i][0:64, :]), I64.rearrange("p (o j) -> p o j", o=1).bcast([64, CPT, 64]), ALU.mult)
        P.reduce(gtok, T3(tmp64[0]), ALU.add)
        P.tt(T3(D1), T3(gc[i][0:64, :]), gtok.rearrange("p (c o) -> p c o", o=1).bcast([64, CPT, 64]), ALU.subtract)
        P.ts(tmp64[0], D1, 0.0, None, ALU.max)
        P.act(dec, tmp64[0], AF.Exp, scale=-1.0)
        P.ts(tmp64[1], D1, 0.0, None, ALU.min)
        P.act(decT, tmp64[1], AF.Exp)
        yield
        psL = bank(C); psLT = bank(C); psQ = bank(C)
        for cc in range(CPT):
            cl = slice(cc*64, (cc+1)*64)
            P.matmul(psL[0:64, cl], kbb[i][:, cl], knb[i][:, cl])
            P.matmul(psLT[0:64, cl], knb[i][:, cl], kbb[i][:, cl])
            P.matmul(psQ[0:64, cl], knb[i][:, cl], qnb[i][:, cl])
        bc64 = lambda m: m.rearrange("p (o j) -> p o j", o=1).bcast([64, CPT, 64])
        P.tt(T3(tmp64[0]), T3(dec), bc64(mLs), ALU.mult)
        P.stt(Pn[0], psL[0:64, :], -1.0, tmp64[0], ALU.mult, ALU.mult)
        P.tt(T3(tmp64[1]), T3(decT), bc64(mUs), ALU.mult)
        P.stt(PTn[0], psLT[0:64, :], -1.0, tmp64[1], ALU.mult, ALU.mult)
        P.tt(T3(tmp64[1]), T3(decT), bc64(mUi), ALU.mult)
        P.tt(intraT[i], psQ[0:64, :], tmp64[1], ALU.mult)
        Y = Ym[0]
        P.tt(T3(Y), T3(PTn[0]), bc64(I64), ALU.add)
        P.copy(Yb[0], Y, eng="act")
        ybi = 0
        yield
        cur = 0
        for n in range(1, 6):
            nxt = 1 - cur
            psP = bank(C); psPT = bank(C)
            for cc in range(CPT):
                cl = slice(cc*64, (cc+1)*64)
                P.matmul(psP[0:64, cl], PTn[cur][:, cl], Pn[cur][:, cl])
                if n < 5:
                    P.matmul(psPT[0:64, cl], Pn[cur][:, cl], PTn[cur][:, cl])
            P.copy(Pn[nxt], psP[0:64, :], eng="act")
            if n < 5:
                P.copy(PTn[nxt], psPT[0:64, :], eng="dve")
            yield
            psY = bank(C)
            for cc in range(CPT):
                cl = slice(cc*64, (cc+1)*64)
                P.matmul(psY[0:64, cl], Pn[nxt][:, cl], Yb[ybi][:, cl])
            Y2 = Ym[1] if Y is Ym[0] else Ym[0]
            P.tt(Y2, Y, psY[0:64, :], ALU.add)
            Y = Y2
            ybi = 1 - ybi
            P.copy(Yb[ybi], Y, eng="act")
            yield
            cur = nxt
        for half in range(2):
            psu = bank(C); psw = bank(C)
            for cq in range(4):
                cc = half*4 + cq
                cl = slice(cc*64, (cc+1)*64)
                P.matmul(psu[0:64, cq*128:(cq+1)*128], Yb[ybi][:, cl], vb_t[i][:, cc, :])
                P.matmul(psw[0:64, cq*128:(cq+1)*128], Yb[ybi][:, cl], kbg_t[i][:, cc, :])
            P.copy(u_t[i][:, half*4:half*4+4, :], psu[0:64, :].rearrange("p (c d) -> p c d", c=4), eng="act")
            P.copy(w_t[:, half*4:half*4+4, :], psw[0:64, :].rearrange("p (c d) -> p c d", c=4), eng="dve")
            yield
        psq = bank(C)
        for cc in range(CPT):
            cl = slice(cc*64, (cc+1)*64)
            P.matmul(psq[:, cl], w_t[:, cc, :], intraT[i][:, cl])
        P.tt(qpT[i], qdf, psq, ALU.subtract)
        yield
        for half in range(2):
            psm = bank(C); psb = bank(C)
            for cq in range(4):
                cc = half*4 + cq
                P.matmul(psm[:, cq*128:(cq+1)*128], w_t[:, cc, :], kdc_t[i][:, cc, :])
                P.matmul(psb[:, cq*128:(cq+1)*128], kdc_t[i][:, cc, :], u_t[i][:, cc, :])
            for cq in range(4):
                cc = half*4 + cq
                P.stt(MT[i][:, cc, :], ident, egl[i][:, cc:cc+1], psm[:, cq*128:(cq+1)*128], ALU.mult, ALU.subtract)
            P.copy(Bc[i][:, half*4:half*4+4, :], psb.rearrange("p (c d) -> p c d", c=4), eng="act")
            yield

    def scan_tile(t, gen):
        i = t % 2
        sl = slice(t*TILE, (t+1)*TILE)
        C.nrot = 7
        pso = C.banks[7]
        for cc in range(CPT):
            c = t*CPT + cc
            cl = slice(cc*64, (cc+1)*64)
            S_cur = Ss[c % 3]; S_nxt = Ss[(c + 1) % 3]
            p1 = bank(C)
            P.matmul(p1[:, 0:128], MT[i][:, cc, :], S_cur)
            P.tt(S_nxt, p1[:, 0:128], Bc[i][:, cc, :], ALU.add)
            P.copy(Sbs[(c + 1) % 3], S_nxt, eng="act")
            P.matmul(pso[:, cl], Sbs[c % 3], qpT[i][:, cl], start=True, stop=False)
            P.matmul(pso[:, cl], u_t[i][:, cc, :], intraT[i][:, cl], start=False, stop=True)
            if gen is not None:
                for _ in range(5):
                    next(gen, None)
        P.copy(ost[i], pso, eng="act")
        fin.append(P.dma(o_d[:, sl], ost[i]))

    for _ in prepass(0):
        pass
    for t in range(NT):
        gen = prepass(t + 1) if t + 1 < NT else None
        scan_tile(t, gen)
        if gen is not None:
            for _ in gen:
                pass
    return fin

def build_gdn():
    nc = new_nc()
    P = Prog(nc)
    C = setup_common(P, nc)
    fin = gdn_phase(P, C, nc)
    P.emit(final_wait_ops=fin)
    return nc

def colv(v):
    v = np.asarray(v, np.float32)
    return np.ascontiguousarray(v.reshape(-1, 128).T)
def make_cols(gate_mix=None, ffn=None, pre=None, mixnorm=None):
    cols = np.zeros((128, 136), np.float32)
    if gate_mix is not None: cols[:, 0:16] = colv(gate_mix)
    if ffn is not None:
        ng, sh, sc, gt = ffn
        cols[:, 16:32] = colv(ng); cols[:, 32:48] = colv(sh); cols[:, 48:64] = colv(sc); cols[:, 64:80] = colv(gt)
    if pre is not None:
        ng, sh, sc = pre
        cols[:, 80:96] = colv(ng)
        if sh is not None:
            cols[:, 96:112] = colv(sh); cols[:, 112:128] = colv(sc)
    if mixnorm is not None:
        m = colv(mixnorm)
        cols[:, 128:128+m.shape[1]] = m
    return cols
def w_in_even(w):
    kr = w[:, 1024:1088]
    return np.ascontiguousarray(np.concatenate([w, kr[:, 32:], kr[:, :32]], axis=1))
def mla_consts():
    half = 32
    inv = (10000.0 ** (-np.arange(half, dtype=np.float32) / half)).astype(np.float32)
    invf = np.zeros(128, np.float32); invf[:64] = np.concatenate([inv, inv])
    sign = np.zeros(128, np.float32); sign[:32] = -1; sign[32:64] = 1
    k = np.arange(128)[:, None, None]; r = np.arange(4)[None, :, None]; q = np.arange(512)[None, None, :]
    masks = (q >= k + 128*r).astype(np.float32)
    return invf, sign, np.ascontiguousarray(masks)
def mla_inputs(proj, positions, q_norm, kv_norm, w_uq, w_ukv, h):
    invf, sign, masks = mla_consts()
    cols = np.zeros((128, 10), np.float32)
    cols[:, 0:4] = colv(q_norm); cols[:, 4:8] = colv(kv_norm); cols[:, 8] = invf; cols[:, 9] = sign
    wqh = w_uq[:, h*192:(h+1)*192]
    wq = np.concatenate([wqh[:, :128], wqh[:, 128:192], wqh[:, 160:192], wqh[:, 128:160]], axis=1)
    wkv = w_ukv[:, h*256:(h+1)*256]
    return {"cqT": np.ascontiguousarray(proj[:, 0:512].T), "ckvT": np.ascontiguousarray(proj[:, 512:1024].T),
            "krT": np.ascontiguousarray(proj[:, 1024:1088].T), "krsT": np.ascontiguousarray(proj[:, 5200:5264].T),
            "mla_cols": cols, "wq": np.ascontiguousarray(wq), "wkv": np.ascontiguousarray(wkv),
            "pos": np.ascontiguousarray(positions.reshape(1, -1).astype(np.int32)), "masks": masks}
def gla_consts():
    tp = np.arange(64)[:, None]; t = np.arange(64)[None, :]
    triS = np.where(tp <= t, -1.0/16.0, 0.0); triR = np.where(tp > t, -1.0/16.0, 0.0)
    maskT = np.where(t >= tp, 1.0, 0.0)
    return np.ascontiguousarray(np.concatenate([triS, triR, maskT], axis=1).astype(np.float32))
def gla_inputs(proj, w_gk2, b_gk2, core):
    h, e = core // 2, core % 2
    q = proj[:, h*256:(h+1)*256]; k = proj[:, 1024+h*256:1024+(h+1)*256]
    v = proj[:, 2048+h*512+e*256:2048+h*512+(e+1)*256]
    gl = proj[:, 6144:6160]
    waug = np.concatenate([w_gk2[:, h*256:(h+1)*256], b_gk2[None, h*256:(h+1)*256]], axis=0)
    return {"gqT": np.ascontiguousarray(q.T), "gkT": np.ascontiguousarray(k.T), "gktok": np.ascontiguousarray(k),
            "gvtok": np.ascontiguousarray(v), "glowT": np.ascontiguousarray(gl.T), "gwaug": np.ascontiguousarray(waug),
            "gcst": gla_consts()}
def gdn_consts():
    p = np.arange(64)[:, None]; f = np.arange(64)[None, :]
    I64 = (p == f); mLs = (p > f); mUi = (f >= p); mUs = (f > p)
    c64 = np.concatenate([I64, mLs, mUi, mUs], axis=1).astype(np.float32)
    rm = np.ones((128, 512), np.float32); rm[:, ::64] = 0.0
    return np.ascontiguousarray(c64), np.eye(128, dtype=np.float32), rm
def gdn_inputs(proj, conv_w, a_log, dt_bias, h):
    q0 = 1088
    x = np.concatenate([proj[:, q0+h*128:q0+(h+1)*128], proj[:, q0+1024+h*128:q0+1024+(h+1)*128],
                        proj[:, q0+2048+h*128:q0+2048+(h+1)*128]], axis=1)
    cw = np.zeros((128, 14), np.float32)
    for kind in range(3):
        cw[:, kind*4:(kind+1)*4] = conv_w[:, kind*1024+h*128:kind*1024+(h+1)*128].T
    cw[:, 12] = a_log[h]; cw[:, 13] = dt_bias[h]
    c64, ident, rm = gdn_consts()
    return {"dxT": np.ascontiguousarray(x.T), "dcw": cw, "dbl": np.ascontiguousarray(proj[:, 5184+h][None]),
            "dal": np.ascontiguousarray(proj[:, 5192+h][None]), "dc64": c64, "did": ident, "drm": rm}

_DBG = {}
_NC_CACHE = {}

def _get(key, fn):
    if key not in _NC_CACHE:
        _NC_CACHE[key] = fn()
    return _NC_CACHE[key]

def _run(nc, maps):
    return run_bass_kernel_spmd(nc, maps, core_ids=list(range(8))).results

def kernel(x, c, positions, norm_g, ada_w, ada_b, ab_w_in, mla_q_norm, mla_w_uq, mla_kv_norm, mla_w_ukv,
           gdn_conv_w, gdn_a_log, gdn_dt_bias, gdn_norm, ab_w_out, gla_w_in, gla_w_gk2, gla_b_gk2, gla_norm,
           gla_w_out, ffn_w1, ffn_w3, ffn_w2, final_norm):
    f32 = lambda a: np.asarray(a, np.float32)
    x = f32(x)[0]; c = f32(c)[0]; positions = np.asarray(positions)
    norm_g = f32(norm_g); NL = 4
    c_col = np.ascontiguousarray(c.reshape(16, 128).T)
    aw = f32(ada_w).reshape(8, 2048, 6144); ab = f32(ada_b).reshape(8, 6144)
    maps = []
    for j in range(8):
        maps.append({"c_col": c_col, "w": np.ascontiguousarray(aw[:, :, j*768:(j+1)*768]),
                     "b": np.ascontiguousarray(ab[:, j*768:(j+1)*768].reshape(8, 6, 128).transpose(2, 0, 1).reshape(128, 48))})
    res = _run(_get("ada", build_ada), maps)
    mods = np.zeros((8, 6144), np.float32)
    for j in range(8):
        mods[:, j*768:(j+1)*768] = res[j]["o"].reshape(128, 8, 6).transpose(1, 2, 0).reshape(8, 768)
    del aw, maps
    SH, SC, GT = slice(0, 2048), slice(2048, 4096), slice(4096, 6144)
    def w_in_for(l):
        return w_in_even(f32(ab_w_in[l // 2])) if l % 2 == 0 else np.ascontiguousarray(f32(gla_w_in[l // 2]))
    xT = [np.ascontiguousarray(x[j*1024:(j+1)*1024].T) for j in range(8)]
    cols = make_cols(pre=(norm_g[0, 0], mods[0][SH], mods[0][SC]))
    w_in = w_in_for(0)
    res = _run(_get(("t", None, 5264, False), lambda: build_t(None, True, 5264, False)),
               [{"xT": xT[j], "cols": cols, "w_in": w_in} for j in range(8)])
    proj = np.concatenate([res[j]["projT"].T for j in range(8)], axis=0)
    for l in range(NL):
        i = l // 2
        if l % 2 == 0:
            res = _run(_get("mla", build_mla), [mla_inputs(proj, positions, f32(mla_q_norm[i]), f32(mla_kv_norm[i]),
                                                             f32(mla_w_uq[i]), f32(mla_w_ukv[i]), h) for h in range(8)])
            o_a = np.concatenate([res[h]["oaT"].T for h in range(8)], axis=1)
            res = _run(_get("gdn", build_gdn), [gdn_inputs(proj, f32(gdn_conv_w[i]), f32(gdn_a_log[i]), f32(gdn_dt_bias[i]), h)
                                                 for h in range(8)])
            o_b = np.concatenate([res[h]["dobT"].T for h in range(8)], axis=1)
            o = np.concatenate([o_a, o_b], axis=1)
            gsrc = proj[:, 4160:5184]
            mixnorm = f32(gdn_norm[i]); w_out = f32(ab_w_out[i]); kind = "ab"
        else:
            res = _run(_get("gla", build_gla), [gla_inputs(proj, f32(gla_w_gk2[i]), f32(gla_b_gk2[i]), cc) for cc in range(8)])
            o = np.concatenate([res[cc]["go"] for cc in range(8)], axis=1)
            gsrc = proj[:, 4096:6144]
            mixnorm = f32(gla_norm[i]); w_out = f32(gla_w_out[i]); kind = "gla"
        _DBG[f"o{l}"] = o
        last = (l == NL - 1)
        if last:
            pre = (f32(final_norm), None, None); f_next = 0
        else:
            pre = (norm_g[l+1, 0], mods[2*(l+1)][SH], mods[2*(l+1)][SC]); f_next = 5264 if (l+1) % 2 == 0 else 6160
        cols = make_cols(gate_mix=mods[2*l][GT], ffn=(norm_g[l, 1], mods[2*l+1][SH], mods[2*l+1][SC], mods[2*l+1][GT]),
                         pre=pre, mixnorm=mixnorm)
        w1 = np.ascontiguousarray(f32(ffn_w1[l])); w3 = np.ascontiguousarray(f32(ffn_w3[l])); w2 = np.ascontiguousarray(f32(ffn_w2[l]))
        w_out = np.ascontiguousarray(w_out)
        maps = []
        w_in = None if last else w_in_for(l + 1)
        for j in range(8):
            tk = slice(j*1024, (j+1)*1024)
            m = {"xT": xT[j], "cols": cols, "oT": np.ascontiguousarray(o[tk].T), "gT": np.ascontiguousarray(gsrc[tk].T),
                 "w_out": w_out, "w1": w1, "w3": w3, "w2": w2}
            if not last:
                m["w_in"] = w_in
            maps.append(m)
        key = ("t", kind, f_next, last)
        res = _run(_get(key, (lambda kind=kind, f_next=f_next, last=last: build_t(kind, not last, f_next, last))), maps)
        xT = [res[j]["xT_out"] for j in range(8)]
        if not last:
            proj = np.concatenate([res[j]["projT"].T for j in range(8)], axis=0)
        _DBG[f"x{l}"] = xT
    out = np.concatenate([xT[j].T for j in range(8)], axis=0)
    return np.ascontiguousarray(out[None]).astype(np.float32)
```

```python
import numpy as np
import concourse.bass as bass
import concourse.mybir as mybir
from concourse.bass_utils import run_bass_kernel_spmd
from contextlib import ExitStack


F32 = mybir.dt.float32
BF16 = mybir.dt.bfloat16
I32 = mybir.dt.int32
AF = mybir.ActivationFunctionType
ALU = mybir.AluOpType
AX = mybir.AxisListType


class Buf:
    __slots__ = ("name", "last_w", "readers")

    def __init__(self, name=""):
        self.name = name
        self.last_w = None
        self.readers = []


class V:
    __slots__ = ("ap", "buf")

    def __init__(self, ap, buf):
        self.ap = ap
        self.buf = buf

    def __getitem__(self, k):
        return V(self.ap[k], self.buf)

    def rearrange(self, s, **kw):
        return V(self.ap.rearrange(s, **kw), self.buf)

    def bitcast(self, dt):
        return V(self.ap.bitcast(dt), self.buf)

    def bcast(self, shape):
        return V(self.ap.broadcast_to(shape), self.buf)

    def sub(self, k, name=""):
        return V(self.ap[k], Buf(name))

    @property
    def shape(self):
        return self.ap.shape


class Op:
    __slots__ = ("eng", "fn", "deps", "signals", "count", "is_dma", "lane", "idx")


ENGS = ("pe", "act", "dve", "pool", "sp")


def _ap(x):
    return x.ap if isinstance(x, V) else x


class Prog:
    N_LANES = 6
    SAME_ENG_WINDOW = 6

    def __init__(self, nc):
        self.nc = nc
        self.ops = {e: [] for e in ENGS}
        self.stack = ExitStack()
        self.nbytes = 0

    def sbuf(self, name, shape, dtype):
        t = self.stack.enter_context(self.nc.sbuf_tensor(name, list(shape), dtype))
        return V(t[:], Buf(name))

    def psum(self, name, shape, dtype=F32):
        t = self.stack.enter_context(self.nc.psum_tensor(name, list(shape), dtype))
        return V(t[:], Buf(name))

    def add(self, eng, fn, reads=(), writes=(), is_dma=False):
        op = Op()
        op.eng = eng
        op.fn = fn
        op.signals = False
        op.count = None
        op.is_dma = is_dma
        op.lane = None
        lst = self.ops[eng]
        op.idx = len(lst)
        deps = []
        for r in reads:
            b = r.buf if isinstance(r, V) else r
            if b is None:
                continue
            if b.last_w is not None:
                deps.append(b.last_w)
        for w in writes:
            b = w.buf if isinstance(w, V) else w
            if b is None:
                continue
            if b.last_w is not None:
                deps.append(b.last_w)
            deps.extend(b.readers)
        fd = []
        seen = set()
        for d in deps:
            if d is op or id(d) in seen:
                continue
            seen.add(id(d))
            if (not d.is_dma) and d.eng == eng and not is_dma and (op.idx - d.idx) > self.SAME_ENG_WINDOW:
                continue
            if (not d.is_dma) and (not is_dma) and d.eng == eng == "pe":
                continue
            if (not d.is_dma) and d.eng == eng and is_dma:
                pass
            fd.append(d)
            d.signals = True
        op.deps = fd
        for r in reads:
            b = r.buf if isinstance(r, V) else r
            if b is None:
                continue
            b.readers = [x for x in b.readers if not (x.eng == eng and not x.is_dma and not is_dma)] + [op]
        for w in writes:
            b = w.buf if isinstance(w, V) else w
            if b is None:
                continue
            b.last_w = op
            b.readers = []
        lst.append(op)
        return op

    F32R = False

    def matmul(self, out, lhsT, rhs, start=True, stop=True, **kw):
        if self.F32R and lhsT.ap.dtype == F32 and rhs.ap.dtype == F32:
            lhsT = lhsT.bitcast(mybir.dt.float32r); rhs = rhs.bitcast(mybir.dt.float32r)
        reads = [lhsT, rhs] + ([] if start else [])
        return self.add("pe", lambda e: e.matmul(out.ap, lhsT.ap, rhs.ap, start=start, stop=stop, **kw),
                        reads=reads, writes=[out])

    def transpose(self, out, in_, ident):
        return self.add("pe", lambda e: e.transpose(out.ap, in_.ap, ident.ap), reads=[in_, ident], writes=[out])

    def act(self, out, in_, func, bias=None, scale=None, accum_out=None, eng="act"):
        reads = [in_]
        kw = {}
        if bias is not None:
            kw["bias"] = _ap(bias)
            if isinstance(bias, V):
                reads.append(bias)
        if scale is not None:
            kw["scale"] = _ap(scale)
            if isinstance(scale, V):
                reads.append(scale)
        writes = [out]
        if accum_out is not None:
            kw["accum_out"] = accum_out.ap
            writes.append(accum_out)
        return self.add(eng, lambda e: e.activation(out.ap, in_.ap, func, **kw), reads=reads, writes=writes)

    def tt(self, out, in0, in1, op, eng="dve"):
        return self.add(eng, lambda e: e.tensor_tensor(out.ap, in0.ap, in1.ap, op), reads=[in0, in1], writes=[out])

    def ts(self, out, in0, s1, s2, op0, op1=None, eng="dve", accum_out=None):
        reads = [in0] + [s for s in (s1, s2) if isinstance(s, V)]
        writes = [out]
        kw = {}
        if op1 is not None:
            kw["op1"] = op1
        if accum_out is not None:
            kw["accum_out"] = accum_out.ap
            writes.append(accum_out)
        return self.add(eng, lambda e: e.tensor_scalar(out.ap, in0.ap, _ap(s1), _ap(s2), op0, **kw),
                        reads=reads, writes=writes)

    def stt(self, out, in0, scalar, in1, op0, op1, eng="dve"):
        reads = [in0, in1] + ([scalar] if isinstance(scalar, V) else [])
        return self.add(eng, lambda e: e.scalar_tensor_tensor(out.ap, in0.ap, _ap(scalar), in1.ap, op0, op1),
                        reads=reads, writes=[out])

    def copy(self, out, in_, eng="dve"):
        if eng == "act":
            return self.add("act", lambda e: e.copy(out.ap, in_.ap), reads=[in_], writes=[out])
        return self.add(eng, lambda e: e.tensor_copy(out.ap, in_.ap), reads=[in_], writes=[out])

    def memset(self, out, val, eng="pool"):
        return self.add(eng, lambda e: e.memset(out.ap, val), writes=[out])

    def recip(self, out, in_):
        return self.add("dve", lambda e: e.reciprocal(out.ap, in_.ap), reads=[in_], writes=[out])

    def reduce(self, out, in_, op, axis=AX.X, eng="dve"):
        return self.add(eng, lambda e: e.tensor_reduce(out.ap, in_.ap, axis, op), reads=[in_], writes=[out])

    def dma(self, out, in_, q="sp", **kw):
        reads = [in_] if isinstance(in_, V) else []
        writes = [out] if isinstance(out, V) else []
        return self.add(q, lambda e: e.dma_start(out=_ap(out), in_=_ap(in_), **kw), reads=reads, writes=writes,
                        is_dma=True)

    def emit(self, final_wait_ops=()):
        nc = self.nc
        st = self.stack
        sem = {e: st.enter_context(nc.semaphore("s_" + e)) for e in ENGS}
        lanes = {e: [st.enter_context(nc.semaphore(f"l_{e}{i}")) for i in range(self.N_LANES)] for e in ENGS}
        for e in ENGS:
            c = 0
            lc = [0] * self.N_LANES
            nd = 0
            for op in self.ops[e]:
                if op.is_dma:
                    op.lane = nd % self.N_LANES
                    nd += 1
                    lc[op.lane] += 16
                    op.count = lc[op.lane]
                    op.signals = True
                elif op.signals:
                    c += 1
                    op.count = c
        for op in final_wait_ops:
            assert op.is_dma
        self.stats = {e: len(self.ops[e]) for e in ENGS}

        def run(e, engobj):
            waited = {}
            for op in self.ops[e]:
                need = {}
                for d in op.deps:
                    s = lanes[d.eng][d.lane] if d.is_dma else sem[d.eng]
                    key = id(s)
                    if need.get(key, (None, 0))[1] < d.count:
                        need[key] = (s, d.count)
                for key, (s, cnt) in need.items():
                    if waited.get(key, 0) >= cnt:
                        continue
                    engobj.wait_ge(s, cnt)
                    waited[key] = cnt
                ins = op.fn(engobj)
                if op.is_dma:
                    ins.then_inc(lanes[e][op.lane], 16)
                elif op.signals:
                    ins.then_inc(sem[e], 1)
            if e == "sp":
                for op in final_wait_ops:
                    engobj.wait_ge(lanes[op.eng][op.lane], op.count)

        with nc.Block() as block:
            @block.tensor
            def _(eng):
                run("pe", eng)

            @block.scalar
            def _(eng):
                run("act", eng)

            @block.vector
            def _(eng):
                run("dve", eng)

            @block.gpsimd
            def _(eng):
                run("pool", eng)

            @block.sync
            def _(eng):
                run("sp", eng)
        st.close()


D = 2048; KC = 16; TOK = 1024; NTT = 2; DFF = 5632; EPS = 1e-6

def new_nc():
    return bass.Bass("TRN2", target_bir_lowering=False)

class Ctx:
    pass

def setup_common(P, nc):
    C = Ctx()
    C.banks = [P.psum(f"pb{i}", [128, 512], F32) for i in range(8)]
    C.bi = 0
    C.ones = P.sbuf("ones_f", [128, 128], F32)
    P.memset(C.ones, 1.0)
    C.ones_b = P.sbuf("ones_bf", [128, 128], BF16)
    P.memset(C.ones_b, 1.0)
    C.sqb = [P.sbuf(f"sqb{i}", [128, 512], BF16) for i in range(2)]
    C.ev = 0
    return C

def bank(C):
    rot = getattr(C, "rot", None)
    if rot is not None:
        b = rot[C.bi % len(rot)]
    else:
        b = C.banks[C.bi % getattr(C, "nrot", 8)]
    C.bi += 1
    return b

def rms_rstd(P, C, src_chunks, nfeat, rstd_out, sqtmp, T=TOK):
    n = len(src_chunks)
    sqtmp = C.sqb
    for tt in range(T // 512):
        ps = bank(C)
        for k, s in enumerate(src_chunks):
            sq = sqtmp[k % len(sqtmp)]
            P.act(sq[:, 0:512], s[:, tt*512:(tt+1)*512], AF.Square)
            P.matmul(ps, C.ones_b, sq[:, 0:512], start=(k == 0), stop=(k == n-1))
        P.act(rstd_out[:, tt*512:(tt+1)*512], ps, AF.Sqrt, bias=EPS, scale=1.0/nfeat)
    P.recip(rstd_out, rstd_out)

def load_w(P, wbuf, w_dram, k0, kn, f0, fw):
    dst = wbuf[:, 0:kn*fw].rearrange("p (k f) -> p k f", f=fw)
    src = w_dram[k0*128:(k0+kn)*128, f0:f0+fw].rearrange("(k p) f -> p k f", p=128)
    P.dma(dst, src, q="pool")
    return dst

def linear(P, C, w_dram, k0, kn, f_lo, f_hi, act, wbufs, consume, T=TOK, BW=256):
    bidx = getattr(C, "wrot", 0)
    for fb in range(f_lo, f_hi, BW):
        fw = min(BW, f_hi - fb)
        wb = load_w(P, wbufs[bidx % len(wbufs)], w_dram, k0, kn, fb, fw)
        bidx += 1
        for fc in range(0, fw, 128):
            fsz = min(128, fw - fc)
            for tt in range(T // 512):
                ps = bank(C)
                for k in range(kn):
                    P.matmul(ps[0:fsz, :], wb[:, k, fc:fc+fsz], act[:, k, tt*512:(tt+1)*512],
                             start=(k == 0), stop=(k == kn-1))
                consume(fb + fc, fsz, tt, ps[0:fsz, :])
    C.wrot = bidx

def build_t(has_post, has_pre, f_next, is_last):
    nc = new_nc()
    dt = lambda name, shape, kind="ExternalInput", d=F32: nc.dram_tensor(name, list(shape), d, kind=kind).ap()
    xT = dt("xT", [D, TOK])
    NCOL = 16 * 8 + 8
    cols_d = dt("cols", [128, NCOL])
    if has_post:
        oT = dt("oT", [D, TOK])
        gT = dt("gT", [1024 if has_post == "ab" else 2048, TOK])
        w_out = dt("w_out", [D, D])
        w1 = dt("w1", [D, DFF]); w3 = dt("w3", [D, DFF]); w2 = dt("w2", [DFF, D])
    if has_pre:
        w_in = dt("w_in", [D, f_next])
        projT = dt("projT", [f_next, TOK], kind="ExternalOutput")
    xT_out = dt("xT_out", [D, TOK], kind="ExternalOutput")

    P = Prog(nc)
    C = setup_common(P, nc)
    x = P.sbuf("x_sb", [128, KC, TOK], F32)
    xs = [x.sub((slice(None), k, slice(None)), f"x{k}") for k in range(KC)]
    hb = P.sbuf("hb", [128, KC, TOK], BF16)
    cols = P.sbuf("cols_sb", [128, NCOL], F32)
    P.dma(cols, cols_d)
    def load_x():
        for k in range(KC):
            P.dma(xs[k], xT[k*128:(k+1)*128, :], q="act")
    if not has_post:
        load_x()
    wA = [P.sbuf(f"wA{i}", [128, 4096], BF16) for i in range(3)]
    wB = [P.sbuf(f"wB{i}", [128, 4096], BF16) for i in range(3)]
    tmp = [P.sbuf(f"tmp{i}", [128, TOK], F32) for i in range(3)]
    rstd = P.sbuf("rstd", [128, TOK], F32)
    stg = [P.sbuf(f"stg{i}", [128, 512], F32) for i in range(3)]
    gmod = P.sbuf("gmod", [128, 16], F32)
    fin = []
    def prenorm(ng0, sh0, sc0):
        rms_rstd(P, C, xs, D, rstd, tmp[0:2])
        P.ts(gmod, cols[:, sc0:sc0+16], 1.0, None, ALU.add)
        P.tt(gmod, gmod, cols[:, ng0:ng0+16], ALU.mult)
        for k in range(KC):
            t = tmp[k % 2]
            P.stt(t, xs[k], gmod[:, k:k+1], rstd, ALU.mult, ALU.mult)
            P.act(hb[:, k, :], t, AF.Identity, bias=cols[:, sh0+k:sh0+k+1])

    def resid_consumer(gate0):
        def consume(f0, fsz, tt, ps):
            k = f0 // 128
            assert f0 % 128 == 0 and fsz == 128
            xv = xs[k][:, tt*512:(tt+1)*512]
            P.stt(xv, ps, cols[:, gate0+k:gate0+k+1], xv, ALU.mult, ALU.add)
        return consume

    if has_post:
        if has_post == "ab":
            for k in range(8):
                t = tmp[k % 2]
                P.dma(t, oT[k*128:(k+1)*128, :])
                P.copy(hb[:, k, :], t, eng="act" if k % 2 else "dve")
            groups = [[8 + k] for k in range(8)]
            ncol0 = 128
        else:
            groups = [[4*h + j for j in range(4)] for h in range(4)]
            ncol0 = 128
        ot = [P.sbuf(f"ot{i}", [128, TOK], F32) for i in range(4)]
        extra = []
        for i in range(3):
            f = wB[i].bitcast(F32)
            extra += [f.sub((slice(None), slice(0, 1024)), f"wBx{i}a"), f.sub((slice(None), slice(1024, 2048)), f"wBx{i}b")]
        obufs = ot + extra[0:4]
        rstds = [rstd, extra[4]]
        zts = [tmp[2], extra[5]]
        nz = 0
        for g, grp in enumerate(groups):
            srcs = []
            for j, k in enumerate(grp):
                ob = obufs[g % 8] if len(grp) == 1 else obufs[(g % 2) * 4 + j]
                P.dma(ob, oT[k*128:(k+1)*128, :])
                srcs.append(ob)
            rs_ = rstds[g % 2]
            rms_rstd(P, C, srcs, 128 * len(grp), rs_, tmp[0:2])
            for j, k in enumerate(grp):
                gk = k - 8 if has_post == "ab" else k
                zt = zts[nz % 2]; nz += 1
                P.dma(zt, gT[gk*128:(gk+1)*128, :])
                P.act(zt, zt, AF.Silu)
                t = tmp[j % 2]
                P.stt(t, srcs[j], cols[:, ncol0+j:ncol0+j+1], rs_, ALU.mult, ALU.mult)
                P.tt(hb[:, k, :], t, zt, ALU.mult)
        for i in range(3):
            P.add("pool", (lambda i: (lambda e: e.memset(wB[i].ap[:, 0:2], 0.0)))(i),
                  writes=[extra[2*i], extra[2*i+1], wB[i]])
        load_x()
        linear(P, C, w_out, 0, KC, 0, D, hb, wA, resid_consumer(0))
        prenorm(16, 32, 48)
        gblk = P.sbuf("gblk", [128, 11, TOK], BF16)
        for bi in range(4):
            c0 = bi * 11
            bidx = 0
            for fb in range(c0*128, (c0+11)*128, 256):
                fw = min(256, (c0+11)*128 - fb)
                wa = load_w(P, wA[bidx % 3], w1, 0, KC, fb, fw)
                wb = load_w(P, wB[bidx % 3], w3, 0, KC, fb, fw)
                bidx += 1
                for fc in range(0, fw, 128):
                    kk = (fb + fc) // 128 - c0
                    for tt in range(NTT):
                        pa = bank(C); pb = bank(C)
                        for k in range(KC):
                            P.matmul(pa, wa[:, k, fc:fc+128], hb[:, k, tt*512:(tt+1)*512], start=(k == 0), stop=(k == KC-1))
                        for k in range(KC):
                            P.matmul(pb, wb[:, k, fc:fc+128], hb[:, k, tt*512:(tt+1)*512], start=(k == 0), stop=(k == KC-1))
                        s = stg[C.ev % 3]; C.ev += 1
                        P.act(s, pa, AF.Silu)
                        P.tt(gblk[:, kk, tt*512:(tt+1)*512], s, pb, ALU.mult)
            linear(P, C, w2, c0, 11, 0, D, gblk, wA, resid_consumer(64))
    if has_pre:
        prenorm(80, 96, 112)
        def consume(f0, fsz, tt, ps):
            s = stg[C.ev % 3]
            if C.ev % 2:
                P.copy(s[0:fsz, :], ps, eng="act")
            else:
                P.copy(s[0:fsz, :], ps, eng="dve")
            C.ev += 1
            fin.append(P.dma(projT[f0:f0+fsz, tt*512:(tt+1)*512], s[0:fsz, :]))
        linear(P, C, w_in, 0, KC, 0, f_next, hb, wA, consume)
    if is_last:
        rms_rstd(P, C, xs, D, rstd, tmp[0:2])
        for k in range(KC):
            P.stt(xs[k], xs[k], cols[:, 80+k:80+k+1], rstd, ALU.mult, ALU.mult)
    for k in range(KC):
        fin.append(P.dma(xT_out[k*128:(k+1)*128, :], xs[k]))
    P.emit(final_wait_ops=fin)
    return nc

def build_ada():
    nc = new_nc()
    c_col = nc.dram_tensor("c_col", [128, 16], F32, kind="ExternalInput").ap()
    w = nc.dram_tensor("w", [8, D, 768], F32, kind="ExternalInput").ap()
    b = nc.dram_tensor("b", [128, 48], F32, kind="ExternalInput").ap()
    o = nc.dram_tensor("o", [128, 48], F32, kind="ExternalOutput").ap()
    P = Prog(nc)
    cc = P.sbuf("cc", [128, 16], F32)
    sc = P.sbuf("sc", [128, 16], F32)
    bb = P.sbuf("bb", [128, 48], F32)
    ob = P.sbuf("ob", [128, 48], F32)
    P.dma(cc, c_col); P.dma(bb, b)
    P.act(sc, cc, AF.Silu)
    wb = [P.sbuf(f"w{i}", [128, 16, 768], F32) for i in range(2)]
    ps = P.psum("ps", [128, 512], F32)
    for m in range(8):
        wt = wb[m % 2]
        for q in range(4):
            P.dma(wt[:, 4*q:4*q+4, :], w[m, q*512:(q+1)*512, :].rearrange("(k p) f -> p k f", p=128), q="sp" if q % 2 else "act")
        for fc in range(6):
            for k in range(16):
                P.matmul(ps[:, m*6+fc:m*6+fc+1], wt[:, k, fc*128:(fc+1)*128], sc[:, k:k+1], start=(k == 0), stop=(k == 15))
    P.tt(ob, ps[:, 0:48], bb, ALU.add)
    f = P.dma(o, ob)
    P.emit(final_wait_ops=[f])
    return nc


S = 8192; NT = 16; EPS = 1e-6

def mla_phase(P, C, nc, pfx=""):
    dt = lambda name, shape, kind="ExternalInput", d=F32: nc.dram_tensor(pfx + name, list(shape), d, kind=kind).ap()
    cqT = dt("cqT", [512, S]); ckvT = dt("ckvT", [512, S]); krT = dt("krT", [64, S]); krsT = dt("krsT", [64, S])
    ncols = dt("mla_cols", [128, 10])
    wq = dt("wq", [512, 256])
    wkv = dt("wkv", [512, 256])
    pos = dt("pos", [1, S], d=I32)
    masks_d = dt("masks", [128, 4, 512])
    oT = dt("oaT", [128, S], kind="ExternalOutput")
    fin = []
    SCALE = 1.0 / np.sqrt(192.0)

    cols = P.sbuf("mcols", [128, 10], F32); P.dma(cols, ncols)
    wq_b = P.sbuf("wq_b", [128, 4, 256], BF16); P.dma(wq_b, wq.rearrange("(k p) f -> p k f", p=128), q="pool")
    wkv_b = P.sbuf("wkv_b", [128, 4, 256], BF16); P.dma(wkv_b, wkv.rearrange("(k p) f -> p k f", p=128), q="pool")
    mk = P.sbuf("mk", [128, 4, 512], BF16); P.dma(mk, masks_d, q="pool")
    ones_b = P.sbuf("ones_b", [128, 128], BF16); P.memset(ones_b, 1.0)

    qa = P.sbuf("qa", [128, S], BF16); qb = P.sbuf("qb", [64, S], BF16)
    ka = P.sbuf("ka", [128, S], BF16); kb = P.sbuf("kb", [64, S], BF16)
    vt = P.sbuf("vt", [128, 64, 128], BF16)
    cos2 = P.sbuf("cos2", [64, S], BF16); sinpm = P.sbuf("sinpm", [64, S], BF16)
    tl = lambda t: slice(t*512, (t+1)*512)
    qa_t = [qa.sub((slice(None), tl(t))) for t in range(NT)]; qb_t = [qb.sub((slice(None), tl(t))) for t in range(NT)]
    ka_t = [ka.sub((slice(None), tl(t))) for t in range(NT)]; kb_t = [kb.sub((slice(None), tl(t))) for t in range(NT)]
    vt_t = [vt.sub((slice(None), slice(4*t, 4*t+4), slice(None))) for t in range(NT)]

    SEG = 512
    pi_ = P.sbuf("pos_i", [64, SEG], I32); u = P.sbuf("rp_u", [64, SEG], F32); kf = P.sbuf("rp_kf", [64, SEG], F32)
    ki = P.sbuf("rp_ki", [64, SEG], I32)
    angf = P.sbuf("rp_ang", [64, SEG], F32)
    TWO_PI = float(2*np.pi)
    for sg in range(S // SEG):
        sl = slice(sg*SEG, (sg+1)*SEG)
        P.dma(pi_, pos[:, sl].partition_broadcast(64))
        P.copy(kf, pi_)
        for which, off, dst in (("sin", 0.5, sinpm), ("cos", 0.75, cos2)):
            P.ts(u, kf, cols[0:64, 8:9], 1.0/TWO_PI, ALU.mult, ALU.mult)
            P.ts(u, u, off, None, ALU.add)
            P.copy(ki, u)
            ang = angf
            P.copy(ang, ki)
            P.tt(u, u, ang, ALU.subtract)
            P.ts(ang, u, 0.0, None, ALU.is_lt)
            P.tt(u, u, ang, ALU.add)
            P.act(dst[:, sl], u, AF.Sin, bias=-float(np.pi), scale=TWO_PI)
        P.ts(sinpm[:, sl], sinpm[:, sl], cols[0:64, 9:10], None, ALU.mult)

    cq = [P.sbuf(f"cq{i}", [128, 4, 512], F32) for i in range(2)]
    cn = [P.sbuf(f"cn{i}", [128, 4, 512], BF16) for i in range(2)]
    sqt = [P.sbuf(f"sqt{i}", [128, 512], F32) for i in range(1)]
    rsts = [P.sbuf(f"rst{i}", [128, 512], F32) for i in range(2)]
    kr = [P.sbuf(f"kr{i}", [64, 512], F32) for i in range(2)]
    krs = [P.sbuf(f"krs{i}", [64, 512], F32) for i in range(2)]
    r1s = [P.sbuf(f"r1_{i}", [64, 512], F32) for i in range(2)]; r2s = [P.sbuf(f"r2_{i}", [64, 512], F32) for i in range(2)]
    tmpns = [P.sbuf(f"tmpn{i}", [128, 512], F32) for i in range(2)]

    def load_norm(src, t, ncol0, i):
        P.dma(cq[i], src[:, tl(t)].rearrange("(k p) t -> p k t", p=128))
        rst = rsts[i]; tmpn = tmpns[i]
        rms_rstd(P, C, [cq[i][:, k, :] for k in range(4)], 512, rst, sqt, T=512)
        for k in range(4):
            P.stt(cn[i][:, k, :], cq[i][:, k, :], cols[:, ncol0+k:ncol0+k+1], rst, ALU.mult, ALU.mult)

    def rope(dst, a, b, t, side):
        r1 = r1s[side]; r2 = r2s[side]
        P.tt(r1, a, cos2[:, tl(t)], ALU.mult)
        P.tt(r2, b, sinpm[:, tl(t)], ALU.mult)
        P.tt(dst, r1, r2, ALU.add)

    for t in range(NT):
        load_norm(cqT, t, 0, 0)
        ps = bank(C)
        for k in range(4):
            P.matmul(ps, wq_b[:, k, 0:128], cn[0][:, k, :], start=(k == 0), stop=(k == 3))
        P.copy(qa_t[t], ps, eng="act")
        ps1 = bank(C)
        for k in range(4):
            P.matmul(ps1[0:64, :], wq_b[:, k, 128:192], cn[0][:, k, :], start=(k == 0), stop=(k == 3))
        ps2 = bank(C)
        for k in range(4):
            P.matmul(ps2[0:64, :], wq_b[:, k, 192:256], cn[0][:, k, :], start=(k == 0), stop=(k == 3))
        rope(qb_t[t], ps1[0:64, :], ps2[0:64, :], t, 0)
        load_norm(ckvT, t, 4, 1)
        ps = bank(C)
        for k in range(4):
            P.matmul(ps, wkv_b[:, k, 0:128], cn[1][:, k, :], start=(k == 0), stop=(k == 3))
        P.copy(ka_t[t], ps, eng="act")
        ps = bank(C)
        for blk in range(4):
            for k in range(4):
                P.matmul(ps[:, blk*128:(blk+1)*128], cn[1][:, k, blk*128:(blk+1)*128], wkv_b[:, k, 128:256],
                         start=(k == 0), stop=(k == 3))
        P.copy(vt_t[t], ps.rearrange("p (b d) -> p b d", b=4), eng="act")
        i = t % 2
        P.dma(kr[i], krT[:, tl(t)]); P.dma(krs[i], krsT[:, tl(t)])
        rope(kb_t[t], kr[i], krs[i], t, 1)

    sb = [C.banks[0], C.banks[1], C.banks[2], C.banks[3], C.banks[4]]
    oacc = [C.banks[5], C.banks[6]]; sbank = C.banks[7]
    pb = [P.sbuf(f"pexp{i}", [128, 512], BF16) for i in range(5)]
    rs = rsts[0]
    ost = [tmpns[0], tmpns[1]]
    pairs = [(j, b) for j in range(NT) for b in range(4*j + 4)]
    LA = 3
    def emit_qk(n):
        j, b = pairs[n]
        ps = sb[n % 5]
        kt, ko = b // 4, (b % 4) * 128
        P.matmul(ps, ka_t[kt][:, ko:ko+128], qa_t[j], start=True, stop=False)
        P.matmul(ps, kb_t[kt][:, ko:ko+128], qb_t[j], start=False, stop=True)
    def emit_rest(n):
        j, b = pairs[n]
        ps = sb[n % 5]; pe = pb[n % 5]
        kt = b // 4
        nkb = 4*j + 4
        oa = oacc[j % 2]
        P.act(pe, ps, AF.Exp, scale=SCALE)
        if b >= 4*j:
            P.tt(pe, pe, mk[:, b - 4*j, :], ALU.mult, eng="pool")
        P.matmul(oa, vt_t[kt][:, b % 4, :], pe, start=(b == 0), stop=(b == nkb-1))
        P.matmul(sbank, ones_b, pe, start=(b == 0), stop=(b == nkb-1))
        if b == nkb - 1:
            P.recip(rs, sbank)
            o = ost[j % 2]
            P.tt(o, oa, rs, ALU.mult)
            fin.append(P.dma(oT[:, tl(j)], o))
    for n in range(len(pairs) + LA):
        if n < len(pairs):
            emit_qk(n)
        if n - LA >= 0:
            emit_rest(n - LA)
    return fin

def build_mla():
    nc = new_nc()
    P = Prog(nc)
    C = setup_common(P, nc)
    fin = mla_phase(P, C, nc)
    P.emit(final_wait_ops=fin)
    return nc


S = 8192; NCH = 128; CH = 64; TILE = 512; CPT = 8

def gla_phase(P, C, nc, pfx=""):
    dt = lambda name, shape, kind="ExternalInput", d=F32: nc.dram_tensor(pfx + name, list(shape), d, kind=kind).ap()
    qT = dt("gqT", [256, S]); kT = dt("gkT", [256, S])
    ktok = dt("gktok", [S, 256]); vtok = dt("gvtok", [S, 256])
    glow = dt("glowT", [16, S]); waug = dt("gwaug", [17, 256])
    cst = dt("gcst", [64, 192])
    o_d = dt("go", [S, 256], kind="ExternalOutput")
    fin = []
    cs = P.sbuf("gcst_sb", [64, 192], F32); P.dma(cs, cst)
    triS = cs[:, 0:64]; triR = cs[:, 64:128]; maskT = cs[:, 128:192]
    wa = P.sbuf("gwa", [17, 256], F32); P.dma(wa, waug)
    st = P.sbuf("gstate", [128, 2, 256], F32); P.memset(st, 0.0)
    stb = P.sbuf("gstate_b", [128, 2, 256], BF16); P.memset(stb, 0.0)
    NB = 3
    q_in = [P.sbuf(f"gq_in{i}", [128, 2, TILE], F32) for i in range(NB)]
    k_in = [P.sbuf(f"gk_in{i}", [128, 2, TILE], F32) for i in range(NB)]
    kt_in = [P.sbuf(f"gkt_in{i}", [64, CPT, 256], F32) for i in range(NB)]
    v_in = [P.sbuf(f"gv_in{i}", [64, CPT, 256], BF16) for i in range(NB)]
    gl_in = [P.sbuf(f"ggl_in{i}", [17, TILE], F32) for i in range(NB)]
    for g in gl_in:
        P.memset(g, 1.0)
    o_st = [P.sbuf(f"go_st{i}", [64, CPT, 256], F32) for i in range(NB)]
    R = 4
    e1 = [P.sbuf(f"ge1{i}", [64, 256], F32) for i in range(R)]
    lnv = [P.sbuf(f"glnv{i}", [64, 256], F32) for i in range(R)]
    E = [P.sbuf(f"gE{i}", [128, 128], F32) for i in range(R+1)]
    Ei = [P.sbuf(f"gEi{i}", [128, 128], F32) for i in range(R)]
    Er = [P.sbuf(f"gEr{i}", [64, 256], F32) for i in range(R)]
    qt = [P.sbuf(f"gqt{i}", [128, 2, 64], BF16) for i in range(R)]
    kt = [P.sbuf(f"gkt{i}", [128, 2, 64], BF16) for i in range(R)]
    kd = [P.sbuf(f"gkd{i}", [64, 256], BF16) for i in range(R)]
    atm = [P.sbuf(f"gatm{i}", [64, 64], BF16) for i in range(R)]
    NTL = S // TILE
    def load(t):
        i = t % NB
        sl = slice(t*TILE, (t+1)*TILE)
        P.dma(q_in[i], qT[:, sl].rearrange("(k p) t -> p k t", p=128))
        P.dma(k_in[i], kT[:, sl].rearrange("(k p) t -> p k t", p=128))
        P.dma(kt_in[i], ktok[sl, :].rearrange("(c p) d -> p c d", p=64))
        P.dma(v_in[i], vtok[sl, :].rearrange("(c p) d -> p c d", p=64), q="pool")
        P.dma(gl_in[i][0:16, :], glow[:, sl])
    def prep(c):
        t, cc = c // CPT, c % CPT
        if cc == 0:
            load(t)
        i = t % NB
        r = c % R
        cl = slice(cc*64, (cc+1)*64)
        ps = bank(C)
        P.matmul(ps[0:64, 0:256], gl_in[i][:, cl], wa)
        P.act(e1[r], ps[0:64, 0:256], AF.Exp, scale=-1.0)
        P.act(lnv[r], e1[r], AF.Ln, bias=1.0)
        ps2 = bank(C)
        for dc in range(2):
            P.matmul(ps2[:, dc*64:(dc+1)*64], lnv[r][:, dc*128:(dc+1)*128], triS)
        ps3 = bank(C)
        P.matmul(ps3[0:64, 0:256], triR, lnv[r])
        Ec = E[c % (R+1)]
        P.act(Ec, ps2[:, 0:128], AF.Exp)
        P.act(Ei[r], ps2[:, 0:128], AF.Exp, scale=-1.0)
        P.act(Er[r], ps3[0:64, 0:256], AF.Exp)
        P.stt(qt[r], q_in[i][:, :, cl], 1.0/16.0, Ec.rearrange("p (k j) -> p k j", k=2), ALU.mult, ALU.mult)
        P.tt(kt[r], k_in[i][:, :, cl], Ei[r].rearrange("p (k j) -> p k j", k=2), ALU.mult)
        P.tt(kd[r], kt_in[i][:, cc, :], Er[r], ALU.mult)
        ps4 = bank(C)
        for dc in range(2):
            P.matmul(ps4[0:64, 0:64], kt[r][:, dc, :], qt[r][:, dc, :], start=(dc == 0), stop=(dc == 1))
        P.tt(atm[r], ps4[0:64, 0:64], maskT, ALU.mult)
    def scan(c):
        t, cc = c // CPT, c % CPT
        i = t % NB
        r = c % R
        Ec = E[c % (R+1)]
        ps5 = bank(C)
        for dc in range(2):
            P.matmul(ps5[0:64, 0:256], qt[r][:, dc, :], stb[:, dc, :], start=(dc == 0), stop=False)
        P.matmul(ps5[0:64, 0:256], atm[r], v_in[i][:, cc, :], start=False, stop=True)
        P.copy(o_st[i][:, cc, :], ps5[0:64, 0:256], eng="act")
        ps6 = bank(C)
        for dc in range(2):
            P.matmul(ps6[:, dc*256:(dc+1)*256], kd[r][:, dc*128:(dc+1)*128], v_in[i][:, cc, :])
        for dc in range(2):
            P.stt(st[:, dc, :], st[:, dc, :], Ec[:, dc*64+63:dc*64+64], ps6[:, dc*256:(dc+1)*256], ALU.mult, ALU.add)
        P.copy(stb, st, eng="act")
        if cc == CPT - 1:
            sl = slice(t*TILE, (t+1)*TILE)
            fin.append(P.dma(o_d[sl, :].rearrange("(c p) e -> p c e", p=64), o_st[i]))
    NC_ = NTL * CPT
    LA = 2
    for c in range(NC_ + LA):
        if c < NC_:
            prep(c)
        if c - LA >= 0:
            scan(c - LA)
    return fin

def build_gla():
    nc = new_nc()
    P = Prog(nc)
    C = setup_common(P, nc)
    fin = gla_phase(P, C, nc)
    P.emit(final_wait_ops=fin)
    return nc


DEBUG = False
S = 8192; TILE = 512; CPT = 8; EPS = 1e-6

def gdn_phase(P, C, nc, pfx=""):
    dt = lambda name, shape, kind="ExternalInput", d=F32: nc.dram_tensor(pfx + name, list(shape), d, kind=kind).ap()
    xT = dt("dxT", [384, S]); cw_d = dt("dcw", [128, 14]); bl_d = dt("dbl", [1, S]); al_d = dt("dal", [1, S])
    c64_d = dt("dc64", [64, 256]); id_d = dt("did", [128, 128]); rm_d = dt("drm", [128, 512])
    o_d = dt("dobT", [128, S], kind="ExternalOutput")
    fin = []
    C.nrot = 7
    cw = P.sbuf("dcw_sb", [128, 14], F32); P.dma(cw, cw_d)
    c64 = P.sbuf("dc64_sb", [64, 256], F32); P.dma(c64, c64_d)
    I64 = c64[:, 0:64]; mLs = c64[:, 64:128]; mUi = c64[:, 128:192]; mUs = c64[:, 192:256]
    ident = P.sbuf("did_sb", [128, 128], F32); P.dma(ident, id_d)
    rmask = P.sbuf("drm_sb", [128, 512], F32); P.dma(rmask, rm_d)
    negA = P.sbuf("dnegA", [128, 1], F32)
    P.act(negA, cw[:, 12:13], AF.Exp)
    Ss = [P.sbuf(f"dS{i}", [128, 128], F32) for i in range(3)]
    P.memset(Ss[0], 0.0)
    T3 = lambda v: v.rearrange("p (c j) -> p c j", j=64)
    def sb(name, shape, d=F32, n=2):
        return [P.sbuf(f"{name}{i}", shape, d) for i in range(n)]
    xin = [sb("dxq", [128, TILE+3]), sb("dxk", [128, TILE+3]), sb("dxv", [128, TILE+3])]
    y = sb("dy", [128, TILE], n=3); sq = sb("dsq", [128, TILE], n=2); rstd = sb("drstd", [128, TILE], n=1)[0]
    qn = sb("dqn", [128, TILE]); kn = sb("dkn", [128, TILE]); vv = sb("dvv", [128, TILE])
    bl = sb("dblb", [128, TILE]); al = sb("dalb", [128, TILE])
    beta = sb("dbeta", [128, TILE]); g = sb("dg", [128, TILE], n=1)[0]; gc = sb("dgc", [128, TILE]); egc = sb("degc", [128, TILE])
    ekd = sb("dekd", [128, TILE]); egl = sb("degl", [128, CPT])
    knb = sb("dknb", [128, TILE], BF16); kbb = sb("dkbb", [128, TILE], BF16); qnb = sb("dqnb", [128, TILE], BF16)
    kb = sb("dkb", [128, TILE], n=1)[0]; kbg = sb("dkbg", [128, TILE], BF16, n=1)[0]; kdc = sb("dkdc", [128, TILE], BF16, n=1)[0]
    vb = sb("dvb", [128, TILE], BF16, n=1)[0]
    kbg_t = sb("dkbgt", [64, CPT, 128], BF16); vb_t = sb("dvbt", [64, CPT, 128], BF16); kdc_t = sb("dkdct", [64, CPT, 128], BF16)
    identb = P.sbuf("didb", [128, 128], BF16); P.copy(identb, ident)
    Sbs = [P.sbuf(f"dSb{i}", [128, 128], BF16) for i in range(3)]
    P.memset(Sbs[0], 0.0)
    gtok = sb("dgtok", [64, CPT], n=1)[0]; tmp64 = sb("dtmp64", [64, TILE], n=2)
    D1 = sb("dD1", [64, TILE], n=1)[0]
    dec = sb("ddec", [64, TILE], n=1)[0]; decT = sb("ddecT", [64, TILE], n=1)[0]
    Ym = sb("dY", [64, TILE]); Pn = sb("dPn", [64, TILE], BF16); PTn = sb("dPTn", [64, TILE], BF16); Yb = sb("dYb", [64, TILE], BF16)
    intraT = sb("dintraT", [64, TILE], BF16)
    u_t = sb("du", [64, CPT, 128], BF16); w_t = sb("dwt", [64, CPT, 128], BF16, n=1)[0]
    MT = sb("dMT", [128, CPT, 128]); Bc = sb("dBc", [128, CPT, 128]); qpT = sb("dqpT", [128, TILE], BF16)
    qdf = sb("dqdf", [128, TILE], n=1)[0]
    ost = sb("dost", [128, TILE])

    NT = S // TILE
    def prepass(t):
        i = t % 2
        sl = slice(t*TILE, (t+1)*TILE)
        for kind in range(3):
            xi = xin[kind][i]
            if t == 0:
                P.memset(xi[:, 0:3], 0.0, eng="dve")
                P.dma(xi[:, 3:TILE+3], xT[kind*128:(kind+1)*128, 0:TILE])
            else:
                P.dma(xi, xT[kind*128:(kind+1)*128, t*TILE-3:(t+1)*TILE])
            yy = y[kind]
            P.ts(yy, xi[:, 0:TILE], cw[:, kind*4:kind*4+1], None, ALU.mult)
            for j in range(1, 4):
                P.stt(yy, xi[:, j:j+TILE], cw[:, kind*4+j:kind*4+j+1], yy, ALU.mult, ALU.add)
            if kind == 2:
                P.act(vv[i], yy, AF.Silu)
            else:
                P.act(yy, yy, AF.Silu)
                P.act(C.sqb[kind], yy, AF.Square)
                ps = bank(C)
                P.matmul(ps, C.ones_b, C.sqb[kind])
                P.act(rstd, ps, AF.Sqrt, bias=EPS)
                P.recip(rstd, rstd)
                if kind == 0:
                    P.stt(qn[i], yy, float(1.0/np.sqrt(128.0)), rstd, ALU.mult, ALU.mult)
                else:
                    P.tt(kn[i], yy, rstd, ALU.mult)
            yield
        P.dma(bl[i], bl_d[:, sl].partition_broadcast(128))
        P.dma(al[i], al_d[:, sl].partition_broadcast(128))
        P.act(beta[i], bl[i], AF.Sigmoid)
        P.act(g, al[i], AF.Exp, bias=cw[:, 13:14])
        P.act(g, g, AF.Ln, bias=1.0)
        P.ts(g, g, negA, -1.0, ALU.mult, ALU.mult)
        P.add("dve", (lambda o, m, gg: (lambda e: e.tensor_tensor_scan(o.ap, m.ap, gg.ap, 0.0, ALU.mult, ALU.add)))(gc[i], rmask, g),
              reads=[rmask, g], writes=[gc[i]])
        P.act(egc[i], gc[i], AF.Exp)
        gl3 = T3(gc[i])[:, :, 63:64]
        P.tt(T3(ekd[i]), gl3.bcast([128, CPT, 64]), T3(gc[i]), ALU.subtract)
        P.act(ekd[i], ekd[i], AF.Exp)
        P.act(egl[i], T3(gc[i])[:, :, 63], AF.Exp)
        yield
        P.copy(knb[i], kn[i], eng="act")
        P.copy(qnb[i], qn[i], eng="act")
        P.tt(kb, kn[i], beta[i], ALU.mult)
        P.copy(kbb[i], kb, eng="act")
        P.tt(kbg, kb, egc[i], ALU.mult)
        P.tt(kdc, kn[i], ekd[i], ALU.mult)
        P.tt(qdf, qn[i], egc[i], ALU.mult)
        P.tt(vb, vv[i], beta[i], ALU.mult)
        yield
        for src, dst in ((kbg, kbg_t[i]), (vb, vb_t[i]), (kdc, kdc_t[i])):
            for half in range(2):
                ps = bank(C).bitcast(BF16)
                for cq in range(4):
                    cc = half*4 + cq
                    P.transpose(ps[0:64, cq*128:(cq+1)*128], src[:, cc*64:(cc+1)*64], identb)
                P.copy(dst[:, half*4:half*4+4, :], ps[0:64, 0:512].rearrange("p (c d) -> p c d", c=4), eng="act" if half else "dve")
                yield
        P.tt(T3(tmp64[0]), T3(gc[i][0:64, :]), I64.rearrange("p (o j) -> p o j", o=1).bcast([64, CPT, 64]), ALU.mult)
        P.reduce(gtok, T3(tmp64[0]), ALU.add)
        P.tt(T3(D1), T3(gc[i][0:64, :]), gtok.rearrange("p (c o) -> p c o", o=1).bcast([64, CPT, 64]), ALU.subtract)
        P.ts(tmp64[0], D1, 0.0, None, ALU.max)
        P.act(dec, tmp64[0], AF.Exp, scale=-1.0)
        P.ts(tmp64[1], D1, 0.0, None, ALU.min)
        P.act(decT, tmp64[1], AF.Exp)
        yield
        psL = bank(C); psLT = bank(C); psQ = bank(C)
        for cc in range(CPT):
            cl = slice(cc*64, (cc+1)*64)
            P.matmul(psL[0:64, cl], kbb[i][:, cl], knb[i][:, cl])
            P.matmul(psLT[0:64, cl], knb[i][:, cl], kbb[i][:, cl])
            P.matmul(psQ[0:64, cl], knb[i][:, cl], qnb[i][:, cl])
        bc64 = lambda m: m.rearrange("p (o j) -> p o j", o=1).bcast([64, CPT, 64])
        P.tt(T3(tmp64[0]), T3(dec), bc64(mLs), ALU.mult)
        P.stt(Pn[0], psL[0:64, :], -1.0, tmp64[0], ALU.mult, ALU.mult)
        P.tt(T3(tmp64[1]), T3(decT), bc64(mUs), ALU.mult)
        P.stt(PTn[0], psLT[0:64, :], -1.0, tmp64[1], ALU.mult, ALU.mult)
        P.tt(T3(tmp64[1]), T3(decT), bc64(mUi), ALU.mult)
        P.tt(intraT[i], psQ[0:64, :], tmp64[1], ALU.mult)
        Y = Ym[0]
        P.tt(T3(Y), T3(PTn[0]), bc64(I64), ALU.add)
        P.copy(Yb[0], Y, eng="act")
        ybi = 0
        yield
        cur = 0
        for n in range(1, 6):
            nxt = 1 - cur
            psP = bank(C); psPT = bank(C)
            for cc in range(CPT):
                cl = slice(cc*64, (cc+1)*64)
                P.matmul(psP[0:64, cl], PTn[cur][:, cl], Pn[cur][:, cl])
                if n < 5:
                    P.matmul(psPT[0:64, cl], Pn[cur][:, cl], PTn[cur][:, cl])
            P.copy(Pn[nxt], psP[0:64, :], eng="act")
            if n < 5:
                P.copy(PTn[nxt], psPT[0:64, :], eng="dve")
            yield
            psY = bank(C)
            for cc in range(CPT):
                cl = slice(cc*64, (cc+1)*64)
                P.matmul(psY[0:64, cl], Pn[nxt][:, cl], Yb[ybi][:, cl])
            Y2 = Ym[1] if Y is Ym[0] else Ym[0]
            P.tt(Y2, Y, psY[0:64, :], ALU.add)
            Y = Y2
            ybi = 1 - ybi
            P.copy(Yb[ybi], Y, eng="act")
            yield
            cur = nxt
        for half in range(2):
            psu = bank(C); psw = bank(C)
            for cq in range(4):
                cc = half*4 + cq
                cl = slice(cc*64, (cc+1)*64)
                P.matmul(psu[0:64, cq*128:(cq+1)*128], Yb[ybi][:, cl], vb_t[i][:, cc, :])
                P.matmul(psw[0:64, cq*128:(cq+1)*128], Yb[ybi][:, cl], kbg_t[i][:, cc, :])
            P.copy(u_t[i][:, half*4:half*4+4, :], psu[0:64, :].rearrange("p (c d) -> p c d", c=4), eng="act")
            P.copy(w_t[:, half*4:half*4+4, :], psw[0:64, :].rearrange("p (c d) -> p c d", c=4), eng="dve")
            yield
        psq = bank(C)
        for cc in range(CPT):
            cl = slice(cc*64, (cc+1)*64)
            P.matmul(psq[:, cl], w_t[:, cc, :], intraT[i][:, cl])
        P.tt(qpT[i], qdf, psq, ALU.subtract)
        yield
        for half in range(2):
            psm = bank(C); psb = bank(C)
            for cq in range(4):
                cc = half*4 + cq
                P.matmul(psm[:, cq*128:(cq+1)*128], w_t[:, cc, :], kdc_t[i][:, cc, :])
                P.matmul(psb[:, cq*128:(cq+1)*128], kdc_t[i][:, cc, :], u_t[i][:, cc, :])
            for cq in range(4):
                cc = half*4 + cq
                P.stt(MT[i][:, cc, :], ident, egl[i][:, cc:cc+1], psm[:, cq*128:(cq+1)*128], ALU.mult, ALU.subtract)
            P.copy(Bc[i][:, half*4:half*4+4, :], psb.rearrange("p (c d) -> p c d", c=4), eng="act")
            yield

    def scan_tile(t, gen):
        i = t % 2
        sl = slice(t*TILE, (t+1)*TILE)
        C.nrot = 7
        pso = C.banks[7]
        for cc in range(CPT):
            c = t*CPT + cc
            cl = slice(cc*64, (cc+1)*64)
            S_cur = Ss[c % 3]; S_nxt = Ss[(c + 1) % 3]
            p1 = bank(C)
            P.matmul(p1[:, 0:128], MT[i][:, cc, :], S_cur)
            P.tt(S_nxt, p1[:, 0:128], Bc[i][:, cc, :], ALU.add)
            P.copy(Sbs[(c + 1) % 3], S_nxt, eng="act")
            P.matmul(pso[:, cl], Sbs[c % 3], qpT[i][:, cl], start=True, stop=False)
            P.matmul(pso[:, cl], u_t[i][:, cc, :], intraT[i][:, cl], start=False, stop=True)
            if gen is not None:
                for _ in range(5):
                    next(gen, None)
        P.copy(ost[i], pso, eng="act")
        fin.append(P.dma(o_d[:, sl], ost[i]))

    for _ in prepass(0):
        pass
    for t in range(NT):
        gen = prepass(t + 1) if t + 1 < NT else None
        scan_tile(t, gen)
        if gen is not None:
            for _ in gen:
                pass
    return fin

def build_gdn():
    nc = new_nc()
    P = Prog(nc)
    C = setup_common(P, nc)
    fin = gdn_phase(P, C, nc)
    P.emit(final_wait_ops=fin)
    return nc

def colv(v):
    v = np.asarray(v, np.float32)
    return np.ascontiguousarray(v.reshape(-1, 128).T)
def make_cols(gate_mix=None, ffn=None, pre=None, mixnorm=None):
    cols = np.zeros((128, 136), np.float32)
    if gate_mix is not None: cols[:, 0:16] = colv(gate_mix)
    if ffn is not None:
        ng, sh, sc, gt = ffn
        cols[:, 16:32] = colv(ng); cols[:, 32:48] = colv(sh); cols[:, 48:64] = colv(sc); cols[:, 64:80] = colv(gt)
    if pre is not None:
        ng, sh, sc = pre
        cols[:, 80:96] = colv(ng)
        if sh is not None:
            cols[:, 96:112] = colv(sh); cols[:, 112:128] = colv(sc)
    if mixnorm is not None:
        m = colv(mixnorm)
        cols[:, 128:128+m.shape[1]] = m
    return cols
def w_in_even(w):
    kr = w[:, 1024:1088]
    return np.ascontiguousarray(np.concatenate([w, kr[:, 32:], kr[:, :32]], axis=1))
def mla_consts():
    half = 32
    inv = (10000.0 ** (-np.arange(half, dtype=np.float32) / half)).astype(np.float32)
    invf = np.zeros(128, np.float32); invf[:64] = np.concatenate([inv, inv])
    sign = np.zeros(128, np.float32); sign[:32] = -1; sign[32:64] = 1
    k = np.arange(128)[:, None, None]; r = np.arange(4)[None, :, None]; q = np.arange(512)[None, None, :]
    masks = (q >= k + 128*r).astype(np.float32)
    return invf, sign, np.ascontiguousarray(masks)
def mla_inputs(proj, positions, q_norm, kv_norm, w_uq, w_ukv, h):
    invf, sign, masks = mla_consts()
    cols = np.zeros((128, 10), np.float32)
    cols[:, 0:4] = colv(q_norm); cols[:, 4:8] = colv(kv_norm); cols[:, 8] = invf; cols[:, 9] = sign
    wqh = w_uq[:, h*192:(h+1)*192]
    wq = np.concatenate([wqh[:, :128], wqh[:, 128:192], wqh[:, 160:192], wqh[:, 128:160]], axis=1)
    wkv = w_ukv[:, h*256:(h+1)*256]
    return {"cqT": np.ascontiguousarray(proj[:, 0:512].T), "ckvT": np.ascontiguousarray(proj[:, 512:1024].T),
            "krT": np.ascontiguousarray(proj[:, 1024:1088].T), "krsT": np.ascontiguousarray(proj[:, 5200:5264].T),
            "mla_cols": cols, "wq": np.ascontiguousarray(wq), "wkv": np.ascontiguousarray(wkv),
            "pos": np.ascontiguousarray(positions.reshape(1, -1).astype(np.int32)), "masks": masks}
def gla_consts():
    tp = np.arange(64)[:, None]; t = np.arange(64)[None, :]
    triS = np.where(tp <= t, -1.0/16.0, 0.0); triR = np.where(tp > t, -1.0/16.0, 0.0)
    maskT = np.where(t >= tp, 1.0, 0.0)
    return np.ascontiguousarray(np.concatenate([triS, triR, maskT], axis=1).astype(np.float32))
def gla_inputs(proj, w_gk2, b_gk2, core):
    h, e = core // 2, core % 2
    q = proj[:, h*256:(h+1)*256]; k = proj[:, 1024+h*256:1024+(h+1)*256]
    v = proj[:, 2048+h*512+e*256:2048+h*512+(e+1)*256]
    gl = proj[:, 6144:6160]
    waug = np.concatenate([w_gk2[:, h*256:(h+1)*256], b_gk2[None, h*256:(h+1)*256]], axis=0)
    return {"gqT": np.ascontiguousarray(q.T), "gkT": np.ascontiguousarray(k.T), "gktok": np.ascontiguousarray(k),
            "gvtok": np.ascontiguousarray(v), "glowT": np.ascontiguousarray(gl.T), "gwaug": np.ascontiguousarray(waug),
            "gcst": gla_consts()}
def gdn_consts():
    p = np.arange(64)[:, None]; f = np.arange(64)[None, :]
    I64 = (p == f); mLs = (p > f); mUi = (f >= p); mUs = (f > p)
    c64 = np.concatenate([I64, mLs, mUi, mUs], axis=1).astype(np.float32)
    rm = np.ones((128, 512), np.float32); rm[:, ::64] = 0.0
    return np.ascontiguousarray(c64), np.eye(128, dtype=np.float32), rm
def gdn_inputs(proj, conv_w, a_log, dt_bias, h):
    q0 = 1088
    x = np.concatenate([proj[:, q0+h*128:q0+(h+1)*128], proj[:, q0+1024+h*128:q0+1024+(h+1)*128],
                        proj[:, q0+2048+h*128:q0+2048+(h+1)*128]], axis=1)
    cw = np.zeros((128, 14), np.float32)
    for kind in range(3):
        cw[:, kind*4:(kind+1)*4] = conv_w[:, kind*1024+h*128:kind*1024+(h+1)*128].T
    cw[:, 12] = a_log[h]; cw[:, 13] = dt_bias[h]
    c64, ident, rm = gdn_consts()
    return {"dxT": np.ascontiguousarray(x.T), "dcw": cw, "dbl": np.ascontiguousarray(proj[:, 5184+h][None]),
            "dal": np.ascontiguousarray(proj[:, 5192+h][None]), "dc64": c64, "did": ident, "drm": rm}

_DBG = {}
_NC_CACHE = {}

def _get(key, fn):
    if key not in _NC_CACHE:
        _NC_CACHE[key] = fn()
    return _NC_CACHE[key]

def _run(nc, maps):
    return run_bass_kernel_spmd(nc, maps, core_ids=list(range(8))).results

def kernel(x, c, positions, norm_g, ada_w, ada_b, ab_w_in, mla_q_norm, mla_w_uq, mla_kv_norm, mla_w_ukv,
           gdn_conv_w, gdn_a_log, gdn_dt_bias, gdn_norm, ab_w_out, gla_w_in, gla_w_gk2, gla_b_gk2, gla_norm,
           gla_w_out, ffn_w1, ffn_w3, ffn_w2, final_norm):
    f32 = lambda a: np.asarray(a, np.float32)
    x = f32(x)[0]; c = f32(c)[0]; positions = np.asarray(positions)
    norm_g = f32(norm_g); NL = 4
    c_col = np.ascontiguousarray(c.reshape(16, 128).T)
    aw = f32(ada_w).reshape(8, 2048, 6144); ab = f32(ada_b).reshape(8, 6144)
    maps = []
    for j in range(8):
        maps.append({"c_col": c_col, "w": np.ascontiguousarray(aw[:, :, j*768:(j+1)*768]),
                     "b": np.ascontiguousarray(ab[:, j*768:(j+1)*768].reshape(8, 6, 128).transpose(2, 0, 1).reshape(128, 48))})
    res = _run(_get("ada", build_ada), maps)
    mods = np.zeros((8, 6144), np.float32)
    for j in range(8):
        mods[:, j*768:(j+1)*768] = res[j]["o"].reshape(128, 8, 6).transpose(1, 2, 0).reshape(8, 768)
    del aw, maps
    SH, SC, GT = slice(0, 2048), slice(2048, 4096), slice(4096, 6144)
    def w_in_for(l):
        return w_in_even(f32(ab_w_in[l // 2])) if l % 2 == 0 else np.ascontiguousarray(f32(gla_w_in[l // 2]))
    xT = [np.ascontiguousarray(x[j*1024:(j+1)*1024].T) for j in range(8)]
    cols = make_cols(pre=(norm_g[0, 0], mods[0][SH], mods[0][SC]))
    w_in = w_in_for(0)
    res = _run(_get(("t", None, 5264, False), lambda: build_t(None, True, 5264, False)),
               [{"xT": xT[j], "cols": cols, "w_in": w_in} for j in range(8)])
    proj = np.concatenate([res[j]["projT"].T for j in range(8)], axis=0)
    for l in range(NL):
        i = l // 2
        if l % 2 == 0:
            res = _run(_get("mla", build_mla), [mla_inputs(proj, positions, f32(mla_q_norm[i]), f32(mla_kv_norm[i]),
                                                             f32(mla_w_uq[i]), f32(mla_w_ukv[i]), h) for h in range(8)])
            o_a = np.concatenate([res[h]["oaT"].T for h in range(8)], axis=1)
            res = _run(_get("gdn", build_gdn), [gdn_inputs(proj, f32(gdn_conv_w[i]), f32(gdn_a_log[i]), f32(gdn_dt_bias[i]), h)
                                                 for h in range(8)])
            o_b = np.concatenate([res[h]["dobT"].T for h in range(8)], axis=1)
            o = np.concatenate([o_a, o_b], axis=1)
            gsrc = proj[:, 4160:5184]
            mixnorm = f32(gdn_norm[i]); w_out = f32(ab_w_out[i]); kind = "ab"
        else:
            res = _run(_get("gla", build_gla), [gla_inputs(proj, f32(gla_w_gk2[i]), f32(gla_b_gk2[i]), cc) for cc in range(8)])
            o = np.concatenate([res[cc]["go"] for cc in range(8)], axis=1)
            gsrc = proj[:, 4096:6144]
            mixnorm = f32(gla_norm[i]); w_out = f32(gla_w_out[i]); kind = "gla"
        _DBG[f"o{l}"] = o
        last = (l == NL - 1)
        if last:
            pre = (f32(final_norm), None, None); f_next = 0
        else:
            pre = (norm_g[l+1, 0], mods[2*(l+1)][SH], mods[2*(l+1)][SC]); f_next = 5264 if (l+1) % 2 == 0 else 6160
        cols = make_cols(gate_mix=mods[2*l][GT], ffn=(norm_g[l, 1], mods[2*l+1][SH], mods[2*l+1][SC], mods[2*l+1][GT]),
                         pre=pre, mixnorm=mixnorm)
        w1 = np.ascontiguousarray(f32(ffn_w1[l])); w3 = np.ascontiguousarray(f32(ffn_w3[l])); w2 = np.ascontiguousarray(f32(ffn_w2[l]))
        w_out = np.ascontiguousarray(w_out)
        maps = []
        w_in = None if last else w_in_for(l + 1)
        for j in range(8):
            tk = slice(j*1024, (j+1)*1024)
            m = {"xT": xT[j], "cols": cols, "oT": np.ascontiguousarray(o[tk].T), "gT": np.ascontiguousarray(gsrc[tk].T),
                 "w_out": w_out, "w1": w1, "w3": w3, "w2": w2}
            if not last:
                m["w_in"] = w_in
            maps.append(m)
        key = ("t", kind, f_next, last)
        res = _run(_get(key, (lambda kind=kind, f_next=f_next, last=last: build_t(kind, not last, f_next, last))), maps)
        xT = [res[j]["xT_out"] for j in range(8)]
        if not last:
            proj = np.concatenate([res[j]["projT"].T for j in range(8)], axis=0)
        _DBG[f"x{l}"] = xT
    out = np.concatenate([xT[j].T for j in range(8)], axis=0)
    return np.ascontiguousarray(out[None]).astype(np.float32)
```

```python
import numpy as np
import concourse.bass as bass
import concourse.mybir as mybir
from concourse.bass_utils import run_bass_kernel_spmd
from contextlib import ExitStack


F32 = mybir.dt.float32
BF16 = mybir.dt.bfloat16
I32 = mybir.dt.int32
AF = mybir.ActivationFunctionType
ALU = mybir.AluOpType
AX = mybir.AxisListType


class Buf:
    __slots__ = ("name", "last_w", "readers")

    def __init__(self, name=""):
        self.name = name
        self.last_w = None
        self.readers = []


class V:
    __slots__ = ("ap", "buf")

    def __init__(self, ap, buf):
        self.ap = ap
        self.buf = buf

    def __getitem__(self, k):
        return V(self.ap[k], self.buf)

    def rearrange(self, s, **kw):
        return V(self.ap.rearrange(s, **kw), self.buf)

    def bitcast(self, dt):
        return V(self.ap.bitcast(dt), self.buf)

    def bcast(self, shape):
        return V(self.ap.broadcast_to(shape), self.buf)

    def sub(self, k, name=""):
        return V(self.ap[k], Buf(name))

    @property
    def shape(self):
        return self.ap.shape


class Op:
    __slots__ = ("eng", "fn", "deps", "signals", "count", "is_dma", "lane", "idx")


ENGS = ("pe", "act", "dve", "pool", "sp")


def _ap(x):
    return x.ap if isinstance(x, V) else x


class Prog:
    N_LANES = 6
    SAME_ENG_WINDOW = 6

    def __init__(self, nc):
        self.nc = nc
        self.ops = {e: [] for e in ENGS}
        self.stack = ExitStack()
        self.nbytes = 0

    def sbuf(self, name, shape, dtype):
        t = self.stack.enter_context(self.nc.sbuf_tensor(name, list(shape), dtype))
        return V(t[:], Buf(name))

    def psum(self, name, shape, dtype=F32):
        t = self.stack.enter_context(self.nc.psum_tensor(name, list(shape), dtype))
        return V(t[:], Buf(name))

    def add(self, eng, fn, reads=(), writes=(), is_dma=False):
        op = Op()
        op.eng = eng
        op.fn = fn
        op.signals = False
        op.count = None
        op.is_dma = is_dma
        op.lane = None
        lst = self.ops[eng]
        op.idx = len(lst)
        deps = []
        for r in reads:
            b = r.buf if isinstance(r, V) else r
            if b is None:
                continue
            if b.last_w is not None:
                deps.append(b.last_w)
        for w in writes:
            b = w.buf if isinstance(w, V) else w
            if b is None:
                continue
            if b.last_w is not None:
                deps.append(b.last_w)
            deps.extend(b.readers)
        fd = []
        seen = set()
        for d in deps:
            if d is op or id(d) in seen:
                continue
            seen.add(id(d))
            if (not d.is_dma) and d.eng == eng and not is_dma and (op.idx - d.idx) > self.SAME_ENG_WINDOW:
                continue
            if (not d.is_dma) and (not is_dma) and d.eng == eng == "pe":
                continue
            if (not d.is_dma) and d.eng == eng and is_dma:
                pass
            fd.append(d)
            d.signals = True
        op.deps = fd
        for r in reads:
            b = r.buf if isinstance(r, V) else r
            if b is None:
                continue
            b.readers = [x for x in b.readers if not (x.eng == eng and not x.is_dma and not is_dma)] + [op]
        for w in writes:
            b = w.buf if isinstance(w, V) else w
            if b is None:
                continue
            b.last_w = op
            b.readers = []
        lst.append(op)
        return op

    F32R = False

    def matmul(self, out, lhsT, rhs, start=True, stop=True, **kw):
        if self.F32R and lhsT.ap.dtype == F32 and rhs.ap.dtype == F32:
            lhsT = lhsT.bitcast(mybir.dt.float32r); rhs = rhs.bitcast(mybir.dt.float32r)
        reads = [lhsT, rhs] + ([] if start else [])
        return self.add("pe", lambda e: e.matmul(out.ap, lhsT.ap, rhs.ap, start=start, stop=stop, **kw),
                        reads=reads, writes=[out])

    def transpose(self, out, in_, ident):
        return self.add("pe", lambda e: e.transpose(out.ap, in_.ap, ident.ap), reads=[in_, ident], writes=[out])

    def act(self, out, in_, func, bias=None, scale=None, accum_out=None, eng="act"):
        reads = [in_]
        kw = {}
        if bias is not None:
            kw["bias"] = _ap(bias)
            if isinstance(bias, V):
                reads.append(bias)
        if scale is not None:
            kw["scale"] = _ap(scale)
            if isinstance(scale, V):
                reads.append(scale)
        writes = [out]
        if accum_out is not None:
            kw["accum_out"] = accum_out.ap
            writes.append(accum_out)
        return self.add(eng, lambda e: e.activation(out.ap, in_.ap, func, **kw), reads=reads, writes=writes)

    def tt(self, out, in0, in1, op, eng="dve"):
        return self.add(eng, lambda e: e.tensor_tensor(out.ap, in0.ap, in1.ap, op), reads=[in0, in1], writes=[out])

    def ts(self, out, in0, s1, s2, op0, op1=None, eng="dve", accum_out=None):
        reads = [in0] + [s for s in (s1, s2) if isinstance(s, V)]
        writes = [out]
        kw = {}
        if op1 is not None:
            kw["op1"] = op1
        if accum_out is not None:
            kw["accum_out"] = accum_out.ap
            writes.append(accum_out)
        return self.add(eng, lambda e: e.tensor_scalar(out.ap, in0.ap, _ap(s1), _ap(s2), op0, **kw),
                        reads=reads, writes=writes)

    def stt(self, out, in0, scalar, in1, op0, op1, eng="dve"):
        reads = [in0, in1] + ([scalar] if isinstance(scalar, V) else [])
        return self.add(eng, lambda e: e.scalar_tensor_tensor(out.ap, in0.ap, _ap(scalar), in1.ap, op0, op1),
                        reads=reads, writes=[out])

    def copy(self, out, in_, eng="dve"):
        if eng == "act":
            return self.add("act", lambda e: e.copy(out.ap, in_.ap), reads=[in_], writes=[out])
        return self.add(eng, lambda e: e.tensor_copy(out.ap, in_.ap), reads=[in_], writes=[out])

    def memset(self, out, val, eng="pool"):
        return self.add(eng, lambda e: e.memset(out.ap, val), writes=[out])

    def recip(self, out, in_):
        return self.add("dve", lambda e: e.reciprocal(out.ap, in_.ap), reads=[in_], writes=[out])

    def reduce(self, out, in_, op, axis=AX.X, eng="dve"):
        return self.add(eng, lambda e: e.tensor_reduce(out.ap, in_.ap, axis, op), reads=[in_], writes=[out])

    def dma(self, out, in_, q="sp", **kw):
        reads = [in_] if isinstance(in_, V) else []
        writes = [out] if isinstance(out, V) else []
        return self.add(q, lambda e: e.dma_start(out=_ap(out), in_=_ap(in_), **kw), reads=reads, writes=writes,
                        is_dma=True)

    def emit(self, final_wait_ops=()):
        nc = self.nc
        st = self.stack
        sem = {e: st.enter_context(nc.semaphore("s_" + e)) for e in ENGS}
        lanes = {e: [st.enter_context(nc.semaphore(f"l_{e}{i}")) for i in range(self.N_LANES)] for e in ENGS}
        for e in ENGS:
            c = 0
            lc = [0] * self.N_LANES
            nd = 0
            for op in self.ops[e]:
                if op.is_dma:
                    op.lane = nd % self.N_LANES
                    nd += 1
                    lc[op.lane] += 16
                    op.count = lc[op.lane]
                    op.signals = True
                elif op.signals:
                    c += 1
                    op.count = c
        for op in final_wait_ops:
            assert op.is_dma
        self.stats = {e: len(self.ops[e]) for e in ENGS}

        def run(e, engobj):
            waited = {}
            for op in self.ops[e]:
                need = {}
                for d in op.deps:
                    s = lanes[d.eng][d.lane] if d.is_dma else sem[d.eng]
                    key = id(s)
                    if need.get(key, (None, 0))[1] < d.count:
                        need[key] = (s, d.count)
                for key, (s, cnt) in need.items():
                    if waited.get(key, 0) >= cnt:
                        continue
                    engobj.wait_ge(s, cnt)
                    waited[key] = cnt
                ins = op.fn(engobj)
                if op.is_dma:
                    ins.then_inc(lanes[e][op.lane], 16)
                elif op.signals:
                    ins.then_inc(sem[e], 1)
            if e == "sp":
                for op in final_wait_ops:
                    engobj.wait_ge(lanes[op.eng][op.lane], op.count)

        with nc.Block() as block:
            @block.tensor
            def _(eng):
                run("pe", eng)

            @block.scalar
            def _(eng):
                run("act", eng)

            @block.vector
            def _(eng):
                run("dve", eng)

            @block.gpsimd
            def _(eng):
                run("pool", eng)

            @block.sync
            def _(eng):
                run("sp", eng)
        st.close()


D = 2048; KC = 16; TOK = 1024; NTT = 2; DFF = 5632; EPS = 1e-6

def new_nc():
    return bass.Bass("TRN2", target_bir_lowering=False)

class Ctx:
    pass

def setup_common(P, nc):
    C = Ctx()
    C.banks = [P.psum(f"pb{i}", [128, 512], F32) for i in range(8)]
    C.bi = 0
    C.ones = P.sbuf("ones_f", [128, 128], F32)
    P.memset(C.ones, 1.0)
    C.ones_b = P.sbuf("ones_bf", [128, 128], BF16)
    P.memset(C.ones_b, 1.0)
    C.sqb = [P.sbuf(f"sqb{i}", [128, 512], BF16) for i in range(2)]
    C.ev = 0
    return C

def bank(C):
    rot = getattr(C, "rot", None)
    if rot is not None:
        b = rot[C.bi % len(rot)]
    else:
        b = C.banks[C.bi % getattr(C, "nrot", 8)]
    C.bi += 1
    return b

def rms_rstd(P, C, src_chunks, nfeat, rstd_out, sqtmp, T=TOK):
    n = len(src_chunks)
    sqtmp = C.sqb
    for tt in range(T // 512):
        ps = bank(C)
        for k, s in enumerate(src_chunks):
            sq = sqtmp[k % len(sqtmp)]
            P.act(sq[:, 0:512], s[:, tt*512:(tt+1)*512], AF.Square)
            P.matmul(ps, C.ones_b, sq[:, 0:512], start=(k == 0), stop=(k == n-1))
        P.act(rstd_out[:, tt*512:(tt+1)*512], ps, AF.Sqrt, bias=EPS, scale=1.0/nfeat)
    P.recip(rstd_out, rstd_out)

def load_w(P, wbuf, w_dram, k0, kn, f0, fw):
    dst = wbuf[:, 0:kn*fw].rearrange("p (k f) -> p k f", f=fw)
    src = w_dram[k0*128:(k0+kn)*128, f0:f0+fw].rearrange("(k p) f -> p k f", p=128)
    P.dma(dst, src, q="pool")
    return dst

def linear(P, C, w_dram, k0, kn, f_lo, f_hi, act, wbufs, consume, T=TOK, BW=256):
    bidx = getattr(C, "wrot", 0)
    for fb in range(f_lo, f_hi, BW):
        fw = min(BW, f_hi - fb)
        wb = load_w(P, wbufs[bidx % len(wbufs)], w_dram, k0, kn, fb, fw)
        bidx += 1
        for fc in range(0, fw, 128):
            fsz = min(128, fw - fc)
            for tt in range(T // 512):
                ps = bank(C)
                for k in range(kn):
                    P.matmul(ps[0:fsz, :], wb[:, k, fc:fc+fsz], act[:, k, tt*512:(tt+1)*512],
                             start=(k == 0), stop=(k == kn-1))
                consume(fb + fc, fsz, tt, ps[0:fsz, :])
    C.wrot = bidx

def build_t(has_post, has_pre, f_next, is_last):
    nc = new_nc()
    dt = lambda name, shape, kind="ExternalInput", d=F32: nc.dram_tensor(name, list(shape), d, kind=kind).ap()
    xT = dt("xT", [D, TOK])
    NCOL = 16 * 8 + 8
    cols_d = dt("cols", [128, NCOL])
    if has_post:
        oT = dt("oT", [D, TOK])
        gT = dt("gT", [1024 if has_post == "ab" else 2048, TOK])
        w_out = dt("w_out", [D, D])
        w1 = dt("w1", [D, DFF]); w3 = dt("w3", [D, DFF]); w2 = dt("w2", [DFF, D])
    if has_pre:
        w_in = dt("w_in", [D, f_next])
        projT = dt("projT", [f_next, TOK], kind="ExternalOutput")
    xT_out = dt("xT_out", [D, TOK], kind="ExternalOutput")

    P = Prog(nc)
    C = setup_common(P, nc)
    x = P.sbuf("x_sb", [128, KC, TOK], F32)
    xs = [x.sub((slice(None), k, slice(None)), f"x{k}") for k in range(KC)]
    hb = P.sbuf("hb", [128, KC, TOK], BF16)
    cols = P.sbuf("cols_sb", [128, NCOL], F32)
    P.dma(cols, cols_d)
    def load_x():
        for k in range(KC):
            P.dma(xs[k], xT[k*128:(k+1)*128, :], q="act")
    if not has_post:
        load_x()
    wA = [P.sbuf(f"wA{i}", [128, 4096], BF16) for i in range(3)]
    wB = [P.sbuf(f"wB{i}", [128, 4096], BF16) for i in range(3)]
    tmp = [P.sbuf(f"tmp{i}", [128, TOK], F32) for i in range(3)]
    rstd = P.sbuf("rstd", [128, TOK], F32)
    stg = [P.sbuf(f"stg{i}", [128, 512], F32) for i in range(3)]
    gmod = P.sbuf("gmod", [128, 16], F32)
    fin = []
    def prenorm(ng0, sh0, sc0):
        rms_rstd(P, C, xs, D, rstd, tmp[0:2])
        P.ts(gmod, cols[:, sc0:sc0+16], 1.0, None, ALU.add)
        P.tt(gmod, gmod, cols[:, ng0:ng0+16], ALU.mult)
        for k in range(KC):
            t = tmp[k % 2]
            P.stt(t, xs[k], gmod[:, k:k+1], rstd, ALU.mult, ALU.mult)
            P.act(hb[:, k, :], t, AF.Identity, bias=cols[:, sh0+k:sh0+k+1])

    def resid_consumer(gate0):
        def consume(f0, fsz, tt, ps):
            k = f0 // 128
            assert f0 % 128 == 0 and fsz == 128
            xv = xs[k][:, tt*512:(tt+1)*512]
            P.stt(xv, ps, cols[:, gate0+k:gate0+k+1], xv, ALU.mult, ALU.add)
        return consume

    if has_post:
        if has_post == "ab":
            for k in range(8):
                t = tmp[k % 2]
                P.dma(t, oT[k*128:(k+1)*128, :])
                P.copy(hb[:, k, :], t, eng="act" if k % 2 else "dve")
            groups = [[8 + k] for k in range(8)]
            ncol0 = 128
        else:
            groups = [[4*h + j for j in range(4)] for h in range(4)]
            ncol0 = 128
        ot = [P.sbuf(f"ot{i}", [128, TOK], F32) for i in range(4)]
        extra = []
        for i in range(3):
            f = wB[i].bitcast(F32)
            extra += [f.sub((slice(None), slice(0, 1024)), f"wBx{i}a"), f.sub((slice(None), slice(1024, 2048)), f"wBx{i}b")]
        obufs = ot + extra[0:4]
        rstds = [rstd, extra[4]]
        zts = [tmp[2], extra[5]]
        nz = 0
        for g, grp in enumerate(groups):
            srcs = []
            for j, k in enumerate(grp):
                ob = obufs[g % 8] if len(grp) == 1 else obufs[(g % 2) * 4 + j]
                P.dma(ob, oT[k*128:(k+1)*128, :])
                srcs.append(ob)
            rs_ = rstds[g % 2]
            rms_rstd(P, C, srcs, 128 * len(grp), rs_, tmp[0:2])
            for j, k in enumerate(grp):
                gk = k - 8 if has_post == "ab" else k
                zt = zts[nz % 2]; nz += 1
                P.dma(zt, gT[gk*128:(gk+1)*128, :])
                P.act(zt, zt, AF.Silu)
                t = tmp[j % 2]
                P.stt(t, srcs[j], cols[:, ncol0+j:ncol0+j+1], rs_, ALU.mult, ALU.mult)
                P.tt(hb[:, k, :], t, zt, ALU.mult)
        for i in range(3):
            P.add("pool", (lambda i: (lambda e: e.memset(wB[i].ap[:, 0:2], 0.0)))(i),
                  writes=[extra[2*i], extra[2*i+1], wB[i]])
        load_x()
        linear(P, C, w_out, 0, KC, 0, D, hb, wA, resid_consumer(0))
        prenorm(16, 32, 48)
        gblk = P.sbuf("gblk", [128, 11, TOK], BF16)
        for bi in range(4):
            c0 = bi * 11
            bidx = 0
            for fb in range(c0*128, (c0+11)*128, 256):
                fw = min(256, (c0+11)*128 - fb)
                wa = load_w(P, wA[bidx % 3], w1, 0, KC, fb, fw)
                wb = load_w(P, wB[bidx % 3], w3, 0, KC, fb, fw)
                bidx += 1
                for fc in range(0, fw, 128):
                    kk = (fb + fc) // 128 - c0
                    for tt in range(NTT):
                        pa = bank(C); pb = bank(C)
                        for k in range(KC):
                            P.matmul(pa, wa[:, k, fc:fc+128], hb[:, k, tt*512:(tt+1)*512], start=(k == 0), stop=(k == KC-1))
                        for k in range(KC):
                            P.matmul(pb, wb[:, k, fc:fc+128], hb[:, k, tt*512:(tt+1)*512], start=(k == 0), stop=(k == KC-1))
                        s = stg[C.ev % 3]; C.ev += 1
                        P.act(s, pa, AF.Silu)
                        P.tt(gblk[:, kk, tt*512:(tt+1)*512], s, pb, ALU.mult)
            linear(P, C, w2, c0, 11, 0, D, gblk, wA, resid_consumer(64))
    if has_pre:
        prenorm(80, 96, 112)
        def consume(f0, fsz, tt, ps):
            s = stg[C.ev % 3]
            if C.ev % 2:
                P.copy(s[0:fsz, :], ps, eng="act")
            else:
                P.copy(s[0:fsz, :], ps, eng="dve")
            C.ev += 1
            fin.append(P.dma(projT[f0:f0+fsz, tt*512:(tt+1)*512], s[0:fsz, :]))
        linear(P, C, w_in, 0, KC, 0, f_next, hb, wA, consume)
    if is_last:
        rms_rstd(P, C, xs, D, rstd, tmp[0:2])
        for k in range(KC):
            P.stt(xs[k], xs[k], cols[:, 80+k:80+k+1], rstd, ALU.mult, ALU.mult)
    for k in range(KC):
        fin.append(P.dma(xT_out[k*128:(k+1)*128, :], xs[k]))
    P.emit(final_wait_ops=fin)
    return nc

def build_ada():
    nc = new_nc()
    c_col = nc.dram_tensor("c_col", [128, 16], F32, kind="ExternalInput").ap()
    w = nc.dram_tensor("w", [8, D, 768], F32, kind="ExternalInput").ap()
    b = nc.dram_tensor("b", [128, 48], F32, kind="ExternalInput").ap()
    o = nc.dram_tensor("o", [128, 48], F32, kind="ExternalOutput").ap()
    P = Prog(nc)
    cc = P.sbuf("cc", [128, 16], F32)
    sc = P.sbuf("sc", [128, 16], F32)
    bb = P.sbuf("bb", [128, 48], F32)
    ob = P.sbuf("ob", [128, 48], F32)
    P.dma(cc, c_col); P.dma(bb, b)
    P.act(sc, cc, AF.Silu)
    wb = [P.sbuf(f"w{i}", [128, 16, 768], F32) for i in range(2)]
    ps = P.psum("ps", [128, 512], F32)
    for m in range(8):
        wt = wb[m % 2]
        for q in range(4):
            P.dma(wt[:, 4*q:4*q+4, :], w[m, q*512:(q+1)*512, :].rearrange("(k p) f -> p k f", p=128), q="sp" if q % 2 else "act")
        for fc in range(6):
            for k in range(16):
                P.matmul(ps[:, m*6+fc:m*6+fc+1], wt[:, k, fc*128:(fc+1)*128], sc[:, k:k+1], start=(k == 0), stop=(k == 15))
    P.tt(ob, ps[:, 0:48], bb, ALU.add)
    f = P.dma(o, ob)
    P.emit(final_wait_ops=[f])
    return nc


S = 8192; NT = 16; EPS = 1e-6

def mla_phase(P, C, nc, pfx=""):
    dt = lambda name, shape, kind="ExternalInput", d=F32: nc.dram_tensor(pfx + name, list(shape), d, kind=kind).ap()
    cqT = dt("cqT", [512, S]); ckvT = dt("ckvT", [512, S]); krT = dt("krT", [64, S]); krsT = dt("krsT", [64, S])
    ncols = dt("mla_cols", [128, 10])
    wq = dt("wq", [512, 256])
    wkv = dt("wkv", [512, 256])
    pos = dt("pos", [1, S], d=I32)
    masks_d = dt("masks", [128, 4, 512])
    oT = dt("oaT", [128, S], kind="ExternalOutput")
    fin = []
    SCALE = 1.0 / np.sqrt(192.0)

    cols = P.sbuf("mcols", [128, 10], F32); P.dma(cols, ncols)
    wq_b = P.sbuf("wq_b", [128, 4, 256], BF16); P.dma(wq_b, wq.rearrange("(k p) f -> p k f", p=128), q="pool")
    wkv_b = P.sbuf("wkv_b", [128, 4, 256], BF16); P.dma(wkv_b, wkv.rearrange("(k p) f -> p k f", p=128), q="pool")
    mk = P.sbuf("mk", [128, 4, 512], BF16); P.dma(mk, masks_d, q="pool")
    ones_b = P.sbuf("ones_b", [128, 128], BF16); P.memset(ones_b, 1.0)

    qa = P.sbuf("qa", [128, S], BF16); qb = P.sbuf("qb", [64, S], BF16)
    ka = P.sbuf("ka", [128, S], BF16); kb = P.sbuf("kb", [64, S], BF16)
    vt = P.sbuf("vt", [128, 64, 128], BF16)
    cos2 = P.sbuf("cos2", [64, S], BF16); sinpm = P.sbuf("sinpm", [64, S], BF16)
    tl = lambda t: slice(t*512, (t+1)*512)
    qa_t = [qa.sub((slice(None), tl(t))) for t in range(NT)]; qb_t = [qb.sub((slice(None), tl(t))) for t in range(NT)]
    ka_t = [ka.sub((slice(None), tl(t))) for t in range(NT)]; kb_t = [kb.sub((slice(None), tl(t))) for t in range(NT)]
    vt_t = [vt.sub((slice(None), slice(4*t, 4*t+4), slice(None))) for t in range(NT)]

    SEG = 512
    pi_ = P.sbuf("pos_i", [64, SEG], I32); u = P.sbuf("rp_u", [64, SEG], F32); kf = P.sbuf("rp_kf", [64, SEG], F32)
    ki = P.sbuf("rp_ki", [64, SEG], I32)
    angf = P.sbuf("rp_ang", [64, SEG], F32)
    TWO_PI = float(2*np.pi)
    for sg in range(S // SEG):
        sl = slice(sg*SEG, (sg+1)*SEG)
        P.dma(pi_, pos[:, sl].partition_broadcast(64))
        P.copy(kf, pi_)
        for which, off, dst in (("sin", 0.5, sinpm), ("cos", 0.75, cos2)):
            P.ts(u, kf, cols[0:64, 8:9], 1.0/TWO_PI, ALU.mult, ALU.mult)
            P.ts(u, u, off, None, ALU.add)
            P.copy(ki, u)
            ang = angf
            P.copy(ang, ki)
            P.tt(u, u, ang, ALU.subtract)
            P.ts(ang, u, 0.0, None, ALU.is_lt)
            P.tt(u, u, ang, ALU.add)
            P.act(dst[:, sl], u, AF.Sin, bias=-float(np.pi), scale=TWO_PI)
        P.ts(sinpm[:, sl], sinpm[:, sl], cols[0:64, 9:10], None, ALU.mult)

    cq = [P.sbuf(f"cq{i}", [128, 4, 512], F32) for i in range(2)]
    cn = [P.sbuf(f"cn{i}", [128, 4, 512], BF16) for i in range(2)]
    sqt = [P.sbuf(f"sqt{i}", [128, 512], F32) for i in range(1)]
    rsts = [P.sbuf(f"rst{i}", [128, 512], F32) for i in range(2)]
    kr = [P.sbuf(f"kr{i}", [64, 512], F32) for i in range(2)]
    krs = [P.sbuf(f"krs{i}", [64, 512], F32) for i in range(2)]
    r1s = [P.sbuf(f"r1_{i}", [64, 512], F32) for i in range(2)]; r2s = [P.sbuf(f"r2_{i}", [64, 512], F32) for i in range(2)]
    tmpns = [P.sbuf(f"tmpn{i}", [128, 512], F32) for i in range(2)]

    def load_norm(src, t, ncol0, i):
        P.dma(cq[i], src[:, tl(t)].rearrange("(k p) t -> p k t", p=128))
        rst = rsts[i]; tmpn = tmpns[i]
        rms_rstd(P, C, [cq[i][:, k, :] for k in range(4)], 512, rst, sqt, T=512)
        for k in range(4):
            P.stt(cn[i][:, k, :], cq[i][:, k, :], cols[:, ncol0+k:ncol0+k+1], rst, ALU.mult, ALU.mult)

    def rope(dst, a, b, t, side):
        r1 = r1s[side]; r2 = r2s[side]
        P.tt(r1, a, cos2[:, tl(t)], ALU.mult)
        P.tt(r2, b, sinpm[:, tl(t)], ALU.mult)
        P.tt(dst, r1, r2, ALU.add)

    for t in range(NT):
        load_norm(cqT, t, 0, 0)
        ps = bank(C)
        for k in range(4):
            P.matmul(ps, wq_b[:, k, 0:128], cn[0][:, k, :], start=(k == 0), stop=(k == 3))
        P.copy(qa_t[t], ps, eng="act")
        ps1 = bank(C)
        for k in range(4):
            P.matmul(ps1[0:64, :], wq_b[:, k, 128:192], cn[0][:, k, :], start=(k == 0), stop=(k == 3))
        ps2 = bank(C)
        for k in range(4):
            P.matmul(ps2[0:64, :], wq_b[:, k, 192:256], cn[0][:, k, :], start=(k == 0), stop=(k == 3))
        rope(qb_t[t], ps1[0:64, :], ps2[0:64, :], t, 0)
        load_norm(ckvT, t, 4, 1)
        ps = bank(C)
        for k in range(4):
            P.matmul(ps, wkv_b[:, k, 0:128], cn[1][:, k, :], start=(k == 0), stop=(k == 3))
        P.copy(ka_t[t], ps, eng="act")
        ps = bank(C)
        for blk in range(4):
            for k in range(4):
                P.matmul(ps[:, blk*128:(blk+1)*128], cn[1][:, k, blk*128:(blk+1)*128], wkv_b[:, k, 128:256],
                         start=(k == 0), stop=(k == 3))
        P.copy(vt_t[t], ps.rearrange("p (b d) -> p b d", b=4), eng="act")
        i = t % 2
        P.dma(kr[i], krT[:, tl(t)]); P.dma(krs[i], krsT[:, tl(t)])
        rope(kb_t[t], kr[i], krs[i], t, 1)

    sb = [C.banks[0], C.banks[1], C.banks[2], C.banks[3], C.banks[4]]
    oacc = [C.banks[5], C.banks[6]]; sbank = C.banks[7]
    pb = [P.sbuf(f"pexp{i}", [128, 512], BF16) for i in range(5)]
    rs = rsts[0]
    ost = [tmpns[0], tmpns[1]]
    pairs = [(j, b) for j in range(NT) for b in range(4*j + 4)]
    LA = 3
    def emit_qk(n):
        j, b = pairs[n]
        ps = sb[n % 5]
        kt, ko = b // 4, (b % 4) * 128
        P.matmul(ps, ka_t[kt][:, ko:ko+128], qa_t[j], start=True, stop=False)
        P.matmul(ps, kb_t[kt][:, ko:ko+128], qb_t[j], start=False, stop=True)
    def emit_rest(n):
        j, b = pairs[n]
        ps = sb[n % 5]; pe = pb[n % 5]
        kt = b // 4
        nkb = 4*j + 4
        oa = oacc[j % 2]
        P.act(pe, ps, AF.Exp, scale=SCALE)
        if b >= 4*j:
            P.tt(pe, pe, mk[:, b - 4*j, :], ALU.mult, eng="pool")
        P.matmul(oa, vt_t[kt][:, b % 4, :], pe, start=(b == 0), stop=(b == nkb-1))
        P.matmul(sbank, ones_b, pe, start=(b == 0), stop=(b == nkb-1))
        if b == nkb - 1:
            P.recip(rs, sbank)
            o = ost[j % 2]
            P.tt(o, oa, rs, ALU.mult)
            fin.append(P.dma(oT[:, tl(j)], o))
    for n in range(len(pairs) + LA):
        if n < len(pairs):
            emit_qk(n)
        if n - LA >= 0:
            emit_rest(n - LA)
    return fin

def build_mla():
    nc = new_nc()
    P = Prog(nc)
    C = setup_common(P, nc)
    fin = mla_phase(P, C, nc)
    P.emit(final_wait_ops=fin)
    return nc


S = 8192; NCH = 128; CH = 64; TILE = 512; CPT = 8

def gla_phase(P, C, nc, pfx=""):
    dt = lambda name, shape, kind="ExternalInput", d=F32: nc.dram_tensor(pfx + name, list(shape), d, kind=kind).ap()
    qT = dt("gqT", [256, S]); kT = dt("gkT", [256, S])
    ktok = dt("gktok", [S, 256]); vtok = dt("gvtok", [S, 256])
    glow = dt("glowT", [16, S]); waug = dt("gwaug", [17, 256])
    cst = dt("gcst", [64, 192])
    o_d = dt("go", [S, 256], kind="ExternalOutput")
    fin = []
    cs = P.sbuf("gcst_sb", [64, 192], F32); P.dma(cs, cst)
    triS = cs[:, 0:64]; triR = cs[:, 64:128]; maskT = cs[:, 128:192]
    wa = P.sbuf("gwa", [17, 256], F32); P.dma(wa, waug)
    st = P.sbuf("gstate", [128, 2, 256], F32); P.memset(st, 0.0)
    stb = P.sbuf("gstate_b", [128, 2, 256], BF16); P.memset(stb, 0.0)
    NB = 3
    q_in = [P.sbuf(f"gq_in{i}", [128, 2, TILE], F32) for i in range(NB)]
    k_in = [P.sbuf(f"gk_in{i}", [128, 2, TILE], F32) for i in range(NB)]
    kt_in = [P.sbuf(f"gkt_in{i}", [64, CPT, 256], F32) for i in range(NB)]
    v_in = [P.sbuf(f"gv_in{i}", [64, CPT, 256], BF16) for i in range(NB)]
    gl_in = [P.sbuf(f"ggl_in{i}", [17, TILE], F32) for i in range(NB)]
    for g in gl_in:
        P.memset(g, 1.0)
    o_st = [P.sbuf(f"go_st{i}", [64, CPT, 256], F32) for i in range(NB)]
    R = 4
    e1 = [P.sbuf(f"ge1{i}", [64, 256], F32) for i in range(R)]
    lnv = [P.sbuf(f"glnv{i}", [64, 256], F32) for i in range(R)]
    E = [P.sbuf(f"gE{i}", [128, 128], F32) for i in range(R+1)]
    Ei = [P.sbuf(f"gEi{i}", [128, 128], F32) for i in range(R)]
    Er = [P.sbuf(f"gEr{i}", [64, 256], F32) for i in range(R)]
    qt = [P.sbuf(f"gqt{i}", [128, 2, 64], BF16) for i in range(R)]
    kt = [P.sbuf(f"gkt{i}", [128, 2, 64], BF16) for i in range(R)]
    kd = [P.sbuf(f"gkd{i}", [64, 256], BF16) for i in range(R)]
    atm = [P.sbuf(f"gatm{i}", [64, 64], BF16) for i in range(R)]
    NTL = S // TILE
    def load(t):
        i = t % NB
        sl = slice(t*TILE, (t+1)*TILE)
        P.dma(q_in[i], qT[:, sl].rearrange("(k p) t -> p k t", p=128))
        P.dma(k_in[i], kT[:, sl].rearrange("(k p) t -> p k t", p=128))
        P.dma(kt_in[i], ktok[sl, :].rearrange("(c p) d -> p c d", p=64))
        P.dma(v_in[i], vtok[sl, :].rearrange("(c p) d -> p c d", p=64), q="pool")
        P.dma(gl_in[i][0:16, :], glow[:, sl])
    def prep(c):
        t, cc = c // CPT, c % CPT
        if cc == 0:
            load(t)
        i = t % NB
        r = c % R
        cl = slice(cc*64, (cc+1)*64)
        ps = bank(C)
        P.matmul(ps[0:64, 0:256], gl_in[i][:, cl], wa)
        P.act(e1[r], ps[0:64, 0:256], AF.Exp, scale=-1.0)
        P.act(lnv[r], e1[r], AF.Ln, bias=1.0)
        ps2 = bank(C)
        for dc in range(2):
            P.matmul(ps2[:, dc*64:(dc+1)*64], lnv[r][:, dc*128:(dc+1)*128], triS)
        ps3 = bank(C)
        P.matmul(ps3[0:64, 0:256], triR, lnv[r])
        Ec = E[c % (R+1)]
        P.act(Ec, ps2[:, 0:128], AF.Exp)
        P.act(Ei[r], ps2[:, 0:128], AF.Exp, scale=-1.0)
        P.act(Er[r], ps3[0:64, 0:256], AF.Exp)
        P.stt(qt[r], q_in[i][:, :, cl], 1.0/16.0, Ec.rearrange("p (k j) -> p k j", k=2), ALU.mult, ALU.mult)
        P.tt(kt[r], k_in[i][:, :, cl], Ei[r].rearrange("p (k j) -> p k j", k=2), ALU.mult)
        P.tt(kd[r], kt_in[i][:, cc, :], Er[r], ALU.mult)
        ps4 = bank(C)
        for dc in range(2):
            P.matmul(ps4[0:64, 0:64], kt[r][:, dc, :], qt[r][:, dc, :], start=(dc == 0), stop=(dc == 1))
        P.tt(atm[r], ps4[0:64, 0:64], maskT, ALU.mult)
    def scan(c):
        t, cc = c // CPT, c % CPT
        i = t % NB
        r = c % R
        Ec = E[c % (R+1)]
        ps5 = bank(C)
        for dc in range(2):
            P.matmul(ps5[0:64, 0:256], qt[r][:, dc, :], stb[:, dc, :], start=(dc == 0), stop=False)
        P.matmul(ps5[0:64, 0:256], atm[r], v_in[i][:, cc, :], start=False, stop=True)
        P.copy(o_st[i][:, cc, :], ps5[0:64, 0:256], eng="act")
        ps6 = bank(C)
        for dc in range(2):
            P.matmul(ps6[:, dc*256:(dc+1)*256], kd[r][:, dc*128:(dc+1)*128], v_in[i][:, cc, :])
        for dc in range(2):
            P.stt(st[:, dc, :], st[:, dc, :], Ec[:, dc*64+63:dc*64+64], ps6[:, dc*256:(dc+1)*256], ALU.mult, ALU.add)
        P.copy(stb, st, eng="act")
        if cc == CPT - 1:
            sl = slice(t*TILE, (t+1)*TILE)
            fin.append(P.dma(o_d[sl, :].rearrange("(c p) e -> p c e", p=64), o_st[i]))
    NC_ = NTL * CPT
    LA = 2
    for c in range(NC_ + LA):
        if c < NC_:
            prep(c)
        if c - LA >= 0:
            scan(c - LA)
    return fin

def build_gla():
    nc = new_nc()
    P = Prog(nc)
    C = setup_common(P, nc)
    fin = gla_phase(P, C, nc)
    P.emit(final_wait_ops=fin)
    return nc


DEBUG = False
S = 8192; TILE = 512; CPT = 8; EPS = 1e-6

def gdn_phase(P, C, nc, pfx=""):
    dt = lambda name, shape, kind="ExternalInput", d=F32: nc.dram_tensor(pfx + name, list(shape), d, kind=kind).ap()
    xT = dt("dxT", [384, S]); cw_d = dt("dcw", [128, 14]); bl_d = dt("dbl", [1, S]); al_d = dt("dal", [1, S])
    c64_d = dt("dc64", [64, 256]); id_d = dt("did", [128, 128]); rm_d = dt("drm", [128, 512])
    o_d = dt("dobT", [128, S], kind="ExternalOutput")
    fin = []
    C.nrot = 7
    cw = P.sbuf("dcw_sb", [128, 14], F32); P.dma(cw, cw_d)
    c64 = P.sbuf("dc64_sb", [64, 256], F32); P.dma(c64, c64_d)
    I64 = c64[:, 0:64]; mLs = c64[:, 64:128]; mUi = c64[:, 128:192]; mUs = c64[:, 192:256]
    ident = P.sbuf("did_sb", [128, 128], F32); P.dma(ident, id_d)
    rmask = P.sbuf("drm_sb", [128, 512], F32); P.dma(rmask, rm_d)
    negA = P.sbuf("dnegA", [128, 1], F32)
    P.act(negA, cw[:, 12:13], AF.Exp)
    Ss = [P.sbuf(f"dS{i}", [128, 128], F32) for i in range(3)]
    P.memset(Ss[0], 0.0)
    T3 = lambda v: v.rearrange("p (c j) -> p c j", j=64)
    def sb(name, shape, d=F32, n=2):
        return [P.sbuf(f"{name}{i}", shape, d) for i in range(n)]
    xin = [sb("dxq", [128, TILE+3]), sb("dxk", [128, TILE+3]), sb("dxv", [128, TILE+3])]
    y = sb("dy", [128, TILE], n=3); sq = sb("dsq", [128, TILE], n=2); rstd = sb("drstd", [128, TILE], n=1)[0]
    qn = sb("dqn", [128, TILE]); kn = sb("dkn", [128, TILE]); vv = sb("dvv", [128, TILE])
    bl = sb("dblb", [128, TILE]); al = sb("dalb", [128, TILE])
    beta = sb("dbeta", [128, TILE]); g = sb("dg", [128, TILE], n=1)[0]; gc = sb("dgc", [128, TILE]); egc = sb("degc", [128, TILE])
    ekd = sb("dekd", [128, TILE]); egl = sb("degl", [128, CPT])
    knb = sb("dknb", [128, TILE], BF16); kbb = sb("dkbb", [128, TILE], BF16); qnb = sb("dqnb", [128, TILE], BF16)
    kb = sb("dkb", [128, TILE], n=1)[0]; kbg = sb("dkbg", [128, TILE], BF16, n=1)[0]; kdc = sb("dkdc", [128, TILE], BF16, n=1)[0]
    vb = sb("dvb", [128, TILE], BF16, n=1)[0]
    kbg_t = sb("dkbgt", [64, CPT, 128], BF16); vb_t = sb("dvbt", [64, CPT, 128], BF16); kdc_t = sb("dkdct", [64, CPT, 128], BF16)
    identb = P.sbuf("didb", [128, 128], BF16); P.copy(identb, ident)
    Sbs = [P.sbuf(f"dSb{i}", [128, 128], BF16) for i in range(3)]
    P.memset(Sbs[0], 0.0)
    gtok = sb("dgtok", [64, CPT], n=1)[0]; tmp64 = sb("dtmp64", [64, TILE], n=2)
    D1 = sb("dD1", [64, TILE], n=1)[0]
    dec = sb("ddec", [64, TILE], n=1)[0]; decT = sb("ddecT", [64, TILE], n=1)[0]
    Ym = sb("dY", [64, TILE]); Pn = sb("dPn", [64, TILE], BF16); PTn = sb("dPTn", [64, TILE], BF16); Yb = sb("dYb", [64, TILE], BF16)
    intraT = sb("dintraT", [64, TILE], BF16)
    u_t = sb("du", [64, CPT, 128], BF16); w_t = sb("dwt", [64, CPT, 128], BF16, n=1)[0]
    MT = sb("dMT", [128, CPT, 128]); Bc = sb("dBc", [128, CPT, 128]); qpT = sb("dqpT", [128, TILE], BF16)
    qdf = sb("dqdf", [128, TILE], n=1)[0]
    ost = sb("dost", [128, TILE])

    NT = S // TILE
    def prepass(t):
        i = t % 2
        sl = slice(t*TILE, (t+1)*TILE)
        for kind in range(3):
            xi = xin[kind][i]
            if t == 0:
                P.memset(xi[:, 0:3], 0.0, eng="dve")
                P.dma(xi[:, 3:TILE+3], xT[kind*128:(kind+1)*128, 0:TILE])
            else:
                P.dma(xi, xT[kind*128:(kind+1)*128, t*TILE-3:(t+1)*TILE])
            yy = y[kind]
            P.ts(yy, xi[:, 0:TILE], cw[:, kind*4:kind*4+1], None, ALU.mult)
            for j in range(1, 4):
                P.stt(yy, xi[:, j:j+TILE], cw[:, kind*4+j:kind*4+j+1], yy, ALU.mult, ALU.add)
            if kind == 2:
                P.act(vv[i], yy, AF.Silu)
            else:
                P.act(yy, yy, AF.Silu)
                P.act(C.sqb[kind], yy, AF.Square)
                ps = bank(C)
                P.matmul(ps, C.ones_b, C.sqb[kind])
                P.act(rstd, ps, AF.Sqrt, bias=EPS)
                P.recip(rstd, rstd)
                if kind == 0:
                    P.stt(qn[i], yy, float(1.0/np.sqrt(128.0)), rstd, ALU.mult, ALU.mult)
                else:
                    P.tt(kn[i], yy, rstd, ALU.mult)
            yield
        P.dma(bl[i], bl_d[:, sl].partition_broadcast(128))
        P.dma(al[i], al_d[:, sl].partition_broadcast(128))
        P.act(beta[i], bl[i], AF.Sigmoid)
        P.act(g, al[i], AF.Exp, bias=cw[:, 13:14])
        P.act(g, g, AF.Ln, bias=1.0)
        P.ts(g, g, negA, -1.0, ALU.mult, ALU.mult)
        P.add("dve", (lambda o, m, gg: (lambda e: e.tensor_tensor_scan(o.ap, m.ap, gg.ap, 0.0, ALU.mult, ALU.add)))(gc[i], rmask, g),
              reads=[rmask, g], writes=[gc[i]])
        P.act(egc[i], gc[i], AF.Exp)
        gl3 = T3(gc[i])[:, :, 63:64]
        P.tt(T3(ekd[i]), gl3.bcast([128, CPT, 64]), T3(gc[i]), ALU.subtract)
        P.act(ekd[i], ekd[i], AF.Exp)
        P.act(egl[i], T3(gc[i])[:, :, 63], AF.Exp)
        yield
        P.copy(knb[i], kn[i], eng="act")
        P.copy(qnb[i], qn[i], eng="act")
        P.tt(kbb[i], kn[i], beta[i], ALU.mult)
        P.tt(kb, kn[i], beta[i], ALU.mult)
        P.tt(kbg, kb, egc[i], ALU.mult)
        P.tt(kdc, kn[i], ekd[i], ALU.mult)
        P.tt(qdf, qn[i], egc[i], ALU.mult)
        P.tt(vb, vv[i], beta[i], ALU.mult)
        yield
        for src, dst in ((kbg, kbg_t[i]), (vb, vb_t[i]), (kdc, kdc_t[i])):
            for half in range(2):
                ps = bank(C).bitcast(BF16)
                for cq in range(4):
                    cc = half*4 + cq
                    P.transpose(ps[0:64, cq*128:(cq+1)*128], src[:, cc*64:(cc+1)*64], identb)
                P.copy(dst[:, half*4:half*4+4, :], ps[0:64, 0:512].rearrange("p (c d) -> p c d", c=4), eng="act" if half else "dve")
                yield
        P.tt(T3(tmp64[0]), T3(gc[i][0:64, :]), I64.rearrange("p (o j) -> p o j", o=1).bcast([64, CPT, 64]), ALU.mult)
        P.reduce(gtok, T3(tmp64[0]), ALU.add)
        P.tt(T3(D1), T3(gc[i][0:64, :]), gtok.rearrange("p (c o) -> p c o", o=1).bcast([64, CPT, 64]), ALU.subtract)
        P.ts(tmp64[0], D1, 0.0, None, ALU.max)
        P.act(dec, tmp64[0], AF.Exp, scale=-1.0)
        P.ts(tmp64[1], D1, 0.0, None, ALU.min)
        P.act(decT, tmp64[1], AF.Exp)
        yield
        psL = bank(C); psLT = bank(C); psQ = bank(C)
        for cc in range(CPT):
            cl = slice(cc*64, (cc+1)*64)
            P.matmul(psL[0:64, cl], kbb[i][:, cl], knb[i][:, cl])
            P.matmul(psLT[0:64, cl], knb[i][:, cl], kbb[i][:, cl])
            P.matmul(psQ[0:64, cl], knb[i][:, cl], qnb[i][:, cl])
        bc64 = lambda m: m.rearrange("p (o j) -> p o j", o=1).bcast([64, CPT, 64])
        P.tt(T3(tmp64[0]), T3(dec), bc64(mLs), ALU.mult)
        P.stt(Pn[0], psL[0:64, :], -1.0, tmp64[0], ALU.mult, ALU.mult)
        P.tt(T3(tmp64[1]), T3(decT), bc64(mUs), ALU.mult)
        P.stt(PTn[0], psLT[0:64, :], -1.0, tmp64[1], ALU.mult, ALU.mult)
        P.tt(T3(tmp64[1]), T3(decT), bc64(mUi), ALU.mult)
        P.tt(intraT[i], psQ[0:64, :], tmp64[1], ALU.mult)
        Y = Ym[0]
        P.tt(T3(Y), T3(PTn[0]), bc64(I64), ALU.add)
        P.copy(Yb[0], Y, eng="act")
        ybi = 0
        yield
        cur = 0
        for n in range(1, 6):
            nxt = 1 - cur
            psP = bank(C); psPT = bank(C)
            for cc in range(CPT):
                cl = slice(cc*64, (cc+1)*64)
                P.matmul(psP[0:64, cl], PTn[cur][:, cl], Pn[cur][:, cl])
                if n < 5:
                    P.matmul(psPT[0:64, cl], Pn[cur][:, cl], PTn[cur][:, cl])
            P.copy(Pn[nxt], psP[0:64, :], eng="act")
            if n < 5:
                P.copy(PTn[nxt], psPT[0:64, :], eng="dve")
            yield
            psY = bank(C)
            for cc in range(CPT):
                cl = slice(cc*64, (cc+1)*64)
                P.matmul(psY[0:64, cl], Pn[nxt][:, cl], Yb[ybi][:, cl])
            Y2 = Ym[1] if Y is Ym[0] else Ym[0]
            ybi = 1 - ybi
            P.tt(Yb[ybi], Y, psY[0:64, :], ALU.add)
            if n < 5:
                P.tt(Y2, Y, psY[0:64, :], ALU.add)
            Y = Y2
            yield
            cur = nxt
        for half in range(2):
            psu = bank(C); psw = bank(C)
            for cq in range(4):
                cc = half*4 + cq
                cl = slice(cc*64, (cc+1)*64)
                P.matmul(psu[0:64, cq*128:(cq+1)*128], Yb[ybi][:, cl], vb_t[i][:, cc, :])
                P.matmul(psw[0:64, cq*128:(cq+1)*128], Yb[ybi][:, cl], kbg_t[i][:, cc, :])
            P.copy(u_t[i][:, half*4:half*4+4, :], psu[0:64, :].rearrange("p (c d) -> p c d", c=4), eng="act")
            P.copy(w_t[:, half*4:half*4+4, :], psw[0:64, :].rearrange("p (c d) -> p c d", c=4), eng="dve")
            yield
        psq = bank(C)
        for cc in range(CPT):
            cl = slice(cc*64, (cc+1)*64)
            P.matmul(psq[:, cl], w_t[:, cc, :], intraT[i][:, cl])
        P.tt(qpT[i], qdf, psq, ALU.subtract)
        yield
        for half in range(2):
            psm = bank(C); psb = bank(C)
            for cq in range(4):
                cc = half*4 + cq
                P.matmul(psm[:, cq*128:(cq+1)*128], w_t[:, cc, :], kdc_t[i][:, cc, :])
                P.matmul(psb[:, cq*128:(cq+1)*128], kdc_t[i][:, cc, :], u_t[i][:, cc, :])
            for cq in range(4):
                cc = half*4 + cq
                P.stt(MT[i][:, cc, :], ident, egl[i][:, cc:cc+1], psm[:, cq*128:(cq+1)*128], ALU.mult, ALU.subtract)
            P.copy(Bc[i][:, half*4:half*4+4, :], psb.rearrange("p (c d) -> p c d", c=4), eng="act")
            yield

    def scan_tile(t, gen):
        i = t % 2
        sl = slice(t*TILE, (t+1)*TILE)
        C.nrot = 7
        pso = C.banks[7]
        for cc in range(CPT):
            c = t*CPT + cc
            cl = slice(cc*64, (cc+1)*64)
            S_cur = Ss[c % 3]; S_nxt = Ss[(c + 1) % 3]
            p1 = bank(C)
            P.matmul(p1[:, 0:128], MT[i][:, cc, :], S_cur)
            P.tt(S_nxt, p1[:, 0:128], Bc[i][:, cc, :], ALU.add)
            P.copy(Sbs[(c + 1) % 3], S_nxt, eng="act")
            P.matmul(pso[:, cl], Sbs[c % 3], qpT[i][:, cl], start=True, stop=False)
            P.matmul(pso[:, cl], u_t[i][:, cc, :], intraT[i][:, cl], start=False, stop=True)
            if gen is not None:
                for _ in range(5):
                    next(gen, None)
        P.copy(ost[i], pso, eng="act")
        fin.append(P.dma(o_d[:, sl], ost[i]))

    for _ in prepass(0):
        pass
    for t in range(NT):
        gen = prepass(t + 1) if t + 1 < NT else None
        scan_tile(t, gen)
        if gen is not None:
            for _ in gen:
                pass
    return fin

def build_gdn():
    nc = new_nc()
    P = Prog(nc)
    C = setup_common(P, nc)
    fin = gdn_phase(P, C, nc)
    P.emit(final_wait_ops=fin)
    return nc

def colv(v):
    v = np.asarray(v, np.float32)
    return np.ascontiguousarray(v.reshape(-1, 128).T)
def make_cols(gate_mix=None, ffn=None, pre=None, mixnorm=None):
    cols = np.zeros((128, 136), np.float32)
    if gate_mix is not None: cols[:, 0:16] = colv(gate_mix)
    if ffn is not None:
        ng, sh, sc, gt = ffn
        cols[:, 16:32] = colv(ng); cols[:, 32:48] = colv(sh); cols[:, 48:64] = colv(sc); cols[:, 64:80] = colv(gt)
    if pre is not None:
        ng, sh, sc = pre
        cols[:, 80:96] = colv(ng)
        if sh is not None:
            cols[:, 96:112] = colv(sh); cols[:, 112:128] = colv(sc)
    if mixnorm is not None:
        m = colv(mixnorm)
        cols[:, 128:128+m.shape[1]] = m
    return cols
def w_in_even(w):
    kr = w[:, 1024:1088]
    return np.ascontiguousarray(np.concatenate([w, kr[:, 32:], kr[:, :32]], axis=1))
def mla_consts():
    half = 32
    inv = (10000.0 ** (-np.arange(half, dtype=np.float32) / half)).astype(np.float32)
    invf = np.zeros(128, np.float32); invf[:64] = np.concatenate([inv, inv])
    sign = np.zeros(128, np.float32); sign[:32] = -1; sign[32:64] = 1
    k = np.arange(128)[:, None, None]; r = np.arange(4)[None, :, None]; q = np.arange(512)[None, None, :]
    masks = (q >= k + 128*r).astype(np.float32)
    return invf, sign, np.ascontiguousarray(masks)
def mla_inputs(proj, positions, q_norm, kv_norm, w_uq, w_ukv, h):
    invf, sign, masks = mla_consts()
    cols = np.zeros((128, 10), np.float32)
    cols[:, 0:4] = colv(q_norm); cols[:, 4:8] = colv(kv_norm); cols[:, 8] = invf; cols[:, 9] = sign
    wqh = w_uq[:, h*192:(h+1)*192]
    wq = np.concatenate([wqh[:, :128], wqh[:, 128:192], wqh[:, 160:192], wqh[:, 128:160]], axis=1)
    wkv = w_ukv[:, h*256:(h+1)*256]
    return {"cqT": np.ascontiguousarray(proj[:, 0:512].T), "ckvT": np.ascontiguousarray(proj[:, 512:1024].T),
            "krT": np.ascontiguousarray(proj[:, 1024:1088].T), "krsT": np.ascontiguousarray(proj[:, 5200:5264].T),
            "mla_cols": cols, "wq": np.ascontiguousarray(wq), "wkv": np.ascontiguousarray(wkv),
            "pos": np.ascontiguousarray(positions.reshape(1, -1).astype(np.int32)), "masks": masks}
def gla_consts():
    tp = np.arange(64)[:, None]; t = np.arange(64)[None, :]
    triS = np.where(tp <= t, -1.0/16.0, 0.0); triR = np.where(tp > t, -1.0/16.0, 0.0)
    maskT = np.where(t >= tp, 1.0, 0.0)
    return np.ascontiguousarray(np.concatenate([triS, triR, maskT], axis=1).astype(np.float32))
def gla_inputs(proj, w_gk2, b_gk2, core):
    h, e = core // 2, core % 2
    q = proj[:, h*256:(h+1)*256]; k = proj[:, 1024+h*256:1024+(h+1)*256]
    v = proj[:, 2048+h*512+e*256:2048+h*512+(e+1)*256]
    gl = proj[:, 6144:6160]
    waug = np.concatenate([w_gk2[:, h*256:(h+1)*256], b_gk2[None, h*256:(h+1)*256]], axis=0)
    return {"gqT": np.ascontiguousarray(q.T), "gkT": np.ascontiguousarray(k.T), "gktok": np.ascontiguousarray(k),
            "gvtok": np.ascontiguousarray(v), "glowT": np.ascontiguousarray(gl.T), "gwaug": np.ascontiguousarray(waug),
            "gcst": gla_consts()}
def gdn_consts():
    p = np.arange(64)[:, None]; f = np.arange(64)[None, :]
    I64 = (p == f); mLs = (p > f); mUi = (f >= p); mUs = (f > p)
    c64 = np.concatenate([I64, mLs, mUi, mUs], axis=1).astype(np.float32)
    rm = np.ones((128, 512), np.float32); rm[:, ::64] = 0.0
    return np.ascontiguousarray(c64), np.eye(128, dtype=np.float32), rm
def gdn_inputs(proj, conv_w, a_log, dt_bias, h):
    q0 = 1088
    x = np.concatenate([proj[:, q0+h*128:q0+(h+1)*128], proj[:, q0+1024+h*128:q0+1024+(h+1)*128],
                        proj[:, q0+2048+h*128:q0+2048+(h+1)*128]], axis=1)
    cw = np.zeros((128, 14), np.float32)
    for kind in range(3):
        cw[:, kind*4:(kind+1)*4] = conv_w[:, kind*1024+h*128:kind*1024+(h+1)*128].T
    cw[:, 12] = a_log[h]; cw[:, 13] = dt_bias[h]
    c64, ident, rm = gdn_consts()
    return {"dxT": np.ascontiguousarray(x.T), "dcw": cw, "dbl": np.ascontiguousarray(proj[:, 5184+h][None]),
            "dal": np.ascontiguousarray(proj[:, 5192+h][None]), "dc64": c64, "did": ident, "drm": rm}

_DBG = {}
_NC_CACHE = {}

def _get(key, fn):
    if key not in _NC_CACHE:
        _NC_CACHE[key] = fn()
    return _NC_CACHE[key]

def _run(nc, maps):
    return run_bass_kernel_spmd(nc, maps, core_ids=list(range(8))).results

def kernel(x, c, positions, norm_g, ada_w, ada_b, ab_w_in, mla_q_norm, mla_w_uq, mla_kv_norm, mla_w_ukv,
           gdn_conv_w, gdn_a_log, gdn_dt_bias, gdn_norm, ab_w_out, gla_w_in, gla_w_gk2, gla_b_gk2, gla_norm,
           gla_w_out, ffn_w1, ffn_w3, ffn_w2, final_norm):
    f32 = lambda a: np.asarray(a, np.float32)
    x = f32(x)[0]; c = f32(c)[0]; positions = np.asarray(positions)
    norm_g = f32(norm_g); NL = 4
    c_col = np.ascontiguousarray(c.reshape(16, 128).T)
    aw = f32(ada_w).reshape(8, 2048, 6144); ab = f32(ada_b).reshape(8, 6144)
    maps = []
    for j in range(8):
        maps.append({"c_col": c_col, "w": np.ascontiguousarray(aw[:, :, j*768:(j+1)*768]),
                     "b": np.ascontiguousarray(ab[:, j*768:(j+1)*768].reshape(8, 6, 128).transpose(2, 0, 1).reshape(128, 48))})
    res = _run(_get("ada", build_ada), maps)
    mods = np.zeros((8, 6144), np.float32)
    for j in range(8):
        mods[:, j*768:(j+1)*768] = res[j]["o"].reshape(128, 8, 6).transpose(1, 2, 0).reshape(8, 768)
    del aw, maps
    SH, SC, GT = slice(0, 2048), slice(2048, 4096), slice(4096, 6144)
    def w_in_for(l):
        return w_in_even(f32(ab_w_in[l // 2])) if l % 2 == 0 else np.ascontiguousarray(f32(gla_w_in[l // 2]))
    xT = [np.ascontiguousarray(x[j*1024:(j+1)*1024].T) for j in range(8)]
    cols = make_cols(pre=(norm_g[0, 0], mods[0][SH], mods[0][SC]))
    w_in = w_in_for(0)
    res = _run(_get(("t", None, 5264, False), lambda: build_t(None, True, 5264, False)),
               [{"xT": xT[j], "cols": cols, "w_in": w_in} for j in range(8)])
    proj = np.concatenate([res[j]["projT"].T for j in range(8)], axis=0)
    for l in range(NL):
        i = l // 2
        if l % 2 == 0:
            res = _run(_get("mla", build_mla), [mla_inputs(proj, positions, f32(mla_q_norm[i]), f32(mla_kv_norm[i]),
                                                             f32(mla_w_uq[i]), f32(mla_w_ukv[i]), h) for h in range(8)])
            o_a = np.concatenate([res[h]["oaT"].T for h in range(8)], axis=1)
            res = _run(_get("gdn", build_gdn), [gdn_inputs(proj, f32(gdn_conv_w[i]), f32(gdn_a_log[i]), f32(gdn_dt_bias[i]), h)
                                                 for h in range(8)])
            o_b = np.concatenate([res[h]["dobT"].T for h in range(8)], axis=1)
            o = np.concatenate([o_a, o_b], axis=1)
            gsrc = proj[:, 4160:5184]
            mixnorm = f32(gdn_norm[i]); w_out = f32(ab_w_out[i]); kind = "ab"
        else:
            res = _run(_get("gla", build_gla), [gla_inputs(proj, f32(gla_w_gk2[i]), f32(gla_b_gk2[i]), cc) for cc in range(8)])
            o = np.concatenate([res[cc]["go"] for cc in range(8)], axis=1)
            gsrc = proj[:, 4096:6144]
            mixnorm = f32(gla_norm[i]); w_out = f32(gla_w_out[i]); kind = "gla"
        _DBG[f"o{l}"] = o
        last = (l == NL - 1)
        if last:
            pre = (f32(final_norm), None, None); f_next = 0
        else:
            pre = (norm_g[l+1, 0], mods[2*(l+1)][SH], mods[2*(l+1)][SC]); f_next = 5264 if (l+1) % 2 == 0 else 6160
        cols = make_cols(gate_mix=mods[2*l][GT], ffn=(norm_g[l, 1], mods[2*l+1][SH], mods[2*l+1][SC], mods[2*l+1][GT]),
                         pre=pre, mixnorm=mixnorm)
        w1 = np.ascontiguousarray(f32(ffn_w1[l])); w3 = np.ascontiguousarray(f32(ffn_w3[l])); w2 = np.ascontiguousarray(f32(ffn_w2[l]))
        w_out = np.ascontiguousarray(w_out)
        maps = []
        w_in = None if last else w_in_for(l + 1)
        for j in range(8):
            tk = slice(j*1024, (j+1)*1024)
            m = {"xT": xT[j], "cols": cols, "oT": np.ascontiguousarray(o[tk].T), "gT": np.ascontiguousarray(gsrc[tk].T),
                 "w_out": w_out, "w1": w1, "w3": w3, "w2": w2}
            if not last:
                m["w_in"] = w_in
            maps.append(m)
        key = ("t", kind, f_next, last)
        res = _run(_get(key, (lambda kind=kind, f_next=f_next, last=last: build_t(kind, not last, f_next, last))), maps)
        xT = [res[j]["xT_out"] for j in range(8)]
        if not last:
            proj = np.concatenate([res[j]["projT"].T for j in range(8)], axis=0)
        _DBG[f"x{l}"] = xT
    out = np.concatenate([xT[j].T for j in range(8)], axis=0)
    return np.ascontiguousarray(out[None]).astype(np.float32)
```
